# Optimizing a Trainium2 kernel written in Bass

```python
import jax, jax.numpy as jnp
from jax import lax
import numpy as np

D_MODEL = 2048
BATCH = 2
SEQ = 4096
DEPTH = 4
DEC_BATCH = 8
DEC_SEQ = 8
PAST_LEN = 16384
PAGE_SIZE = 128

HEAD_DIM = 128
MIX_WIDTH = D_MODEL
MEM_HEADS = 4
N_MEM = 256
MEM_WIDTH = MEM_HEADS * HEAD_DIM
TOK_WIDTH = MIX_WIDTH - MEM_WIDTH
NSA_HEADS = TOK_WIDTH // HEAD_DIM
NSA_KV = 4
CMP_BLOCK = 32
SLC_BLOCK = 64
TOP_N = 16
WINDOW = 512
FORCE_BONUS = 1.0e4
NSA_IN = NSA_HEADS * HEAD_DIM + 6 * NSA_KV * HEAD_DIM + 3 * NSA_HEADS + MEM_WIDTH
RET_HEAD_DIM = 256
RET_HEADS = TOK_WIDTH // RET_HEAD_DIM
RET_CHUNK = 128
ROPE_BASE = 10000.0
RET_IN = 4 * TOK_WIDTH + MEM_WIDTH
FFN_DIM = 5632
CONV_W = 3
NORM_EPS = 1e-6
Q_BLOCK = 128

kernel_name = 'nsa_retention_hybrid_step'


def rms_norm(x, g):
    xf = x.astype(jnp.float32)
    y = xf * lax.rsqrt(jnp.mean(xf * xf, axis=-1, keepdims=True) + NORM_EPS)
    return (y * g.astype(jnp.float32)).astype(x.dtype)


def masked_softmax(s, mask):
    s = jnp.where(mask, s.astype(jnp.float32), -jnp.inf)
    m = jnp.max(s, axis=-1, keepdims=True)
    m = jnp.where(jnp.isfinite(m), m, 0.0)
    e = jnp.where(mask, jnp.exp(s - m), 0.0)
    return e / jnp.maximum(jnp.sum(e, axis=-1, keepdims=True), 1e-30)


def split_cols(p, sizes):
    return jnp.split(p, np.cumsum(sizes)[:-1].tolist(), axis=-1)


def compress_blocks(rows, pe, w1, w2):
    b, l, g, d = rows.shape
    nc = l // CMP_BLOCK
    blk = rows[:, :nc * CMP_BLOCK].reshape(b, nc, CMP_BLOCK, g, d) + pe[None, None, :, None, :]
    flat = blk.transpose(0, 1, 3, 2, 4).reshape(b, nc, g, CMP_BLOCK * d)
    return jax.nn.gelu(flat @ w1) @ w2


def nsa_project(h, w_in):
    b, t, _ = h.shape
    kvw = NSA_KV * HEAD_DIM
    q, kc, vc, ks, vs, kw, vw, gates, qm = split_cols(h @ w_in, [NSA_HEADS * HEAD_DIM] + [kvw] * 6 + [3 * NSA_HEADS, MEM_WIDTH])
    kvs = (b, t, NSA_KV, HEAD_DIM)
    rows = jnp.stack([kc.reshape(kvs), vc.reshape(kvs), ks.reshape(kvs), vs.reshape(kvs)], axis=2)
    win = jnp.stack([kw.reshape(kvs), vw.reshape(kvs)], axis=2)
    return (q.reshape(b, t, NSA_HEADS, HEAD_DIM), rows, win,
            gates.reshape(b, t, NSA_HEADS, 3), qm.reshape(b, t, MEM_HEADS, HEAD_DIM))


def nsa_global(q, q_pos, rows, pe, w1, w2):
    b, t, h, d = q.shape
    L, g = rows.shape[1], rows.shape[3]
    r = h // g
    scale = d ** -0.5
    qg = q.reshape(b, t, g, r, d)
    kc = compress_blocks(rows[:, :, 0], pe[0], w1[0], w2[0])
    vc = compress_blocks(rows[:, :, 1], pe[1], w1[1], w2[1])
    nc = kc.shape[1]
    cmp_end = jnp.arange(nc) * CMP_BLOCK + (CMP_BLOCK - 1)
    cmask = (cmp_end[None, :] <= q_pos[:, None])[None, :, None, None, :]
    sc = jnp.einsum('btgrd,bngd->btgrn', qg, kc) * scale
    pc = masked_softmax(sc, cmask)
    o_cmp = jnp.einsum('btgrn,bngd->btgrd', pc.astype(vc.dtype), vc).reshape(b, t, h, d)
    ratio = SLC_BLOCK // CMP_BLOCK
    ns = -(-L // SLC_BLOCK)
    p_grp = jnp.pad(jnp.sum(pc, axis=3), ((0, 0), (0, 0), (0, 0), (0, ns * ratio - nc)))
    p_slc = p_grp.reshape(b, t, g, ns, ratio).sum(-1)
    blk = jnp.arange(ns)[None, :]
    cur = (q_pos // SLC_BLOCK)[:, None]
    valid = blk * SLC_BLOCK <= q_pos[:, None]
    forced = (blk == 0) | (blk == cur) | (blk == cur - 1)
    score = jnp.where(valid[None, :, None, :], p_slc + jnp.where(forced, FORCE_BONUS, 0.0)[None, :, None, :], -jnp.inf)
    n_sel = min(TOP_N, ns)
    _, sel = lax.top_k(score, n_sel)
    pad = ns * SLC_BLOCK - L
    def to_blocks(x):
        x = jnp.pad(x, ((0, 0), (0, pad), (0, 0), (0, 0)))
        return x.reshape(b, ns, SLC_BLOCK, g, d).transpose(0, 3, 1, 2, 4)
    ks_blk = to_blocks(rows[:, :, 2])
    vs_blk = to_blocks(rows[:, :, 3])
    qc = Q_BLOCK if t % Q_BLOCK == 0 else t
    nq = t // qc
    bi = jnp.arange(b)[:, None, None, None]
    gi = jnp.arange(g)[None, None, :, None]
    def attend(args):
        qb, selb, posb = args
        kb = ks_blk[bi, gi, selb]
        vb = vs_blk[bi, gi, selb]
        kpos = selb[..., None] * SLC_BLOCK + jnp.arange(SLC_BLOCK)
        mask = (kpos <= posb[None, :, None, None, None]).reshape(b, qc, g, 1, n_sel * SLC_BLOCK)
        s = jnp.einsum('bqgrd,bqgnkd->bqgrnk', qb, kb).reshape(b, qc, g, r, n_sel * SLC_BLOCK) * scale
        p = masked_softmax(s, mask).reshape(b, qc, g, r, n_sel, SLC_BLOCK)
        return jnp.einsum('bqgrnk,bqgnkd->bqgrd', p.astype(vb.dtype), vb)
    xs = (qg.reshape(b, nq, qc, g, r, d).swapaxes(0, 1),
          sel.reshape(b, nq, qc, g, n_sel).swapaxes(0, 1),
          q_pos.reshape(nq, qc))
    o_slc = lax.map(attend, xs).swapaxes(0, 1).reshape(b, t, h, d)
    return o_cmp, o_slc


def window_prompt(q, kw, vw):
    b, s, h, d = q.shape
    g = kw.shape[2]
    r = h // g
    nb = s // Q_BLOCK
    npb = WINDOW // Q_BLOCK
    front = ((0, 0), (npb * Q_BLOCK, 0), (0, 0), (0, 0))
    def band(x):
        xp = jnp.pad(x, front)
        return jnp.concatenate([xp[:, j * Q_BLOCK: j * Q_BLOCK + s].reshape(b, nb, Q_BLOCK, g, d) for j in range(npb + 1)], axis=2)
    kb, vb = band(kw), band(vw)
    qb = q.reshape(b, nb, Q_BLOCK, g, r, d)
    sc = jnp.einsum('bnqgrd,bnkgd->bnqgrk', qb, kb) * (d ** -0.5)
    qq = jnp.arange(s).reshape(nb, Q_BLOCK)[:, :, None]
    kk = ((jnp.arange(nb)[:, None] - npb) * Q_BLOCK + jnp.arange((npb + 1) * Q_BLOCK)[None, :])[:, None, :]
    mask = (kk <= qq) & (kk > qq - WINDOW) & (kk >= 0)
    p = masked_softmax(sc, mask[None, :, :, None, None, :])
    o = jnp.einsum('bnqgrk,bnkgd->bnqgrd', p.astype(vb.dtype), vb)
    return o.reshape(b, s, h, d)


def window_sample(q, q_pos, kw, vw, k_pos):
    b, t, h, d = q.shape
    g = kw.shape[2]
    qg = q.reshape(b, t, g, h // g, d)
    s = jnp.einsum('btgrd,bkgd->btgrk', qg, kw) * (d ** -0.5)
    mask = (k_pos[None, :] <= q_pos[:, None]) & (k_pos[None, :] > q_pos[:, None] - WINDOW)
    p = masked_softmax(s, mask[None, :, None, None, :])
    return jnp.einsum('btgrk,bkgd->btgrd', p.astype(vw.dtype), vw).reshape(b, t, h, d)


def nsa_combine(o_cmp, o_slc, o_win, gates):
    gt = jax.nn.sigmoid(gates.astype(jnp.float32))
    o = (gt[..., 0:1] * o_cmp.astype(jnp.float32) + gt[..., 1:2] * o_slc.astype(jnp.float32)
         + gt[..., 2:3] * o_win.astype(jnp.float32))
    b, t = o.shape[:2]
    return o.reshape(b, t, TOK_WIDTH).astype(o_cmp.dtype)


def rotate(x, pos):
    half = x.shape[-1] // 2
    inv = ROPE_BASE ** (-jnp.arange(half, dtype=jnp.float32) / half)
    ang = pos.astype(jnp.float32)[:, None] * inv[None, :]
    cos = jnp.cos(ang)[None, :, None, :]
    sin = jnp.sin(ang)[None, :, None, :]
    xf = x.astype(jnp.float32)
    x1, x2 = xf[..., :half], xf[..., half:]
    return jnp.concatenate([x1 * cos - x2 * sin, x1 * sin + x2 * cos], axis=-1).astype(x.dtype)


def retention_chunkwise(q, k, v, s0):
    b, t, h, _ = q.shape
    dv = v.shape[-1]
    c = RET_CHUNK if t % RET_CHUNK == 0 else t
    n = t // c
    log_g = jnp.log1p(-jnp.exp2(-5.0 - jnp.arange(h, dtype=jnp.float32)))
    i = jnp.arange(c, dtype=jnp.float32)
    diff = i[:, None] - i[None, :]
    dmat = jnp.where(diff >= 0, jnp.exp(jnp.maximum(diff, 0.0)[None] * log_g[:, None, None]), 0.0)
    q_decay = jnp.exp((i + 1.0)[None, :] * log_g[:, None])[..., None]
    k_decay = jnp.exp((c - 1.0 - i)[None, :] * log_g[:, None])[..., None]
    c_decay = jnp.exp(c * log_g)[:, None, None]
    def split(x):
        return x.astype(jnp.float32).reshape(b, n, c, h, x.shape[-1]).transpose(1, 0, 3, 2, 4)
    def step(s, xs):
        qc, kc, vc = xs
        inner = jnp.einsum('bhid,bhjd->bhij', qc, kc) * dmat
        o = jnp.einsum('bhij,bhjv->bhiv', inner, vc) + jnp.einsum('bhid,bhdv->bhiv', qc, s) * q_decay
        s = s * c_decay + jnp.einsum('bhjd,bhjv->bhdv', kc * k_decay, vc)
        return s, o
    s, o = lax.scan(step, s0, (split(q), split(k), split(v)))
    return o.transpose(1, 0, 3, 2, 4).reshape(b, t, h, dv), s


def retention_mix(h, w_in, gn_g, pos, s0):
    b, t, _ = h.shape
    q, k, v, gate, qm = split_cols(h @ w_in, [TOK_WIDTH] * 4 + [MEM_WIDTH])
    hs = (b, t, RET_HEADS, RET_HEAD_DIM)
    q = rotate(q.reshape(hs), pos)
    k = rotate(k.reshape(hs), pos) * (RET_HEAD_DIM ** -0.5)
    o, s_new = retention_chunkwise(q, k, v.reshape(hs), s0.astype(jnp.float32))
    mu = jnp.mean(o, axis=-1, keepdims=True)
    var = jnp.mean(jnp.square(o - mu), axis=-1, keepdims=True)
    y = (o - mu) * lax.rsqrt(var + NORM_EPS) * gn_g.reshape(RET_HEADS, RET_HEAD_DIM).astype(jnp.float32)
    tok = (jax.nn.silu(gate.astype(jnp.float32)) * y.reshape(b, t, TOK_WIDTH)).astype(h.dtype)
    return tok, qm.reshape(b, t, MEM_HEADS, HEAD_DIM), s_new.astype(h.dtype)


def mem_attend(qm, mem_kv):
    b, t = qm.shape[:2]
    s = jnp.einsum('bthd,bmhd->bthm', qm, mem_kv[:, :, 0]) * (HEAD_DIM ** -0.5)
    p = jax.nn.softmax(s.astype(jnp.float32), axis=-1)
    return jnp.einsum('bthm,bmhd->bthd', p.astype(qm.dtype), mem_kv[:, :, 1]).reshape(b, t, MEM_WIDTH)


def conv_ffn(x, buf, norm_g, w_in, conv_w, conv_b, w_out):
    h = rms_norm(x, norm_g)
    a, gv = jnp.split(h @ w_in, 2, axis=-1)
    t = a.shape[1]
    ap = jnp.concatenate([buf.astype(a.dtype), a], axis=1)
    ac = conv_b
    for j in range(CONV_W):
        ac = ac + ap[:, j:j + t] * conv_w[j]
    out = (jax.nn.silu(ac) * gv) @ w_out
    return x + out, ap[:, -(CONV_W - 1):]


def setup_inputs(seed: int = 0) -> dict:
    key = jax.random.key(seed)
    keys = iter(jax.random.split(key, 40))
    f32 = jnp.float32
    n_a = (DEPTH + 1) // 2
    n_b = DEPTH // 2
    n_pages = PAST_LEN // PAGE_SIZE
    n_phys = (5 * DEC_BATCH * n_pages + 3) // 4
    win_buf = min(WINDOW, PAST_LEN)
    def nrm(shape, scale):
        return jax.random.normal(next(keys), shape, f32) * scale
    def gain(shape):
        return 1.0 + nrm(shape, 0.05)
    perm = jax.random.permutation(next(keys), n_phys)[:DEC_BATCH * n_pages]
    page_table = perm.reshape(DEC_BATCH, n_pages).astype(jnp.int32)
    return {
        'x_prompt': nrm((BATCH, SEQ, D_MODEL), 1.0),
        'x_sample': nrm((DEC_BATCH, DEC_SEQ, D_MODEL), 1.0),
        'cache_nsa_kv': nrm((n_a, n_phys, PAGE_SIZE, 4, NSA_KV, HEAD_DIM), 1.0),
        'state_nsa_win': nrm((n_a, DEC_BATCH, win_buf, 2, NSA_KV, HEAD_DIM), 1.0),
        'state_ret': nrm((n_b, DEC_BATCH, RET_HEADS, RET_HEAD_DIM, RET_HEAD_DIM), 0.3),
        'state_ffn_conv': nrm((DEPTH, DEC_BATCH, CONV_W - 1, FFN_DIM), 1.0),
        'cache_mem_kv': nrm((DEPTH, DEC_BATCH, N_MEM, 2, MEM_HEADS, HEAD_DIM), 1.0),
        'page_table': page_table,
        'mem_prompt': nrm((BATCH, N_MEM, D_MODEL), 1.0),
        'norm1_g': gain((DEPTH, D_MODEL)),
        'nsa_w_in': nrm((n_a, D_MODEL, NSA_IN), D_MODEL ** -0.5),
        'nsa_cmp_pe': nrm((n_a, 2, CMP_BLOCK, HEAD_DIM), 0.1),
        'nsa_cmp_w1': nrm((n_a, 2, CMP_BLOCK * HEAD_DIM, HEAD_DIM), (CMP_BLOCK * HEAD_DIM) ** -0.5),
        'nsa_cmp_w2': nrm((n_a, 2, HEAD_DIM, HEAD_DIM), HEAD_DIM ** -0.5),
        'ret_w_in': nrm((n_b, D_MODEL, RET_IN), D_MODEL ** -0.5),
        'ret_gn_g': gain((n_b, TOK_WIDTH)),
        'mem_norm_g': gain((DEPTH, D_MODEL)),
        'w_mem_kv': nrm((DEPTH, D_MODEL, 2 * MEM_WIDTH), D_MODEL ** -0.5),
        'w_o': nrm((DEPTH, MIX_WIDTH, D_MODEL), MIX_WIDTH ** -0.5),
        'norm2_g': gain((DEPTH, D_MODEL)),
        'ffn_w_in': nrm((DEPTH, D_MODEL, 2 * FFN_DIM), D_MODEL ** -0.5),
        'ffn_conv_w': nrm((DEPTH, CONV_W, FFN_DIM), CONV_W ** -0.5),
        'ffn_conv_b': nrm((DEPTH, FFN_DIM), 0.02),
        'ffn_w_out': nrm((DEPTH, FFN_DIM, D_MODEL), FFN_DIM ** -0.5),
        'final_norm_g': gain((D_MODEL,)),
    }


def reference(x_prompt, x_sample, cache_nsa_kv, state_nsa_win, state_ret, state_ffn_conv, cache_mem_kv,
              page_table, mem_prompt, norm1_g, nsa_w_in, nsa_cmp_pe, nsa_cmp_w1, nsa_cmp_w2, ret_w_in,
              ret_gn_g, mem_norm_g, w_mem_kv, w_o, norm2_g, ffn_w_in, ffn_conv_w, ffn_conv_b, ffn_w_out,
              final_norm_g):
    bp, s_len, _ = x_prompt.shape
    db, t_len, _ = x_sample.shape
    past_len = page_table.shape[1] * cache_nsa_kv.shape[2]
    pos_p = jnp.arange(s_len, dtype=jnp.int32)
    pos_s = past_len + jnp.arange(t_len, dtype=jnp.int32)
    wb = state_nsa_win.shape[2]
    win_kpos = past_len - wb + jnp.arange(wb + t_len, dtype=jnp.int32)
    keep_p = min(WINDOW, s_len)
    xp, xs = x_prompt, x_sample
    kv_p_l, kv_s_l, win_p_l, win_s_l, ret_p_l, ret_s_l, conv_p_l, conv_s_l, mem_p_l = ([] for _ in range(9))
    for i in range(DEPTH):
        hp = rms_norm(xp, norm1_g[i])
        hs = rms_norm(xs, norm1_g[i])
        mem_kv_p = (rms_norm(mem_prompt, mem_norm_g[i]) @ w_mem_kv[i]).reshape(bp, N_MEM, 2, MEM_HEADS, HEAD_DIM)
        mem_p_l.append(mem_kv_p)
        if i % 2 == 0:
            a = i // 2
            cmp_w = (nsa_cmp_pe[a], nsa_cmp_w1[a], nsa_cmp_w2[a])
            q_p, rows_p, win_p, gates_p, qm_p = nsa_project(hp, nsa_w_in[a])
            oc, osl = nsa_global(q_p, pos_p, rows_p, *cmp_w)
            ow = window_prompt(q_p, win_p[:, :, 0], win_p[:, :, 1])
            tok_p = nsa_combine(oc, osl, ow, gates_p)
            q_s, rows_s, win_s, gates_s, qm_s = nsa_project(hs, nsa_w_in[a])
            past = cache_nsa_kv[a][page_table].reshape(db, past_len, 4, NSA_KV, HEAD_DIM)
            full = jnp.concatenate([past.astype(rows_s.dtype), rows_s], axis=1)
            oc, osl = nsa_global(q_s, pos_s, full, *cmp_w)
            wfull = jnp.concatenate([state_nsa_win[a].astype(win_s.dtype), win_s], axis=1)
            ow = window_sample(q_s, pos_s, wfull[:, :, 0], wfull[:, :, 1], win_kpos)
            tok_s = nsa_combine(oc, osl, ow, gates_s)
            kv_p_l.append(rows_p)
            kv_s_l.append(rows_s)
            win_p_l.append(win_p[:, s_len - keep_p:])
            win_s_l.append(wfull[:, -wb:])
        else:
            bl = i // 2
            zero_state = jnp.zeros((bp, RET_HEADS, RET_HEAD_DIM, RET_HEAD_DIM), jnp.float32)
            tok_p, qm_p, sp = retention_mix(hp, ret_w_in[bl], ret_gn_g[bl], pos_p, zero_state)
            tok_s, qm_s, ss = retention_mix(hs, ret_w_in[bl], ret_gn_g[bl], pos_s, state_ret[bl])
            ret_p_l.append(sp)
            ret_s_l.append(ss)
        xp = xp + jnp.concatenate([tok_p, mem_attend(qm_p, mem_kv_p)], axis=-1) @ w_o[i]
        xs = xs + jnp.concatenate([tok_s, mem_attend(qm_s, cache_mem_kv[i].astype(qm_s.dtype))], axis=-1) @ w_o[i]
        xp, cp = conv_ffn(xp, jnp.zeros((bp, CONV_W - 1, FFN_DIM), xp.dtype), norm2_g[i], ffn_w_in[i],
                          ffn_conv_w[i], ffn_conv_b[i], ffn_w_out[i])
        xs, cs = conv_ffn(xs, state_ffn_conv[i], norm2_g[i], ffn_w_in[i], ffn_conv_w[i], ffn_conv_b[i], ffn_w_out[i])
        conv_p_l.append(cp)
        conv_s_l.append(cs)
    y_prompt = rms_norm(xp, final_norm_g)
    y_sample = rms_norm(xs, final_norm_g)
    return (y_prompt, y_sample, jnp.stack(kv_p_l), jnp.stack(kv_s_l), jnp.stack(win_p_l), jnp.stack(win_s_l),
            jnp.stack(ret_p_l), jnp.stack(ret_s_l), jnp.stack(conv_p_l), jnp.stack(conv_s_l), jnp.stack(mem_p_l))
```

```python
from contextlib import ExitStack
import numpy as np
import concourse.bass as bass
import concourse.mybir as mybir
from concourse.bass_utils import run_bass_kernel_spmd

F32 = mybir.dt.float32
BF16 = mybir.dt.bfloat16
I32 = mybir.dt.int32
AF = mybir.ActivationFunctionType
ALU = mybir.AluOpType
AX = mybir.AxisListType

D = 2048
SEQ = 4096
DEPTH = 4
T_S = 8
NTOK = SEQ + T_S
KC = 16
NSA_IN = 5156
RET_IN = 6656
FFN = 5632
FC = FFN // 128
EPS = 1e-6
NDMA = 24
NEG = -30000.0
PAST = 16384
NPAGE = 128
SCALE = 128 ** -0.5
LAYERS = list(range(DEPTH))


class Res:
    __slots__ = ("w", "r", "multi")

    def __init__(self, multi=False):
        self.w = {}
        self.r = {}
        self.multi = multi


class Sched:
    def __init__(self, nc):
        self.nc = nc
        self.eng = {"pe": nc.tensor, "act": nc.scalar, "dve": nc.vector,
                    "pool": nc.gpsimd, "sp": nc.sync}
        self.sem, self.cnt, self.known = {}, {}, {}
        for k in self.eng:
            self.sem[k] = nc.alloc_semaphore(name="sem_" + k)
            self.cnt[k] = 0
            self.known[k] = {}
        for i in range(NDMA):
            k = "d%d" % i
            self.sem[k] = nc.alloc_semaphore(name="sem_" + k)
            self.cnt[k] = 0
        self.rr = 0
        self.n_inst = 0

    def _wait(self, e, s, v):
        if s == "pe" and e == "pe":
            return
        if self.known[e].get(s, 0) >= v:
            return
        self.known[e][s] = v
        self.eng[e].wait_ge(self.sem[s], v)
        self.n_inst += 1

    def _deps(self, e, reads, writes):
        for r in reads:
            for s, v in r.w.items():
                self._wait(e, s, v)
        for w in writes:
            for s, v in w.r.items():
                self._wait(e, s, v)
            if not w.multi:
                for s, v in w.w.items():
                    self._wait(e, s, v)

    def _record(self, s, v, reads, writes):
        for r in reads:
            if r.r.get(s, 0) < v:
                r.r[s] = v
        for w in writes:
            if w.multi:
                if w.w.get(s, 0) < v:
                    w.w[s] = v
            else:
                w.w = {s: v}
            w.r = {}

    def op(self, e, fn, reads=(), writes=()):
        self._deps(e, reads, writes)
        inst = fn(self.eng[e])
        self.cnt[e] += 1
        inst.then_inc(self.sem[e], 1)
        self.n_inst += 1
        self._record(e, self.cnt[e], reads, writes)

    def mm_group(self, fns, reads=(), writes=()):
        self._deps("pe", reads, writes)
        inst = None
        for fn in fns:
            inst = fn(self.eng["pe"])
            self.n_inst += 1
        self.cnt["pe"] += 1
        inst.then_inc(self.sem["pe"], 1)
        self._record("pe", self.cnt["pe"], reads, writes)

    def dma(self, q, out, in_, reads=(), writes=(), fn=None):
        self._deps(q, reads, writes)
        i = self.rr
        self.rr = (i + 1) % NDMA
        k = "d%d" % i
        if self.cnt[k] > 0:
            self._wait(q, k, 16 * self.cnt[k])
        self.cnt[k] += 1
        if fn is None:
            inst = self.eng[q].dma_start(out=out, in_=in_)
        else:
            inst = fn(self.eng[q])
        inst.then_inc(self.sem[k], 16)
        self.n_inst += 1
        self._record(k, 16 * self.cnt[k], reads, writes)

    def barrier(self):
        for e in ("pe", "act", "dve", "pool", "sp"):
            for k, c in self.cnt.items():
                if c > 0 and k != e:
                    self._wait(e, k, 16 * c if k[1:].isdigit() else c)

    def finish(self):
        for i in range(NDMA):
            k = "d%d" % i
            if self.cnt[k] > 0:
                self._wait("sp", k, 16 * self.cnt[k])
        for e in ("pe", "act", "dve", "pool"):
            if self.cnt[e] > 0:
                self._wait("sp", e, self.cnt[e])


class Pool:
    def __init__(self, tiles):
        self.tiles = [(t, Res()) for t in tiles]
        self.i = 0

    def get(self):
        t = self.tiles[self.i]
        self.i = (self.i + 1) % len(self.tiles)
        return t


def gamma(h):
    return 1.0 - 2.0 ** (-5.0 - h)


def make_tables():
    T = {}
    T["ident"] = np.eye(128, dtype=np.float32)
    q = np.arange(SEQ)[:, None]
    n = np.arange(128)[None, :]
    T["cmpmask"] = np.where(n * 32 + 31 <= q, 0.0, NEG).astype(np.float32)
    blk = np.arange(64)[None, :]
    cur = q // 64
    valid = blk * 64 <= q
    forced = (blk == 0) | (blk == cur) | (blk == cur - 1)
    T["bonus"] = np.where(valid, np.where(forced, 1.0e4, 0.0), -1.0e30).astype(np.float32)
    p = np.arange(128)[:, None]
    k = np.arange(128)[None, :]
    T["tri_le"] = np.where(k <= p, 0.0, NEG).astype(np.float32)
    T["tri_gt"] = np.where(k > p, 0.0, NEG).astype(np.float32)
    sb = np.zeros((128, 264), np.float32)
    sb[:, [0, 255, 256]] = 1.0e4
    sb[:, 257:] = -1.0e30
    T["s_bonus"] = sb
    t = np.arange(128)[:, None]
    j = np.arange(8)[None, :]
    T["s_tri8"] = np.where(j <= t, 0.0, NEG).astype(np.float32)
    r = np.arange(512)[None, :]
    T["s_winmask"] = np.where(r > t, 0.0, NEG).astype(np.float32)
    pos = np.concatenate([np.arange(SEQ), PAST + np.arange(T_S)]).astype(np.float32)
    inv = (10000.0 ** (-np.arange(128, dtype=np.float32) / 128.0)).astype(np.float32)
    ang = (pos[None, :] * inv[:, None]).astype(np.float32)
    T["cosT"] = np.cos(ang).astype(np.float32)
    T["sinT"] = np.sin(ang).astype(np.float32)
    for C, tag in ((128, "128"), (8, "8")):
        i = np.arange(C, dtype=np.float64)
        dm = np.zeros((6, 128, C), np.float32)
        qd = np.zeros((6, 128, C), np.float32)
        kd = np.zeros((128, 6), np.float32)
        for h in range(6):
            lg = np.log1p(-2.0 ** (-5.0 - h))
            diff = i[None, :] - i[:, None]
            dm[h, :C, :] = np.where(diff >= 0, np.exp(np.maximum(diff, 0.0) * lg), 0.0)
            qd[h, :, :] = np.exp((i + 1.0) * lg)[None, :]
            kd[:C, h] = np.exp((C - 1.0 - i) * lg)
        T["dmT" + tag] = dm
        T["qd" + tag] = qd
        T["kd" + tag] = kd
    return T


def cdec(h, C):
    return float(np.exp(C * np.log1p(-2.0 ** (-5.0 - h))))


def build_program(nlayers=DEPTH, debug=False):
    nc = bass.Bass("TRN2", target_bir_lowering=False)
    S = Sched(nc)
    uid = [0]

    def nm(p):
        uid[0] += 1
        return "%s_%d" % (p, uid[0])

    def din(name, shape, dt=F32):
        return nc.dram_tensor(name, list(shape), dt, kind="ExternalInput").ap()

    def dout(name, shape, dt=F32):
        return nc.dram_tensor(name, list(shape), dt, kind="ExternalOutput").ap()

    def dscr(name, shape, dt):
        kind = "ExternalOutput" if (debug and name in ("xbuf", "tokb", "xmid_dbg")) else "Internal"
        return nc.dram_tensor(name, list(shape), dt, kind=kind).ap()

    xp = din("xp", [SEQ, D])
    xs = din("xs", [T_S, D])
    memp = din("memp", [256, D])
    win_state = din("win_state", [2, 512, 1024])
    ret_state = din("ret_state", [2, 1536, 256])
    conv_state = din("conv_state", [DEPTH, 128, FC, 2])
    mem_cache = din("mem_cache", [DEPTH, 256, 1024])
    cache = din("cache", [2, 1280 * 128, 2048])
    page_tab = din("page_tab", [1, NPAGE], I32)
    norm1_g = din("norm1_g", [DEPTH, D])
    norm2_g = din("norm2_g", [DEPTH, D])
    mem_norm_g = din("mem_norm_g", [DEPTH, D])
    final_g = din("final_g", [1, D])
    nsa_w_in = din("nsa_w_in", [2, D, NSA_IN])
    ret_w_in = din("ret_w_in", [2, D, RET_IN])
    ret_gn = din("ret_gn", [2, 1536])
    w_mem_kv = din("w_mem_kv", [DEPTH, D, 1024])
    w_o = din("w_o", [DEPTH, D, D])
    ffn_w_in = din("ffn_w_in", [DEPTH, D, 2 * FFN])
    ffn_w_out = din("ffn_w_out", [DEPTH, FFN, D])
    convp = din("convp", [DEPTH, 128, 4, FC])
    cmp_peT = din("cmp_peT", [2, 2, 128, 32])
    cmp_w1 = din("cmp_w1", [2, 2, 4096, 128])
    cmp_w2 = din("cmp_w2", [2, 2, 128, 128])
    tabs = {}
    TS = make_tables()
    for k_, v_ in TS.items():
        tabs[k_] = din("t_" + k_, list(v_.shape))

    o_y_p = dout("o_y_p", [SEQ, D])
    o_y_s = dout("o_y_s", [T_S, D])
    o_kv_p = dout("o_kv_p", [2, SEQ, 2048])
    o_kv_s = dout("o_kv_s", [2, T_S, 2048])
    o_win_p = dout("o_win_p", [2, 512, 1024])
    o_win_s = dout("o_win_s", [2, 512, 1024])
    o_ret_p = dout("o_ret_p", [2, 1536, 256])
    o_ret_s = dout("o_ret_s", [2, 1536, 256])
    o_conv_p = dout("o_conv_p", [DEPTH, 128, FC, 2])
    o_conv_s = dout("o_conv_s", [DEPTH, 128, FC, 2])
    o_mem_p = dout("o_mem_p", [DEPTH, 256, 1024])

    xbuf = dscr("xbuf", [NTOK, D], F32)
    r_xb = [Res(multi=True) for _ in range(9)]

    def xres(r0):
        return r_xb[min(r0 // 512, 8)]
    xmid_dbg = dscr("xmid_dbg", [NTOK, D], F32) if debug else None
    tokb = dscr("tokb", [NTOK, D], BF16)
    r_tokb = Res(multi=True)
    QT = dscr("QT", [12, 128, NTOK], BF16)
    r_QT = Res(multi=True)
    KT12 = dscr("KT12", [12, 128, NTOK], BF16)
    r_KT12 = Res(multi=True)
    rcT = dscr("rcT", [8, 128, SEQ], BF16)
    r_rcT = Res(multi=True)
    ksT = dscr("ksT", [4, 128, NTOK], BF16)
    r_ksT = Res(multi=True)
    kwT = dscr("kwT", [4, 128, NTOK], BF16)
    r_kwT = Res(multi=True)
    qmT = dscr("qmT", [4, 128, NTOK], BF16)
    r_qmT = Res(multi=True)
    gates = dscr("gates", [NTOK, 36], F32)
    r_gates = Res(multi=True)
    winr = dscr("winr", [NTOK, 1024], F32)
    r_winr = Res(multi=True)
    vtm = dscr("vtm", [NTOK, 1536], BF16)
    r_vtm = Res(multi=True)
    sgate = dscr("sgate", [NTOK, 1536], F32)
    r_sgate = Res(multi=True)
    r_okv = Res(multi=True)
    ksT_s = dscr("ksT_s", [4, 128, PAST], BF16)
    r_ksT_s = Res(multi=True)
    vs_s = dscr("vs_s", [4, PAST, 128], BF16)
    r_vs_s = Res(multi=True)

    def gsb(name, shape, dt):
        return nc.alloc_sbuf_tensor(name, list(shape), dt)

    ident_f = gsb("ident_f", [128, 128], F32)
    ident_b = gsb("ident_b", [128, 128], BF16)
    r_ident = Res()
    psA = Pool([nc.alloc_psum_tensor("psA%d" % i, [128, 512], F32) for i in range(4)])
    psB = Pool([nc.alloc_psum_tensor("psB%d" % i, [128, 512], F32) for i in range(2)])
    psT = Pool([nc.alloc_psum_tensor("psT%d" % i, [128, 1024], BF16) for i in range(2)])
    st_pool = Pool([gsb("stat%d" % i, [128, 8], F32) for i in range(6)])

    evac_rr = [0]

    def evac_engine():
        evac_rr[0] ^= 1
        return "act" if evac_rr[0] else "dve"

    def copy_op(e, out, in_, reads, writes):
        if e == "act":
            S.op("act", lambda a: a.copy(out, in_), reads, writes)
        else:
            S.op(e, lambda v: v.tensor_copy(out, in_), reads, writes)

    S.dma("sp", ident_f[:], tabs["ident"], writes=[r_ident])
    S.op("dve", lambda v: v.tensor_copy(ident_b[:], ident_f[:]), reads=[r_ident], writes=[r_ident])

    class Scope:
        def __init__(self):
            self.es = ExitStack()

        def __enter__(self):
            self.es.__enter__()
            return self

        def __exit__(self, *a):
            if a[0] is None:
                S.barrier()
            return self.es.__exit__(*a)

        def sb(self, name, shape, dt):
            return self.es.enter_context(nc.sbuf_tensor(nm(name), list(shape), dt))

        def pool(self, name, n, shape, dt):
            return Pool([self.sb(name, shape, dt) for _ in range(n)])

    BLOCKS = [(b * 512, 512) for b in range(SEQ // 512)] + [(SEQ, T_S)]

    def x_rows(layer, r0, n):
        if layer == 0:
            return xp[r0:r0 + n, :] if r0 < SEQ else xs[r0 - SEQ:r0 - SEQ + n, :]
        return xbuf[r0:r0 + n, :]

    class NormBufs:
        def __init__(self, sc):
            self.g_bc = sc.pool("g_bc", 2, [128, D], F32)
            self.xt = sc.pool("xt", 2, [128, D], F32)
            self.hb = sc.pool("hb", 2, [128, D], BF16)
            self.junk = sc.sb("junk", [128, D], BF16)
            self.r_junk = Res()

    def load_gain(nb, g_row_ap):
        t, r = nb.g_bc.get()
        S.dma("sp", t[:], g_row_ap.to_broadcast([128, D]), writes=[r])
        return t, r

    def rstd_of(xt, rx, P, nb):
        st, rs = st_pool.get()
        S.op("act", lambda a: a.activation(out=nb.junk[:P, :], in_=xt[:P, :], func=AF.Square,
                                           accum_out=st[:P, 0:1]),
             reads=[rx], writes=[nb.r_junk, rs])
        S.op("act", lambda a: a.activation(out=st[:P, 1:2], in_=st[:P, 0:1], func=AF.Sqrt,
                                           scale=1.0 / D, bias=EPS),
             reads=[rs], writes=[rs])
        S.op("dve", lambda v: v.reciprocal(st[:P, 2:3], st[:P, 1:2]), reads=[rs], writes=[rs])
        return st, rs

    def transpose_into(src_bf, rsrc, P, nchunks, dst_fn, rdst):
        for c0 in range(0, nchunks, 8):
            n = min(8, nchunks - c0)
            pt, rp = psT.get()
            fns = []
            for j in range(n):
                fns.append(lambda pe, j=j: pe.transpose(
                    pt[:, j * 128:j * 128 + P], src_bf[:P, (c0 + j) * 128:(c0 + j + 1) * 128],
                    ident_b[:P, :P]))
            S.mm_group(fns, reads=[rsrc, r_ident], writes=[rp])
            src = pt[:].rearrange("p (j q) -> p j q", q=128)[:, :n, :P]
            copy_op(evac_engine(), dst_fn(c0, n), src, [rp], [rdst])

    def norm_tile(nb, x_src_ap, xsrc_res, P, gt, gr, col0, hT_t, r_hT_t):
        xt, rx = nb.xt.get()
        S.dma("sp", xt[:P, :], x_src_ap, reads=xsrc_res, writes=[rx])
        st, rs = rstd_of(xt, rx, P, nb)
        hb, rh = nb.hb.get()
        S.op("dve", lambda v: v.scalar_tensor_tensor(out=hb[:P, :], in0=xt[:P, :], scalar=st[:P, 2:3],
                                                     in1=gt[:P, :], op0=ALU.mult, op1=ALU.mult),
             reads=[rx, rs, gr], writes=[rh])
        transpose_into(hb, rh, P, KC, lambda c0, n: hT_t[:, c0:c0 + n, col0:col0 + P], r_hT_t)

    def load_w(w_pool, W2d, col0, ncols, dst_col=0, tile=None):
        if tile is None:
            wt, rw = w_pool.get()
        else:
            wt, rw = tile
        src = W2d[:, col0:col0 + ncols].rearrange("(kc p) n -> p kc n", p=128)
        S.dma("pool", wt[:, :, dst_col:dst_col + ncols], src, writes=[rw])
        return wt, rw

    def linear_tm(w_pool, hT_t, r_hT_t, n, W2d, col0, ncols, sink):
        for cb0 in range(0, ncols, 512):
            cw = min(512, ncols - cb0)
            wt, rw = load_w(w_pool, W2d, col0 + cb0, cw)
            for t0 in range(0, n, 128):
                P = min(128, n - t0)
                ps, rp = psA.get()
                fns = []
                for kc in range(KC):
                    fns.append(lambda pe, kc=kc: pe.matmul(
                        ps[:P, :cw], hT_t[:, kc, t0:t0 + P], wt[:, kc, :cw],
                        start=(kc == 0), stop=(kc == KC - 1)))
                S.mm_group(fns, reads=[r_hT_t, rw], writes=[rp])
                sink(t0, P, cb0, cw, ps, rp)

    def linear_fm(w_pool, hT_t, r_hT_t, n, W2d, col0, nchunks, sink):
        for j0 in range(0, nchunks, 4):
            nj = min(4, nchunks - j0)
            wt, rw = load_w(w_pool, W2d, col0 + j0 * 128, nj * 128)
            for jj in range(nj):
                ps, rp = psA.get()
                fns = []
                for kc in range(KC):
                    fns.append(lambda pe, kc=kc: pe.matmul(
                        ps[:, :n], wt[:, kc, jj * 128:(jj + 1) * 128], hT_t[:, kc, :n],
                        start=(kc == 0), stop=(kc == KC - 1)))
                S.mm_group(fns, reads=[r_hT_t, rw], writes=[rp])
                sink(j0 + jj, ps, rp)

    def tm_sink_dram(stage_pool, dst2d, rdst, func=None, dt_stage=F32):
        def f(t0, P, cb0, cw, ps, rp):
            stg, rs = stage_pool.get()
            if func is None:
                copy_op(evac_engine(), stg[:P, :cw], ps[:P, :cw], [rp], [rs])
            else:
                S.op("act", lambda a: a.activation(out=stg[:P, :cw], in_=ps[:P, :cw], func=func),
                     reads=[rp], writes=[rs])
            S.dma("sp", dst2d[t0:t0 + P, cb0:cb0 + cw], stg[:P, :cw], reads=[rs], writes=rdst)
        return f

    def fm_sink_dram(stage_pool, dst3d, rdst, r0, n, add_tab=None):
        def f(j, ps, rp):
            stg, rs = stage_pool.get()
            if add_tab is None:
                copy_op(evac_engine(), stg[:, :n], ps[:, :n], [rp], [rs])
            else:
                tab, rt = add_tab(j)
                S.op("dve", lambda v: v.tensor_tensor(
                    out=stg[:, :n].rearrange("p (b j) -> p b j", j=32),
                    in0=ps[:, :n].rearrange("p (b j) -> p b j", j=32),
                    in1=tab.unsqueeze(1).to_broadcast([128, n // 32, 32]), op=ALU.add),
                    reads=[rp, rt], writes=[rs])
            S.dma("sp", dst3d[j, :, r0:r0 + n], stg[:, :n], reads=[rs], writes=rdst)
        return f

    def softmax_pv(ab, nq, nk, S_sb, rS, vchunk, out_fn):
        st, rst = st_pool.get()
        S.op("dve", lambda v: v.reduce_max(out=st[:nq, 0:1], in_=S_sb[:nq, :nk], axis=AX.X),
             reads=[rS], writes=[rst])
        S.op("dve", lambda v: v.tensor_scalar(out=st[:nq, 1:2], in0=st[:nq, 0:1], scalar1=-1.0e4,
                                              scalar2=-1.0, op0=ALU.max, op1=ALU.mult),
             reads=[rst], writes=[rst])
        P, rP = ab.P.get()
        S.op("act", lambda a: a.activation(out=P[:nq, :nk], in_=S_sb[:nq, :nk], func=AF.Exp,
                                           bias=st[:nq, 1:2], scale=1.0, accum_out=st[:nq, 2:3]),
             reads=[rS, rst], writes=[rP, rst])
        S.op("dve", lambda v: v.tensor_scalar(out=st[:nq, 3:4], in0=st[:nq, 2:3], scalar1=1.0e-30,
                                              scalar2=None, op0=ALU.max),
             reads=[rst], writes=[rst])
        S.op("dve", lambda v: v.reciprocal(st[:nq, 3:4], st[:nq, 3:4]), reads=[rst], writes=[rst])
        nch = (nk + 127) // 128
        PT, rPT = ab.PT.get()
        per_bank = 1024 // nq
        for c0 in range(0, nch, per_bank):
            n = min(per_bank, nch - c0)
            pt, rp = psT.get()
            fns = []
            for j in range(n):
                c = c0 + j
                kk = min(128, nk - c * 128)
                fns.append(lambda pe, j=j, c=c, kk=kk: pe.transpose(
                    pt[:kk, j * nq:(j + 1) * nq], P[:nq, c * 128:c * 128 + kk], ident_b[:nq, :nq]))
            S.mm_group(fns, reads=[rP, r_ident], writes=[rp])
            copy_op(evac_engine(), PT[:, c0 * nq:(c0 + n) * nq], pt[:, :n * nq], [rp], [rPT])
        ps, rp2 = psB.get()
        fns = []
        vres = []
        for c in range(nch):
            vap, vr, kk = vchunk(c)
            if vr not in vres:
                vres.append(vr)
            fns.append(lambda pe, c=c, vap=vap, kk=kk: pe.matmul(
                ps[:nq, :128], PT[:kk, c * nq:(c + 1) * nq], vap,
                start=(c == 0), stop=(c == nch - 1)))
        S.mm_group(fns, reads=[rPT] + vres, writes=[rp2])
        out_fn(ps, rp2, st, rst)

    class AttnBufs:
        def __init__(self, sc, nq, nkmax):
            self.S = sc.pool("S_sb", 2 if nq > 8 else 1, [nq, nkmax], F32)
            self.P = sc.pool("P_bf", 2 if nq > 8 else 1, [nq, nkmax], BF16)
            nch = (nkmax + 127) // 128
            self.PT = sc.pool("PT", 2 if nq > 8 else 1, [128, nch * nq], BF16)

    def topk_selneg(sc_pool, score, rscore, nq, nblk, selneg, rsel):
        st, rst = st_pool.get()
        m8, rm8 = sc_pool.get()
        S.op("dve", lambda v: v.max(out=m8[:nq, 0:8], in_=score[:nq, :nblk]),
             reads=[rscore], writes=[rm8])
        S.op("dve", lambda v: v.match_replace(out=m8[:nq, 16:16 + nblk], in_to_replace=m8[:nq, 0:8],
                                              in_values=score[:nq, :nblk], imm_value=-3.0e38),
             reads=[rscore, rm8], writes=[rm8])
        S.op("dve", lambda v: v.max(out=m8[:nq, 8:16], in_=m8[:nq, 16:16 + nblk]),
             reads=[rm8], writes=[rm8])
        S.op("dve", lambda v: v.tensor_scalar(out=selneg[:nq, :nblk], in0=score[:nq, :nblk],
                                              scalar1=m8[:nq, 15:16], scalar2=None, op0=ALU.is_ge),
             reads=[rscore, rm8], writes=[rsel])
        S.op("dve", lambda v: v.tensor_scalar(out=selneg[:nq, :nblk], in0=selneg[:nq, :nblk],
                                              scalar1=-1.0, scalar2=-NEG, op0=ALU.add, op1=ALU.mult),
             reads=[rsel], writes=[rsel])

    def mem_prepare(layer, mk, w_pool, nb, stage_pool, hT, r_hT):
        gt, gr = load_gain(nb, mem_norm_g[layer:layer + 1, :])
        for t in range(2):
            norm_tile(nb, memp[t * 128:(t + 1) * 128, :], [], 128, gt, gr, t * 128, hT, r_hT)

        def sink_tm(t0, P, cb0, cw, ps, rp):
            stg, rs = stage_pool.get()
            copy_op(evac_engine(), stg[:P, :cw], ps[:P, :cw], [rp], [rs])
            S.dma("sp", o_mem_p[layer][t0:t0 + P, cb0:cb0 + cw], stg[:P, :cw], reads=[rs])
            if cb0 == 512:
                S.op("pool", lambda g: g.tensor_copy(mk["pV"][:, t0 // 128, :], stg[:, :512]),
                     reads=[rs], writes=[mk["r_pV"]])
        linear_tm(w_pool, hT, r_hT, 256, w_mem_kv[layer], 0, 1024, sink_tm)

        def sink_fm(j, ps, rp):
            copy_op(evac_engine(), mk["pKT"][:, j, :], ps[:, :256], [rp], [mk["r_pKT"]])
        linear_fm(w_pool, hT, r_hT, 256, w_mem_kv[layer], 0, 4, sink_fm)
        for t in range(2):
            xt, rx = nb.xt.get()
            S.dma("sp", xt[:, :1024], mem_cache[layer][t * 128:(t + 1) * 128, :], writes=[rx])
            S.op("pool", lambda g: g.tensor_copy(mk["sV"][:, t, :], xt[:, 512:1024]),
                 reads=[rx], writes=[mk["r_sV"]])
            hb, rh = nb.hb.get()
            S.op("dve", lambda v: v.tensor_copy(hb[:, :512], xt[:, :512]), reads=[rx], writes=[rh])
            transpose_into(hb, rh, 128, 4,
                           lambda c0, n: mk["sKT"][:, c0:c0 + n, t * 128:(t + 1) * 128], mk["r_sKT"])

    def phaseA_nsa(layer, mk):
        a = layer // 2
        W = nsa_w_in[a]
        with Scope() as sc:
            nb = NormBufs(sc)
            hT = sc.sb("hT", [128, KC, 512], BF16)
            r_hT = Res()
            w_pool = sc.pool("wbuf", 3, [128, KC, 512], BF16)
            stage = sc.pool("stage", 4, [128, 512], F32)
            stage_b = sc.pool("stageb", 4, [128, 512], BF16)
            peT = sc.sb("peT", [128, 2, 32], F32)
            r_peT = Res()
            S.dma("sp", peT[:], cmp_peT[a].rearrange("t d j -> d t j"), writes=[r_peT])
            mem_prepare(layer, mk, w_pool, nb, stage, hT, r_hT)
            gt, gr = load_gain(nb, norm1_g[layer:layer + 1, :])
            for (r0, n) in BLOCKS:
                samp = r0 >= SEQ
                for t0 in range(0, n, 128):
                    P = min(128, n - t0)
                    norm_tile(nb, x_rows(layer, r0 + t0, P), [xres(r0)] if layer > 0 else [], P, gt, gr,
                              t0, hT, r_hT)
                okv = o_kv_s[a] if samp else o_kv_p[a][r0:r0 + n, :]
                linear_tm(w_pool, hT, r_hT, n, W, 1536, 2048, tm_sink_dram(stage, okv, [r_okv]))
                linear_tm(w_pool, hT, r_hT, n, W, 3584, 1024,
                          tm_sink_dram(stage, winr[r0:r0 + n, :], [r_winr]))
                linear_tm(w_pool, hT, r_hT, n, W, 4608, 36,
                          tm_sink_dram(stage, gates[r0:r0 + n, :], [r_gates], func=AF.Sigmoid))
                linear_fm(w_pool, hT, r_hT, n, W, 0, 12, fm_sink_dram(stage_b, QT, [r_QT], r0, n))
                if not samp:
                    linear_fm(w_pool, hT, r_hT, n, W, 1536, 8,
                              fm_sink_dram(stage_b, rcT, [r_rcT], r0, n,
                                           add_tab=lambda j: (peT[:, j // 4, :], r_peT)))
                linear_fm(w_pool, hT, r_hT, n, W, 2560, 4, fm_sink_dram(stage_b, ksT, [r_ksT], r0, n))
                linear_fm(w_pool, hT, r_hT, n, W, 3584, 4, fm_sink_dram(stage_b, kwT, [r_kwT], r0, n))
                linear_fm(w_pool, hT, r_hT, n, W, 4644, 4, fm_sink_dram(stage_b, qmT, [r_qmT], r0, n))
            S.dma("sp", o_win_p[a], winr[SEQ - 512:SEQ, :], reads=[r_winr])
            S.dma("sp", o_win_s[a][504:512, :], winr[SEQ:SEQ + T_S, :], reads=[r_winr])
            S.dma("sp", o_win_s[a][0:504, :], win_state[a][8:512, :])

    def phaseA_ret(layer, mk):
        bl = layer // 2
        W = ret_w_in[bl]
        with Scope() as sc:
            nb = NormBufs(sc)
            hT = sc.sb("hT", [128, KC, 512], BF16)
            r_hT = Res()
            w_pool = sc.pool("wbuf", 3, [128, KC, 512], BF16)
            stage = sc.pool("stage", 4, [128, 512], F32)
            stage_b = sc.pool("stageb", 4, [128, 512], BF16)
            tmp = sc.pool("rot", 4, [128, 512], F32)
            cosb = sc.sb("cosb", [128, 512], F32)
            sinb = sc.sb("sinb", [128, 512], F32)
            r_cs = Res()
            mem_prepare(layer, mk, w_pool, nb, stage, hT, r_hT)
            gt, gr = load_gain(nb, norm1_g[layer:layer + 1, :])
            for (r0, n) in BLOCKS:
                for t0 in range(0, n, 128):
                    P = min(128, n - t0)
                    norm_tile(nb, x_rows(layer, r0 + t0, P), [xres(r0)], P, gt, gr, t0, hT, r_hT)
                S.dma("sp", cosb[:, :n], tabs["cosT"][:, r0:r0 + n], writes=[r_cs])
                S.dma("sp", sinb[:, :n], tabs["sinT"][:, r0:r0 + n], writes=[r_cs])

                def rot_sink(dst, rdst, scl):
                    held = {}

                    def f(j, ps, rp):
                        if j % 2 == 0:
                            held["x1"] = (ps, rp)
                            return
                        p1, r1 = held["x1"]
                        p2, r2 = ps, rp
                        ta, ra = tmp.get()
                        tb, rb = tmp.get()
                        o1, ro1 = stage_b.get()
                        o2, ro2 = stage_b.get()
                        S.op("dve", lambda v: v.scalar_tensor_tensor(
                            out=ta[:, :n], in0=p1[:, :n], scalar=scl, in1=cosb[:, :n],
                            op0=ALU.mult, op1=ALU.mult), reads=[r1, r_cs], writes=[ra])
                        S.op("dve", lambda v: v.scalar_tensor_tensor(
                            out=tb[:, :n], in0=p2[:, :n], scalar=scl, in1=sinb[:, :n],
                            op0=ALU.mult, op1=ALU.mult), reads=[r2, r_cs], writes=[rb])
                        S.op("pool", lambda g: g.tensor_tensor(out=o1[:, :n], in0=ta[:, :n], in1=tb[:, :n],
                                                               op=ALU.subtract),
                             reads=[ra, rb], writes=[ro1])
                        tc_, rc_ = tmp.get()
                        td, rd = tmp.get()
                        S.op("dve", lambda v: v.scalar_tensor_tensor(
                            out=tc_[:, :n], in0=p1[:, :n], scalar=scl, in1=sinb[:, :n],
                            op0=ALU.mult, op1=ALU.mult), reads=[r1, r_cs], writes=[rc_])
                        S.op("dve", lambda v: v.scalar_tensor_tensor(
                            out=td[:, :n], in0=p2[:, :n], scalar=scl, in1=cosb[:, :n],
                            op0=ALU.mult, op1=ALU.mult), reads=[r2, r_cs], writes=[rd])
                        S.op("pool", lambda g: g.tensor_tensor(out=o2[:, :n], in0=tc_[:, :n], in1=td[:, :n],
                                                               op=ALU.add),
                             reads=[rc_, rd], writes=[ro2])
                        S.dma("sp", dst[j - 1, :, r0:r0 + n], o1[:, :n], reads=[ro1], writes=rdst)
                        S.dma("sp", dst[j, :, r0:r0 + n], o2[:, :n], reads=[ro2], writes=rdst)
                    return f
                linear_fm(w_pool, hT, r_hT, n, W, 0, 12, rot_sink(QT, [r_QT], 1.0))
                linear_fm(w_pool, hT, r_hT, n, W, 1536, 12, rot_sink(KT12, [r_KT12], 1.0 / 16.0))
                linear_fm(w_pool, hT, r_hT, n, W, 6144, 4, fm_sink_dram(stage_b, qmT, [r_qmT], r0, n))

                def v_sink(t0, P, cb0, cw, ps, rp):
                    stg, rs = stage_b.get()
                    copy_op(evac_engine(), stg[:P, :cw], ps[:P, :cw], [rp], [rs])
                    S.dma("sp", vtm[r0 + t0:r0 + t0 + P, cb0:cb0 + cw], stg[:P, :cw], reads=[rs],
                          writes=[r_vtm])
                linear_tm(w_pool, hT, r_hT, n, W, 3072, 1536, v_sink)
                linear_tm(w_pool, hT, r_hT, n, W, 4608, 1536,
                          tm_sink_dram(stage, sgate[r0:r0 + n, :], [r_sgate], func=AF.Silu))

    def gelu_tanh(sc_tmp, ps, rp, n, dst, rdst):
        x, rx = sc_tmp.get()
        u, ru = sc_tmp.get()
        copy_op("act", x[:, :n], ps[:, :n], [rp], [rx])
        S.op("dve", lambda v: v.tensor_tensor(out=u[:, :n], in0=x[:, :n], in1=x[:, :n], op=ALU.mult),
             reads=[rx], writes=[ru])
        S.op("dve", lambda v: v.tensor_scalar(out=u[:, :n], in0=u[:, :n], scalar1=0.044715, scalar2=1.0,
                                              op0=ALU.mult, op1=ALU.add), reads=[ru], writes=[ru])
        S.op("dve", lambda v: v.tensor_tensor(out=u[:, :n], in0=u[:, :n], in1=x[:, :n], op=ALU.mult),
             reads=[ru, rx], writes=[ru])
        S.op("act", lambda a: a.activation(out=u[:, :n], in_=u[:, :n], func=AF.Sigmoid,
                                           scale=2.0 * 0.7978845608028654),
             reads=[ru], writes=[ru])
        S.op("dve", lambda v: v.tensor_tensor(out=dst, in0=u[:, :n], in1=x[:, :n], op=ALU.mult),
             reads=[ru, rx], writes=[rdst])

    def compress(sc, a, ty, rc_ap, rrc, nblk, kcT_dst, vc_dst, rdst, w1t, w2t, rw, tmp):
        ps, rp = psA.get()
        rc3 = rc_ap.rearrange("p (b j) -> p j b", j=32)
        fns = []
        for j in range(32):
            fns.append(lambda pe, j=j: pe.matmul(ps[:, :nblk], w1t[:, ty, j, :], rc3[:, j, :],
                                                 start=(j == 0), stop=(j == 31)))
        S.mm_group(fns, reads=[rrc, rw], writes=[rp])
        gT, rg = tmp["g"].get()
        gelu_tanh(tmp["f"], ps, rp, nblk, gT[:, :nblk], rg)
        if ty == 0:
            ps2, rp2 = psA.get()
            S.mm_group([lambda pe: pe.matmul(ps2[:, :nblk], w2t[:, 0, :], gT[:, :nblk], start=True, stop=True)],
                       reads=[rg, rw], writes=[rp2])
            copy_op(evac_engine(), kcT_dst, ps2[:, :nblk], [rp2], [rdst])
        else:
            for c in range(nblk // 128):
                ps2, rp2 = psA.get()
                S.mm_group([lambda pe, c=c: pe.matmul(ps2[:, :128], gT[:, c * 128:(c + 1) * 128], w2t[:, 1, :],
                                                      start=True, stop=True)],
                           reads=[rg, rw], writes=[rp2])
                copy_op(evac_engine(), vc_dst(c), ps2[:, :128], [rp2], [rdst])

    def load_cmp_weights(sc, a):
        w1t = sc.sb("w1t", [128, 2, 32, 128], BF16)
        w2t = sc.sb("w2t", [128, 2, 128], BF16)
        rw = Res()
        for ty in range(2):
            S.dma("pool", w1t[:, ty, :, :], cmp_w1[a][ty].rearrange("(j d) o -> d j o", d=128), writes=[rw])
            S.dma("pool", w2t[:, ty, :], cmp_w2[a][ty], writes=[rw])
        return w1t, w2t, rw

    def nsa_mix_prompt(layer):
        a = layer // 2
        with Scope() as sc:
            ab = AttnBufs(sc, 128, 4096)
            cmpmask_t = sc.sb("cmpmask", [128, 32, 128], F32)
            bonus_t = sc.sb("bonus", [128, 32, 64], F32)
            tri_le = sc.sb("tri_le", [128, 128], F32)
            tri_gt = sc.sb("tri_gt", [128, 128], F32)
            gates_t = sc.sb("gates_t", [128, 32, 36], F32)
            r_tab = Res()
            S.dma("sp", cmpmask_t[:], tabs["cmpmask"].rearrange("(t p) n -> p t n", p=128), writes=[r_tab])
            S.dma("sp", bonus_t[:], tabs["bonus"].rearrange("(t p) n -> p t n", p=128), writes=[r_tab])
            S.dma("sp", tri_le[:], tabs["tri_le"], writes=[r_tab])
            S.dma("sp", tri_gt[:], tabs["tri_gt"], writes=[r_tab])
            S.dma("sp", gates_t[:], gates[0:SEQ, :].rearrange("(t p) c -> p t c", p=128),
                  reads=[r_gates], writes=[r_tab])
            kcT = sc.sb("kcT", [128, 4, 128], BF16)
            vc = sc.sb("vc", [128, 4, 128], BF16)
            r_kv = Res()
            with Scope() as scc:
                w1t, w2t, rw = load_cmp_weights(scc, a)
                tmp = {"g": scc.pool("gT", 2, [128, 512], BF16), "f": scc.pool("gf", 4, [128, 512], F32)}
                rcb = scc.pool("rcb", 2, [128, SEQ], BF16)
                for g in range(4):
                    for ty in range(2):
                        rc, rrc = rcb.get()
                        S.dma("sp", rc[:], rcT[ty * 4 + g], reads=[r_rcT], writes=[rrc])
                        compress(scc, a, ty, rc[:], rrc, 128, kcT[:, g, :], lambda c, g=g: vc[:, g, :], r_kv,
                                 w1t, w2t, rw, tmp)
            QTg = sc.sb("QTg", [128, 3, SEQ], BF16)
            ksTg = sc.sb("ksTg", [128, SEQ], BF16)
            kwTg = sc.sb("kwTg", [128, SEQ], BF16)
            vsg = sc.sb("vsg", [128, 32, 128], BF16)
            vwg = sc.sb("vwg", [128, 32, 128], BF16)
            r_grp = Res()
            small = sc.pool("small", 6, [128, 128], F32)
            pbf = sc.pool("pbf", 2, [128, 128], BF16)
            ptc = sc.pool("ptc", 2, [128, 128], BF16)
            sel_pool = sc.pool("selp", 2, [128, 64], F32)
            m8_pool = sc.pool("m8", 2, [128, 16 + 64], F32)
            ocomb_pool = sc.pool("ocomb", 2, [128, 384], F32)
            ob_pool = sc.pool("ob", 2, [128, 384], BF16)
            for g in range(4):
                S.dma("sp", QTg[:], QT[3 * g:3 * g + 3, :, 0:SEQ].rearrange("h p n -> p h n"),
                      reads=[r_QT], writes=[r_grp])
                S.dma("sp", ksTg[:], ksT[g, :, 0:SEQ], reads=[r_ksT], writes=[r_grp])
                S.dma("sp", kwTg[:], kwT[g, :, 0:SEQ], reads=[r_kwT], writes=[r_grp])
                S.dma("pool", vsg[:], o_kv_p[a][:, 1536 + g * 128:1536 + (g + 1) * 128].rearrange(
                    "(c p) d -> p c d", p=128), reads=[r_okv], writes=[r_grp])
                S.dma("pool", vwg[:], winr[0:SEQ, 512 + g * 128:512 + (g + 1) * 128].rearrange(
                    "(c p) d -> p c d", p=128), reads=[r_winr], writes=[r_grp])
                for i in range(32):
                    qs = slice(i * 128, (i + 1) * 128)
                    oc, roc = ocomb_pool.get()
                    pgrp, rpg = small.get()
                    first = [True]

                    def gated_out(col, hh, oc=oc, roc=roc, i=i):
                        h = 3 * g + hh
                        gcol = gates_t[:, i, h * 3 + col:h * 3 + col + 1]

                        def f(ps, rp, st, rst):
                            if st is not None:
                                S.op("dve", lambda v: v.tensor_tensor(out=st[:, 4:5], in0=st[:, 3:4], in1=gcol,
                                                                      op=ALU.mult),
                                     reads=[rst, r_tab], writes=[rst])
                                sc_ap, rr = st[:, 4:5], [rst]
                            else:
                                sc_ap, rr = gcol, [r_tab]
                            dst = oc[:, hh * 128:(hh + 1) * 128]
                            if col == 0:
                                S.op("dve", lambda v: v.tensor_scalar(out=dst, in0=ps[:, :128], scalar1=sc_ap,
                                                                      scalar2=None, op0=ALU.mult),
                                     reads=[rp] + rr, writes=[roc])
                            else:
                                S.op("dve", lambda v: v.scalar_tensor_tensor(
                                    out=dst, in0=ps[:, :128], scalar=sc_ap, in1=dst, op0=ALU.mult, op1=ALU.add),
                                    reads=[rp] + rr, writes=[roc])
                        return f
                    for hh in range(3):
                        ps, rp = psA.get()
                        S.mm_group([lambda pe, hh=hh: pe.matmul(ps[:, :128], QTg[:, hh, qs], kcT[:, g, :],
                                                                start=True, stop=True)],
                                   reads=[r_grp, r_kv], writes=[rp])
                        sc_t, rsc = small.get()
                        S.op("dve", lambda v: v.scalar_tensor_tensor(
                            out=sc_t[:], in0=ps[:, :128], scalar=SCALE, in1=cmpmask_t[:, i, :],
                            op0=ALU.mult, op1=ALU.add), reads=[rp, r_tab], writes=[rsc])
                        st, rst = st_pool.get()
                        S.op("dve", lambda v: v.reduce_max(out=st[:, 0:1], in_=sc_t[:], axis=AX.X),
                             reads=[rsc], writes=[rst])
                        S.op("dve", lambda v: v.tensor_scalar(out=st[:, 1:2], in0=st[:, 0:1], scalar1=-1.0e4,
                                                              scalar2=-1.0, op0=ALU.max, op1=ALU.mult),
                             reads=[rst], writes=[rst])
                        S.op("act", lambda a_: a_.activation(out=sc_t[:], in_=sc_t[:], func=AF.Exp,
                                                             bias=st[:, 1:2], scale=1.0, accum_out=st[:, 2:3]),
                             reads=[rsc, rst], writes=[rsc, rst])
                        S.op("dve", lambda v: v.tensor_scalar(out=st[:, 3:4], in0=st[:, 2:3], scalar1=1.0e-30,
                                                              scalar2=None, op0=ALU.max), reads=[rst], writes=[rst])
                        S.op("dve", lambda v: v.reciprocal(st[:, 3:4], st[:, 3:4]), reads=[rst], writes=[rst])
                        S.op("dve", lambda v: v.tensor_scalar(out=sc_t[:], in0=sc_t[:], scalar1=st[:, 3:4],
                                                              scalar2=None, op0=ALU.mult),
                             reads=[rsc, rst], writes=[rsc])
                        if hh == 0:
                            S.op("pool", lambda g_: g_.tensor_copy(pgrp[:], sc_t[:]), reads=[rsc], writes=[rpg])
                        else:
                            S.op("pool", lambda g_: g_.tensor_tensor(out=pgrp[:], in0=pgrp[:], in1=sc_t[:],
                                                                     op=ALU.add), reads=[rsc], writes=[rpg])
                        pb, rpb = pbf.get()
                        S.op("act", lambda a_: a_.copy(pb[:], sc_t[:]), reads=[rsc], writes=[rpb])
                        pt, rpt = psT.get()
                        S.mm_group([lambda pe: pe.transpose(pt[:, :128], pb[:], ident_b[:])],
                                   reads=[rpb, r_ident], writes=[rpt])
                        pc, rpc = ptc.get()
                        copy_op("act", pc[:], pt[:, :128], [rpt], [rpc])
                        ps2, rp2 = psB.get()
                        S.mm_group([lambda pe: pe.matmul(ps2[:, :128], pc[:], vc[:, g, :], start=True, stop=True)],
                                   reads=[rpc, r_kv], writes=[rp2])
                        gated_out(0, hh)(ps2, rp2, None, None)
                    score, rscore = sel_pool.get()
                    pg3 = pgrp[:].rearrange("p (b two) -> p b two", two=2)
                    S.op("dve", lambda v: v.tensor_tensor(out=score[:], in0=pg3[:, :, 0], in1=pg3[:, :, 1],
                                                          op=ALU.add), reads=[rpg], writes=[rscore])
                    S.op("dve", lambda v: v.tensor_tensor(out=score[:], in0=score[:], in1=bonus_t[:, i, :],
                                                          op=ALU.add), reads=[rscore, r_tab], writes=[rscore])
                    selneg, rsel = sel_pool.get()
                    topk_selneg(m8_pool, score, rscore, 128, 64, selneg, rsel)
                    for hh in range(3):
                        nk = (i + 1) * 128
                        Ssb, rS = ab.S.get()
                        for kb in range(0, nk, 512):
                            w = min(512, nk - kb)
                            ps, rp = psA.get()
                            S.mm_group([lambda pe, hh=hh, kb=kb, w=w: pe.matmul(
                                ps[:, :w], QTg[:, hh, qs], ksTg[:, kb:kb + w], start=True, stop=True)],
                                reads=[r_grp], writes=[rp])
                            nb_ = w // 64
                            S.op("dve", lambda v, kb=kb, w=w, nb_=nb_: v.scalar_tensor_tensor(
                                out=Ssb[:, kb:kb + w].rearrange("p (b k) -> p b k", k=64),
                                in0=ps[:, :w].rearrange("p (b k) -> p b k", k=64), scalar=SCALE,
                                in1=selneg[:, kb // 64:kb // 64 + nb_].unsqueeze(2).to_broadcast([128, nb_, 64]),
                                op0=ALU.mult, op1=ALU.add), reads=[rp, rsel], writes=[rS])
                        S.op("pool", lambda g_: g_.tensor_tensor(out=Ssb[:, i * 128:(i + 1) * 128],
                                                                 in0=Ssb[:, i * 128:(i + 1) * 128], in1=tri_le[:],
                                                                 op=ALU.add), reads=[r_tab], writes=[rS])
                        softmax_pv(ab, 128, nk, Ssb, rS, lambda c: (vsg[:, c, :], r_grp, 128), gated_out(1, hh))
                        c0 = max(0, i - 4)
                        nk = (i + 1 - c0) * 128
                        Ssb, rS = ab.S.get()
                        for kb in range(0, nk, 512):
                            w = min(512, nk - kb)
                            ps, rp = psA.get()
                            S.mm_group([lambda pe, hh=hh, kb=kb, w=w, c0=c0: pe.matmul(
                                ps[:, :w], QTg[:, hh, qs], kwTg[:, c0 * 128 + kb:c0 * 128 + kb + w],
                                start=True, stop=True)], reads=[r_grp], writes=[rp])
                            S.op("act", lambda a_, kb=kb, w=w: a_.activation(
                                out=Ssb[:, kb:kb + w], in_=ps[:, :w], func=AF.Identity, scale=SCALE),
                                reads=[rp], writes=[rS])
                        if i >= 4:
                            S.op("pool", lambda g_: g_.tensor_tensor(out=Ssb[:, 0:128], in0=Ssb[:, 0:128],
                                                                     in1=tri_gt[:], op=ALU.add),
                                 reads=[r_tab], writes=[rS])
                        S.op("pool", lambda g_, nk=nk: g_.tensor_tensor(out=Ssb[:, nk - 128:nk], in0=Ssb[:, nk - 128:nk],
                                                                        in1=tri_le[:], op=ALU.add),
                             reads=[r_tab], writes=[rS])
                        softmax_pv(ab, 128, nk, Ssb, rS, lambda c, c0=c0: (vwg[:, c0 + c, :], r_grp, 128),
                                   gated_out(2, hh))
                    ob, rob = ob_pool.get()
                    S.op("act", lambda a_: a_.copy(ob[:], oc[:]), reads=[roc], writes=[rob])
                    S.dma("sp", tokb[i * 128:(i + 1) * 128, g * 384:(g + 1) * 384], ob[:], reads=[rob],
                          writes=[r_tokb])

    def nsa_mix_sample(layer):
        a = layer // 2
        cache2d = cache.rearrange("a r c -> (a r) c")
        with Scope() as sc:
            kcT = sc.sb("kcT_s", [128, 4, 512], BF16)
            vc = sc.sb("vc_s", [128, 4, 4, 128], BF16)
            r_kv = Res()
            peT = sc.sb("peT_s", [128, 2, 32], F32)
            r_peT = Res()
            S.dma("sp", peT[:], cmp_peT[a].rearrange("t d j -> d t j"), writes=[r_peT])
            idx = sc.sb("idx", [128, NPAGE], I32)
            idxf = sc.sb("idxf", [128, NPAGE], F32)
            iot = sc.sb("iot", [128, 1], F32)
            r_idx = Res()
            S.dma("sp", idx[:], page_tab[0:1, :].to_broadcast([128, NPAGE]), writes=[r_idx])
            S.op("pool", lambda g_: g_.iota(iot[:], pattern=[[0, 1]], base=0, channel_multiplier=1,
                                            allow_small_or_imprecise_dtypes=True), writes=[r_idx])
            S.op("dve", lambda v: v.tensor_copy(idxf[:], idx[:]), reads=[r_idx], writes=[r_idx])
            S.op("dve", lambda v: v.tensor_scalar(out=idxf[:], in0=idxf[:], scalar1=128.0, scalar2=iot[:, 0:1],
                                                  op0=ALU.mult, op1=ALU.add), reads=[r_idx], writes=[r_idx])
            if a > 0:
                S.op("dve", lambda v: v.tensor_scalar(out=idxf[:], in0=idxf[:], scalar1=float(a * 1280 * 128),
                                                      scalar2=None, op0=ALU.add), reads=[r_idx], writes=[r_idx])
            S.op("dve", lambda v: v.tensor_copy(idx[:], idxf[:]), reads=[r_idx], writes=[r_idx])
            with Scope() as sc1:
                w1t, w2t, rw = load_cmp_weights(sc1, a)
                tmp = {"g": sc1.pool("gT", 2, [128, 512], BF16), "f": sc1.pool("gf", 4, [128, 512], F32)}
                page_pool = sc1.pool("page", 3, [128, 2048], F32)
                rcb = sc1.sb("rcb_s", [128, 8, 4096], BF16)
                r_rcb = Res()
                ksst = sc1.pool("ksst", 2, [128, 4, 512], BF16)
                vsst = sc1.pool("vsst", 2, [128, 4, 4, 128], BF16)
                psF = psB
                for batch in range(4):
                    for pq in range(8):
                        kst, rks = ksst.get()
                        vst, rvs = vsst.get()
                        for pp in range(4):
                            pg = batch * 32 + pq * 4 + pp
                            pt_, rpg = page_pool.get()
                            S.dma("pool", None, None, reads=[r_idx], writes=[rpg],
                                  fn=lambda g_, pt_=pt_, pg=pg: g_.indirect_dma_start(
                                      out=pt_[:, :], out_offset=None, in_=cache2d[:, :],
                                      in_offset=bass.IndirectOffsetOnAxis(ap=idx[:, pg:pg + 1], axis=0)))
                            S.op("pool", lambda g_, pt_=pt_, pp=pp: g_.tensor_copy(
                                vst[:, pp, :, :], pt_[:, 1536:2048].rearrange("p (g d) -> p g d", d=128)),
                                reads=[rpg], writes=[rvs])
                            for ty in range(3):
                                ps, rp = psF.get()
                                fns = []
                                for gg in range(4):
                                    col = (ty * 4 + gg) * 128
                                    fns.append(lambda pe, gg=gg, col=col, pt_=pt_: pe.transpose(
                                        ps[:, gg * 128:(gg + 1) * 128], pt_[:, col:col + 128], ident_f[:]))
                                S.mm_group(fns, reads=[rpg, r_ident], writes=[rp])
                                src = ps[:].rearrange("p (g r) -> p g r", r=128)
                                loc = (pq * 4 + pp) * 128
                                if ty < 2:
                                    S.op("dve", lambda v, ty=ty, loc=loc, src=src: v.tensor_tensor(
                                        out=rcb[:, ty * 4:(ty + 1) * 4, loc:loc + 128].rearrange(
                                            "p g (b j) -> p g b j", j=32),
                                        in0=src.rearrange("p g (b j) -> p g b j", j=32),
                                        in1=peT[:, ty, :].unsqueeze(1).unsqueeze(1).to_broadcast([128, 4, 4, 32]),
                                        op=ALU.add), reads=[rp, r_peT], writes=[r_rcb])
                                else:
                                    copy_op("act", kst[:, :, pp * 128:(pp + 1) * 128], src, [rp], [rks])
                        p0 = (batch * 32 + pq * 4) * 128
                        S.dma("sp", ksT_s[:, :, p0:p0 + 512].rearrange("g p n -> p g n"), kst[:], reads=[rks],
                              writes=[r_ksT_s])
                        for gg in range(4):
                            S.dma("sp", vs_s[gg, p0:p0 + 512, :].rearrange("(q r) d -> r q d", r=128),
                                  vst[:, :, gg, :], reads=[rvs], writes=[r_vs_s])
                    for g in range(4):
                        for ty in range(2):
                            compress(sc1, a, ty, rcb[:, ty * 4 + g, :], r_rcb, 128,
                                     kcT[:, g, batch * 128:(batch + 1) * 128],
                                     lambda c, g=g, batch=batch: vc[:, g, batch, :], r_kv, w1t, w2t, rw, tmp)
            with Scope() as sc2:
                NK = PAST + T_S
                ab = AttnBufs(sc2, T_S, NK)
                s_bonus = sc2.sb("s_bonus", [T_S, 264], F32)
                s_tri8 = sc2.sb("s_tri8", [T_S, 8], F32)
                s_winm = sc2.sb("s_winm", [T_S, 512], F32)
                gates_t = sc2.sb("gates_s", [T_S, 36], F32)
                r_tab = Res()
                S.dma("sp", s_bonus[:], tabs["s_bonus"][0:T_S, :], writes=[r_tab])
                S.dma("sp", s_tri8[:], tabs["s_tri8"][0:T_S, :], writes=[r_tab])
                S.dma("sp", s_winm[:], tabs["s_winmask"][0:T_S, :], writes=[r_tab])
                S.dma("sp", gates_t[:], gates[SEQ:NTOK, :], reads=[r_gates], writes=[r_tab])
                QTg = sc2.sb("QTg_s", [128, 3, T_S], BF16)
                ksTg = sc2.sb("ksTg_s", [128, NK], BF16)
                vsg = sc2.sb("vsg_s", [128, 129, 128], BF16)
                kwTg = sc2.sb("kwTg_s", [128, 520], BF16)
                vwg = sc2.sb("vwg_s", [128, 5, 128], BF16)
                wst = sc2.sb("wst", [128, 4, 128], F32)
                wsb = sc2.sb("wsb", [128, 512], BF16)
                r_ws = Res()
                r_grp = Res()
                small = sc2.pool("small_s", 3, [T_S, 512], F32)
                pgrp_t = sc2.sb("pgrp_s", [T_S, 512], F32)
                r_pgrp = Res()
                pbf = sc2.pool("pbf_s", 2, [T_S, 512], BF16)
                ptc = sc2.pool("ptc_s", 2, [128, 4 * T_S], BF16)
                sel_pool = sc2.pool("selp_s", 2, [T_S, 264], F32)
                m8_pool = sc2.pool("m8_s", 2, [T_S, 16 + 264], F32)
                oc = sc2.sb("ocomb_s", [T_S, 384], F32)
                roc = Res()
                ob = sc2.sb("ob_s", [T_S, 384], BF16)
                for g in range(4):
                    S.dma("sp", QTg[:], QT[3 * g:3 * g + 3, :, SEQ:NTOK].rearrange("h p n -> p h n"),
                          reads=[r_QT], writes=[r_grp])
                    S.dma("sp", ksTg[:, 0:PAST], ksT_s[g], reads=[r_ksT_s], writes=[r_grp])
                    S.dma("sp", ksTg[:, PAST:NK], ksT[g, :, SEQ:NTOK], reads=[r_ksT], writes=[r_grp])
                    S.dma("sp", vsg[:, 0:128, :], vs_s[g].rearrange("(c p) d -> p c d", p=128),
                          reads=[r_vs_s], writes=[r_grp])
                    S.dma("pool", vsg[:T_S, 128, :], o_kv_s[a][:, 1536 + g * 128:1536 + (g + 1) * 128],
                          reads=[r_okv], writes=[r_grp])
                    S.dma("sp", wst[:], win_state[a][:, g * 128:(g + 1) * 128].rearrange("(c p) d -> p c d", p=128),
                          writes=[r_ws])
                    S.op("dve", lambda v: v.tensor_copy(wsb[:], wst[:].rearrange("p c d -> p (c d)")),
                         reads=[r_ws], writes=[r_ws])
                    transpose_into(wsb, r_ws, 128, 4,
                                   lambda c0, n: kwTg[:, 0:512].rearrange("p (c r) -> p c r", r=128)[:, c0:c0 + n, :],
                                   r_grp)
                    S.dma("sp", kwTg[:, 512:520], kwT[g, :, SEQ:NTOK], reads=[r_kwT], writes=[r_grp])
                    S.dma("pool", vwg[:, 0:4, :],
                          win_state[a][:, 512 + g * 128:512 + (g + 1) * 128].rearrange("(c p) d -> p c d", p=128),
                          writes=[r_grp])
                    S.dma("pool", vwg[:T_S, 4, :], winr[SEQ:NTOK, 512 + g * 128:512 + (g + 1) * 128],
                          reads=[r_winr], writes=[r_grp])
                    pgrp, rpg = pgrp_t, r_pgrp

                    def gated_out(col, hh):
                        h = 3 * g + hh
                        gcol = gates_t[:, h * 3 + col:h * 3 + col + 1]

                        def f(ps, rp, st, rst):
                            if st is not None:
                                S.op("dve", lambda v: v.tensor_tensor(out=st[:T_S, 4:5], in0=st[:T_S, 3:4], in1=gcol,
                                                                      op=ALU.mult), reads=[rst, r_tab], writes=[rst])
                                sc_ap, rr = st[:T_S, 4:5], [rst]
                            else:
                                sc_ap, rr = gcol, [r_tab]
                            dst = oc[:, hh * 128:(hh + 1) * 128]
                            if col == 0:
                                S.op("dve", lambda v: v.tensor_scalar(out=dst, in0=ps[:T_S, :128], scalar1=sc_ap,
                                                                      scalar2=None, op0=ALU.mult),
                                     reads=[rp] + rr, writes=[roc])
                            else:
                                S.op("dve", lambda v: v.scalar_tensor_tensor(
                                    out=dst, in0=ps[:T_S, :128], scalar=sc_ap, in1=dst, op0=ALU.mult, op1=ALU.add),
                                    reads=[rp] + rr, writes=[roc])
                        return f
                    for hh in range(3):
                        ps, rp = psA.get()
                        S.mm_group([lambda pe, hh=hh: pe.matmul(ps[:T_S, :512], QTg[:, hh, :], kcT[:, g, :],
                                                                start=True, stop=True)],
                                   reads=[r_grp, r_kv], writes=[rp])
                        sc_t, rsc = small.get()
                        S.op("act", lambda a_: a_.activation(out=sc_t[:], in_=ps[:T_S, :512], func=AF.Identity,
                                                             scale=SCALE), reads=[rp], writes=[rsc])
                        st, rst = st_pool.get()
                        S.op("dve", lambda v: v.reduce_max(out=st[:T_S, 0:1], in_=sc_t[:], axis=AX.X),
                             reads=[rsc], writes=[rst])
                        S.op("dve", lambda v: v.tensor_scalar(out=st[:T_S, 1:2], in0=st[:T_S, 0:1], scalar1=-1.0,
                                                              scalar2=None, op0=ALU.mult), reads=[rst], writes=[rst])
                        S.op("act", lambda a_: a_.activation(out=sc_t[:], in_=sc_t[:], func=AF.Exp,
                                                             bias=st[:T_S, 1:2], scale=1.0, accum_out=st[:T_S, 2:3]),
                             reads=[rsc, rst], writes=[rsc, rst])
                        S.op("dve", lambda v: v.reciprocal(st[:T_S, 3:4], st[:T_S, 2:3]), reads=[rst], writes=[rst])
                        S.op("dve", lambda v: v.tensor_scalar(out=sc_t[:], in0=sc_t[:], scalar1=st[:T_S, 3:4],
                                                              scalar2=None, op0=ALU.mult),
                             reads=[rsc, rst], writes=[rsc])
                        if hh == 0:
                            S.op("pool", lambda g_: g_.tensor_copy(pgrp[:], sc_t[:]), reads=[rsc], writes=[rpg])
                        else:
                            S.op("pool", lambda g_: g_.tensor_tensor(out=pgrp[:], in0=pgrp[:], in1=sc_t[:],
                                                                     op=ALU.add), reads=[rsc], writes=[rpg])
                        pb, rpb = pbf.get()
                        S.op("act", lambda a_: a_.copy(pb[:], sc_t[:]), reads=[rsc], writes=[rpb])
                        pt, rpt = psT.get()
                        S.mm_group([lambda pe, c=c: pe.transpose(pt[:, c * T_S:(c + 1) * T_S],
                                                                 pb[:, c * 128:(c + 1) * 128], ident_b[:T_S, :T_S])
                                    for c in range(4)], reads=[rpb, r_ident], writes=[rpt])
                        pc, rpc = ptc.get()
                        copy_op("act", pc[:], pt[:, :4 * T_S], [rpt], [rpc])
                        ps2, rp2 = psB.get()
                        S.mm_group([lambda pe, c=c: pe.matmul(ps2[:T_S, :128], pc[:, c * T_S:(c + 1) * T_S],
                                                              vc[:, g, c, :], start=(c == 0), stop=(c == 3))
                                    for c in range(4)], reads=[rpc, r_kv], writes=[rp2])
                        gated_out(0, hh)(ps2, rp2, None, None)
                    score, rscore = sel_pool.get()
                    pg3 = pgrp[:].rearrange("p (b two) -> p b two", two=2)
                    S.op("dve", lambda v: v.tensor_copy(score[:], s_bonus[:]), reads=[r_tab], writes=[rscore])
                    S.op("dve", lambda v: v.tensor_tensor(out=score[:, 0:256], in0=score[:, 0:256], in1=pg3[:, :, 0],
                                                          op=ALU.add), reads=[rpg], writes=[rscore])
                    S.op("dve", lambda v: v.tensor_tensor(out=score[:, 0:256], in0=score[:, 0:256], in1=pg3[:, :, 1],
                                                          op=ALU.add), reads=[rpg], writes=[rscore])
                    selneg, rsel = sel_pool.get()
                    topk_selneg(m8_pool, score, rscore, T_S, 264, selneg, rsel)
                    for hh in range(3):
                        Ssb, rS = ab.S.get()
                        for kb in range(0, PAST, 512):
                            ps, rp = psA.get()
                            S.mm_group([lambda pe, hh=hh, kb=kb: pe.matmul(
                                ps[:T_S, :512], QTg[:, hh, :], ksTg[:, kb:kb + 512], start=True, stop=True)],
                                reads=[r_grp], writes=[rp])
                            S.op("dve", lambda v, kb=kb: v.scalar_tensor_tensor(
                                out=Ssb[:, kb:kb + 512].rearrange("p (b k) -> p b k", k=64),
                                in0=ps[:T_S, :512].rearrange("p (b k) -> p b k", k=64), scalar=SCALE,
                                in1=selneg[:, kb // 64:kb // 64 + 8].unsqueeze(2).to_broadcast([T_S, 8, 64]),
                                op0=ALU.mult, op1=ALU.add), reads=[rp, rsel], writes=[rS])
                        ps, rp = psA.get()
                        S.mm_group([lambda pe, hh=hh: pe.matmul(ps[:T_S, :T_S], QTg[:, hh, :], ksTg[:, PAST:NK],
                                                                start=True, stop=True)], reads=[r_grp], writes=[rp])
                        S.op("dve", lambda v: v.scalar_tensor_tensor(
                            out=Ssb[:, PAST:NK], in0=ps[:T_S, :T_S], scalar=SCALE, in1=s_tri8[:],
                            op0=ALU.mult, op1=ALU.add), reads=[rp, r_tab], writes=[rS])
                        S.op("dve", lambda v: v.tensor_scalar(out=Ssb[:, PAST:NK], in0=Ssb[:, PAST:NK],
                                                              scalar1=selneg[:, 256:257], scalar2=None, op0=ALU.add),
                             reads=[rsel], writes=[rS])
                        softmax_pv(ab, T_S, NK, Ssb, rS,
                                   lambda c: (vsg[:, c, :], r_grp, 128) if c < 128 else (vsg[:T_S, 128, :], r_grp, T_S),
                                   gated_out(1, hh))
                        Ssb, rS = ab.S.get()
                        ps, rp = psA.get()
                        S.mm_group([lambda pe, hh=hh: pe.matmul(ps[:T_S, :512], QTg[:, hh, :], kwTg[:, 0:512],
                                                                start=True, stop=True)], reads=[r_grp], writes=[rp])
                        S.op("dve", lambda v: v.scalar_tensor_tensor(
                            out=Ssb[:, 0:512], in0=ps[:T_S, :512], scalar=SCALE, in1=s_winm[:],
                            op0=ALU.mult, op1=ALU.add), reads=[rp, r_tab], writes=[rS])
                        ps, rp = psA.get()
                        S.mm_group([lambda pe, hh=hh: pe.matmul(ps[:T_S, :T_S], QTg[:, hh, :], kwTg[:, 512:520],
                                                                start=True, stop=True)], reads=[r_grp], writes=[rp])
                        S.op("dve", lambda v: v.scalar_tensor_tensor(
                            out=Ssb[:, 512:520], in0=ps[:T_S, :T_S], scalar=SCALE, in1=s_tri8[:],
                            op0=ALU.mult, op1=ALU.add), reads=[rp, r_tab], writes=[rS])
                        softmax_pv(ab, T_S, 520, Ssb, rS,
                                   lambda c: (vwg[:, c, :], r_grp, 128) if c < 4 else (vwg[:T_S, 4, :], r_grp, T_S),
                                   gated_out(2, hh))
                    S.op("act", lambda a_: a_.copy(ob[:], oc[:]), reads=[roc], writes=[roc])
                    S.dma("sp", tokb[SEQ:NTOK, g * 384:(g + 1) * 384], ob[:], reads=[roc], writes=[r_tokb])

    def ret_mix(layer):
        bl = layer // 2
        with Scope() as sc:
            S32 = sc.sb("S32", [128, 6, 2, 256], F32)
            Sb = sc.sb("Sb", [128, 6, 2, 256], BF16)
            r_S = [Res() for _ in range(6)]
            gn_bc = sc.sb("gn_bc", [128, 1536], F32)
            r_gn = Res()
            S.dma("sp", gn_bc[:], ret_gn[bl:bl + 1, :].to_broadcast([128, 1536]), writes=[r_gn])
            qc_pool = sc.pool("qc", 2, [128, 12, 128], BF16)
            kc_pool = sc.pool("kc", 2, [128, 12, 128], BF16)
            v_pool = sc.pool("vch", 2, [128, 1536], BF16)
            sg_pool = sc.pool("sgch", 2, [128, 1536], F32)
            qd_pool = sc.pool("qdT", 2, [128, 2, 128], BF16)
            in_pool = sc.pool("inT", 2, [128, 128], BF16)
            kd_pool = sc.pool("kd", 2, [128, 256], BF16)
            y_pool = sc.pool("yf", 2, [128, 256], F32)
            tok_pool = sc.pool("tokc", 2, [128, 1536], BF16)
            bn_pool = sc.pool("bn", 4, [128, 8], F32)
            for mode in ("prompt", "sample"):
                C = 128 if mode == "prompt" else T_S
                tag = "128" if mode == "prompt" else "8"
                dmT = sc.sb("dmT" + tag, [128, 6, C], F32)
                qdt = sc.sb("qd" + tag, [128, 6, C], F32)
                kdt = sc.sb("kd" + tag, [128, 6], F32)
                r_dt = Res()
                S.dma("sp", dmT[:], tabs["dmT" + tag].rearrange("h j i -> j h i"), writes=[r_dt])
                S.dma("sp", qdt[:], tabs["qd" + tag].rearrange("h j i -> j h i"), writes=[r_dt])
                S.dma("sp", kdt[:], tabs["kd" + tag], writes=[r_dt])
                if mode == "prompt":
                    for h in range(6):
                        S.op("pool", lambda g_, h=h: g_.memset(S32[:, h, :, :], 0.0), writes=[r_S[h]])
                        S.op("pool", lambda g_, h=h: g_.memset(Sb[:, h, :, :], 0.0), writes=[r_S[h]])
                    chunks = [(c * 128, 128) for c in range(32)]
                else:
                    for h in range(6):
                        S.dma("sp", S32[:, h, :, :],
                              ret_state[bl][h * 256:(h + 1) * 256, :].rearrange("(dc p) v -> p dc v", p=128),
                              writes=[r_S[h]])
                        S.op("act", lambda a_, h=h: a_.copy(Sb[:, h, :, :], S32[:, h, :, :]), reads=[r_S[h]],
                             writes=[r_S[h]])
                    chunks = [(SEQ, T_S)]
                for (r0, Cn) in chunks:
                    qc, rq = qc_pool.get()
                    kc, rk = kc_pool.get()
                    vch, rv = v_pool.get()
                    sg, rsg = sg_pool.get()
                    S.dma("sp", qc[:, :, :Cn], QT[:, :, r0:r0 + Cn].rearrange("j p n -> p j n"), reads=[r_QT],
                          writes=[rq])
                    S.dma("sp", kc[:, :, :Cn], KT12[:, :, r0:r0 + Cn].rearrange("j p n -> p j n"), reads=[r_KT12],
                          writes=[rk])
                    S.dma("sp", vch[:Cn, :], vtm[r0:r0 + Cn, :], reads=[r_vtm], writes=[rv])
                    S.dma("sp", sg[:Cn, :], sgate[r0:r0 + Cn, :], reads=[r_sgate], writes=[rsg])
                    tk, rtk = tok_pool.get()
                    for h in range(6):
                        ps, rp = psA.get()
                        S.mm_group([lambda pe, dc=dc, h=h: pe.matmul(ps[:Cn, :Cn], kc[:, 2 * h + dc, :Cn],
                                                                     qc[:, 2 * h + dc, :Cn], start=(dc == 0),
                                                                     stop=(dc == 1)) for dc in range(2)],
                                   reads=[rq, rk], writes=[rp])
                        inT, rin = in_pool.get()
                        S.op("dve", lambda v, h=h: v.tensor_tensor(out=inT[:Cn, :Cn], in0=ps[:Cn, :Cn],
                                                                   in1=dmT[:Cn, h, :Cn], op=ALU.mult),
                             reads=[rp, r_dt], writes=[rin])
                        qd, rqd = qd_pool.get()
                        S.op("pool", lambda g_, h=h: g_.tensor_tensor(
                            out=qd[:, :, :Cn], in0=qc[:, 2 * h:2 * h + 2, :Cn],
                            in1=qdt[:, h, :Cn].unsqueeze(1).to_broadcast([128, 2, Cn]), op=ALU.mult),
                            reads=[rq, r_dt], writes=[rqd])
                        po, rpo = psB.get()
                        fns = [lambda pe, h=h: pe.matmul(po[:Cn, :256], inT[:Cn, :Cn], vch[:Cn, h * 256:(h + 1) * 256],
                                                         start=True, stop=False)]
                        for dc in range(2):
                            fns.append(lambda pe, dc=dc, h=h: pe.matmul(po[:Cn, :256], qd[:, dc, :Cn], Sb[:, h, dc, :],
                                                                        start=False, stop=(dc == 1)))
                        S.mm_group(fns, reads=[rin, rv, rqd, r_S[h]], writes=[rpo])
                        pt, rpt = psT.get()
                        S.mm_group([lambda pe, dc=dc, h=h: pe.transpose(pt[:Cn, dc * 128:(dc + 1) * 128],
                                                                        kc[:, 2 * h + dc, :Cn], ident_b[:, :])
                                    for dc in range(2)], reads=[rk, r_ident], writes=[rpt])
                        kd, rkd = kd_pool.get()
                        S.op("dve", lambda v, h=h: v.tensor_scalar(out=kd[:Cn, :], in0=pt[:Cn, :256],
                                                                   scalar1=kdt[:Cn, h:h + 1], scalar2=None,
                                                                   op0=ALU.mult), reads=[rpt, r_dt], writes=[rkd])
                        for dc in range(2):
                            pss, rps = psA.get()
                            S.mm_group([lambda pe, dc=dc, h=h: pe.matmul(
                                pss[:, :256], kd[:Cn, dc * 128:(dc + 1) * 128], vch[:Cn, h * 256:(h + 1) * 256],
                                start=True, stop=True)], reads=[rkd, rv], writes=[rps])
                            S.op("dve", lambda v, dc=dc, h=h: v.scalar_tensor_tensor(
                                out=S32[:, h, dc, :], in0=S32[:, h, dc, :], scalar=cdec(h, C), in1=pss[:, :256],
                                op0=ALU.mult, op1=ALU.add), reads=[rps], writes=[r_S[h]])
                        S.op("act", lambda a_, h=h: a_.copy(Sb[:, h, :, :], S32[:, h, :, :]), reads=[],
                             writes=[r_S[h]])
                        bn, rbn = bn_pool.get()
                        S.op("dve", lambda v: v.bn_stats(out=bn[:Cn, 0:6], in_=po[:Cn, :256]), reads=[rpo],
                             writes=[rbn])
                        S.op("dve", lambda v: v.bn_aggr(out=bn[:Cn, 6:8], in_=bn[:Cn, 0:6]), reads=[rbn],
                             writes=[rbn])
                        S.op("act", lambda a_: a_.activation(out=bn[:Cn, 0:1], in_=bn[:Cn, 7:8], func=AF.Sqrt,
                                                             scale=1.0, bias=EPS), reads=[rbn], writes=[rbn])
                        S.op("dve", lambda v: v.reciprocal(bn[:Cn, 1:2], bn[:Cn, 0:1]), reads=[rbn], writes=[rbn])
                        yf, ry = y_pool.get()
                        S.op("dve", lambda v: v.tensor_scalar(out=yf[:Cn, :], in0=po[:Cn, :256], scalar1=bn[:Cn, 6:7],
                                                              scalar2=bn[:Cn, 1:2], op0=ALU.subtract, op1=ALU.mult),
                             reads=[rpo, rbn], writes=[ry])
                        S.op("pool", lambda g_, h=h: g_.tensor_tensor(out=yf[:Cn, :], in0=yf[:Cn, :],
                                                                      in1=gn_bc[:Cn, h * 256:(h + 1) * 256],
                                                                      op=ALU.mult), reads=[r_gn], writes=[ry])
                        S.op("pool", lambda g_, h=h: g_.tensor_tensor(out=tk[:Cn, h * 256:(h + 1) * 256],
                                                                      in0=yf[:Cn, :],
                                                                      in1=sg[:Cn, h * 256:(h + 1) * 256], op=ALU.mult),
                             reads=[ry, rsg], writes=[rtk])
                    S.dma("sp", tokb[r0:r0 + Cn, 0:1536], tk[:Cn, :], reads=[rtk], writes=[r_tokb])
                dst = o_ret_p[bl] if mode == "prompt" else o_ret_s[bl]
                for h in range(6):
                    S.dma("sp", dst[h * 256:(h + 1) * 256, :].rearrange("(dc p) v -> p dc v", p=128),
                          S32[:, h, :, :], reads=[r_S[h]])

    def mem_mix(layer, mk):
        with Scope() as sc:
            ab = AttnBufs(sc, 128, 256)
            qm_pool = sc.pool("qmt", 2, [128, 4, 128], BF16)
            om_pool = sc.pool("om", 2, [128, 512], BF16)
            tiles = [(t * 128, 128) for t in range(32)] + [(SEQ, T_S)]
            for (r0, nq) in tiles:
                samp = r0 >= SEQ
                KTt, rKT = (mk["sKT"], mk["r_sKT"]) if samp else (mk["pKT"], mk["r_pKT"])
                Vt, rV = (mk["sV"], mk["r_sV"]) if samp else (mk["pV"], mk["r_pV"])
                qm, rqm = qm_pool.get()
                S.dma("sp", qm[:, :, :nq], qmT[:, :, r0:r0 + nq].rearrange("h p n -> p h n"), reads=[r_qmT],
                      writes=[rqm])
                om, rom = om_pool.get()
                for h in range(4):
                    ps, rp = psA.get()
                    S.mm_group([lambda pe, h=h: pe.matmul(ps[:nq, :256], qm[:, h, :nq], KTt[:, h, :],
                                                          start=True, stop=True)], reads=[rqm, rKT], writes=[rp])
                    Ssb, rS = ab.S.get()
                    S.op("act", lambda a_: a_.activation(out=Ssb[:nq, :256], in_=ps[:nq, :256], func=AF.Identity,
                                                         scale=SCALE), reads=[rp], writes=[rS])

                    def out_fn(ps2, rp2, st, rst, h=h):
                        S.op("dve", lambda v: v.tensor_scalar(out=om[:nq, h * 128:(h + 1) * 128], in0=ps2[:nq, :128],
                                                              scalar1=st[:nq, 3:4], scalar2=None, op0=ALU.mult),
                             reads=[rp2, rst], writes=[rom])
                    softmax_pv(ab, nq, 256, Ssb, rS, lambda c, h=h: (Vt[:, c, h * 128:(h + 1) * 128], rV, 128), out_fn)
                S.dma("sp", tokb[r0:r0 + nq, 1536:2048], om[:nq, :], reads=[rom], writes=[r_tokb])

    def out_proj(layer):
        with Scope() as sc:
            Wo = sc.sb("Wo", [128, KC, D], BF16)
            r_Wo = Res()
            for cb in range(4):
                S.dma("pool", Wo[:, :, cb * 512:(cb + 1) * 512],
                      w_o[layer][:, cb * 512:(cb + 1) * 512].rearrange("(kc p) n -> p kc n", p=128), writes=[r_Wo])
            tk_pool = sc.pool("tkt", 2, [128, D], BF16)
            tT_pool = sc.pool("tokT", 2, [128, KC, 128], BF16)
            x_pool = sc.pool("xo", 2, [128, D], F32)
            tiles = [(t * 128, 128) for t in range(32)] + [(SEQ, T_S)]
            for (r0, P) in tiles:
                tk, rtk = tk_pool.get()
                S.dma("sp", tk[:P, :], tokb[r0:r0 + P, :], reads=[r_tokb], writes=[rtk])
                tT, rtT = tT_pool.get()
                transpose_into(tk, rtk, P, KC, lambda c0, n: tT[:, c0:c0 + n, :P], rtT)
                xt, rx = x_pool.get()
                S.dma("sp", xt[:P, :], x_rows(layer, r0, P), reads=[xres(r0)] if layer > 0 else [], writes=[rx])
                for cb in range(4):
                    ps, rp = psA.get()
                    S.mm_group([lambda pe, kc=kc, cb=cb: pe.matmul(ps[:P, :512], tT[:, kc, :P],
                                                                   Wo[:, kc, cb * 512:(cb + 1) * 512],
                                                                   start=(kc == 0), stop=(kc == KC - 1))
                                for kc in range(KC)], reads=[rtT, r_Wo], writes=[rp])
                    S.op("dve", lambda v, cb=cb: v.tensor_tensor(out=xt[:P, cb * 512:(cb + 1) * 512],
                                                                 in0=xt[:P, cb * 512:(cb + 1) * 512], in1=ps[:P, :512],
                                                                 op=ALU.add), reads=[rp], writes=[rx])
                S.dma("sp", xbuf[r0:r0 + P, :], xt[:P, :], reads=[rx], writes=[xres(r0)])
                if debug and layer == 0:
                    S.dma("sp", xmid_dbg[r0:r0 + P, :], xt[:P, :], reads=[rx])

    def ffn(layer, last):
        with Scope() as sc:
            nb = NormBufs(sc)
            hT = sc.sb("hT", [128, KC, 512], BF16)
            r_hT = Res()
            uT = sc.sb("uT", [128, FC, 512], BF16)
            r_uT = Res()
            w_pool = sc.pool("wbuf", 2, [128, KC, 512], BF16)
            wo_pool = sc.pool("wobuf", 3, [128, 11, 512], BF16)
            cp = sc.sb("convp", [128, 4, FC], F32)
            carry = sc.sb("carry", [128, FC, 2], F32)
            r_cp = Res()
            r_carry = Res()
            S.dma("sp", cp[:], convp[layer], writes=[r_cp])
            a_pool = sc.pool("abuf", 2, [128, 514], F32)
            acc_pool = sc.pool("acc", 2, [128, 512], F32)
            gt, gr = load_gain(nb, norm2_g[layer:layer + 1, :])
            if last:
                gtf, grf = load_gain(nb, final_g[0:1, :])
            Win = ffn_w_in[layer]
            Wout = ffn_w_out[layer]
            for (r0, n) in BLOCKS:
                samp = r0 >= SEQ
                if r0 == 0:
                    S.op("pool", lambda g_: g_.memset(carry[:], 0.0), writes=[r_carry])
                if samp:
                    S.dma("sp", o_conv_p[layer], carry[:], reads=[r_carry])
                    S.dma("sp", carry[:], conv_state[layer], writes=[r_carry])
                for t0 in range(0, n, 128):
                    P = min(128, n - t0)
                    norm_tile(nb, xbuf[r0 + t0:r0 + t0 + P, :], [xres(r0)], P, gt, gr, t0, hT, r_hT)
                for j0 in range(0, FC, 2):
                    wt, rw = w_pool.get()
                    load_w(None, Win, j0 * 128, 256, dst_col=0, tile=(wt, rw))
                    load_w(None, Win, FFN + j0 * 128, 256, dst_col=256, tile=(wt, rw))
                    for jj in range(2):
                        j = j0 + jj
                        pa, rpa = psA.get()
                        S.mm_group([lambda pe, kc=kc, jj=jj: pe.matmul(pa[:, :n], wt[:, kc, jj * 128:(jj + 1) * 128],
                                                                       hT[:, kc, :n], start=(kc == 0),
                                                                       stop=(kc == KC - 1)) for kc in range(KC)],
                                   reads=[r_hT, rw], writes=[rpa])
                        pg, rpg = psA.get()
                        S.mm_group([lambda pe, kc=kc, jj=jj: pe.matmul(pg[:, :n],
                                                                       wt[:, kc, 256 + jj * 128:256 + (jj + 1) * 128],
                                                                       hT[:, kc, :n], start=(kc == 0),
                                                                       stop=(kc == KC - 1)) for kc in range(KC)],
                                   reads=[r_hT, rw], writes=[rpg])
                        ab_, rab = a_pool.get()
                        S.op("act", lambda a_, j=j: a_.copy(ab_[:, 2:2 + n], pa[:, :n]), reads=[rpa], writes=[rab])
                        S.op("pool", lambda g_, j=j: g_.tensor_copy(ab_[:, 0:2], carry[:, j, :]), reads=[r_carry],
                             writes=[rab])
                        acc, racc = acc_pool.get()
                        S.op("dve", lambda v, j=j: v.tensor_scalar(out=acc[:, :n], in0=ab_[:, 2:2 + n],
                                                                   scalar1=cp[:, 2, j:j + 1], scalar2=cp[:, 3, j:j + 1],
                                                                   op0=ALU.mult, op1=ALU.add),
                             reads=[rab, r_cp], writes=[racc])
                        S.op("dve", lambda v, j=j: v.scalar_tensor_tensor(out=acc[:, :n], in0=ab_[:, 1:1 + n],
                                                                          scalar=cp[:, 1, j:j + 1], in1=acc[:, :n],
                                                                          op0=ALU.mult, op1=ALU.add),
                             reads=[rab, r_cp], writes=[racc])
                        S.op("dve", lambda v, j=j: v.scalar_tensor_tensor(out=acc[:, :n], in0=ab_[:, 0:n],
                                                                          scalar=cp[:, 0, j:j + 1], in1=acc[:, :n],
                                                                          op0=ALU.mult, op1=ALU.add),
                             reads=[rab, r_cp], writes=[racc])
                        S.op("pool", lambda g_, j=j: g_.tensor_copy(carry[:, j, :], ab_[:, n:n + 2]), reads=[rab],
                             writes=[r_carry])
                        S.op("act", lambda a_: a_.activation(out=acc[:, :n], in_=acc[:, :n], func=AF.Silu),
                             reads=[racc], writes=[racc])
                        S.op("dve", lambda v, j=j: v.tensor_tensor(out=uT[:, j, :n], in0=acc[:, :n], in1=pg[:, :n],
                                                                   op=ALU.mult), reads=[racc, rpg], writes=[r_uT])
                if samp:
                    S.dma("sp", o_conv_s[layer], carry[:], reads=[r_carry])
                xtiles = [(t0, min(128, n - t0)) for t0 in range(0, n, 128)]
                for cb in range(4):
                    pss = [psA.get() for _ in xtiles]
                    for q4 in range(4):
                        wt, rw = wo_pool.get()
                        src = Wout[q4 * 11 * 128:(q4 + 1) * 11 * 128, cb * 512:(cb + 1) * 512].rearrange(
                            "(kc p) n -> p kc n", p=128)
                        S.dma("pool", wt[:], src, writes=[rw])
                        for ti, (t0, P) in enumerate(xtiles):
                            ps, rp = pss[ti]
                            S.mm_group([lambda pe, kc=kc, q4=q4, t0=t0, P=P, ps=ps, wt=wt: pe.matmul(
                                ps[:P, :512], uT[:, q4 * 11 + kc, t0:t0 + P], wt[:, kc, :],
                                start=(q4 == 0 and kc == 0), stop=(q4 == 3 and kc == 10)) for kc in range(11)],
                                reads=[r_uT, rw], writes=[rp])
                    for ti, (t0, P) in enumerate(xtiles):
                        ps, rp = pss[ti]
                        stg, rs = acc_pool.get()
                        S.dma("sp", stg[:P, :], xbuf[r0 + t0:r0 + t0 + P, cb * 512:(cb + 1) * 512],
                              reads=[xres(r0)], writes=[rs])
                        S.op("dve", lambda v, ps=ps, P=P, stg=stg: v.tensor_tensor(out=stg[:P, :], in0=stg[:P, :],
                                                                                 in1=ps[:P, :512], op=ALU.add),
                             reads=[rp], writes=[rs])
                        S.dma("sp", xbuf[r0 + t0:r0 + t0 + P, cb * 512:(cb + 1) * 512], stg[:P, :], reads=[rs],
                              writes=[xres(r0)])
                if last:
                    for t0 in range(0, n, 128):
                        P = min(128, n - t0)
                        xt, rx = nb.xt.get()
                        S.dma("sp", xt[:P, :], xbuf[r0 + t0:r0 + t0 + P, :], reads=[xres(r0)], writes=[rx])
                        st, rs = rstd_of(xt, rx, P, nb)
                        S.op("dve", lambda v: v.scalar_tensor_tensor(out=xt[:P, :], in0=xt[:P, :], scalar=st[:P, 2:3],
                                                                     in1=gtf[:P, :], op0=ALU.mult, op1=ALU.mult),
                             reads=[rs, grf], writes=[rx])
                        dst = o_y_s[:, :] if samp else o_y_p[r0 + t0:r0 + t0 + P, :]
                        S.dma("sp", dst, xt[:P, :], reads=[rx])

    mkp = {}
    mkp["pKT"] = gsb("m_pKT", [128, 4, 256], BF16)
    mkp["pV"] = gsb("m_pV", [128, 2, 512], BF16)
    mkp["sKT"] = gsb("m_sKT", [128, 4, 256], BF16)
    mkp["sV"] = gsb("m_sV", [128, 2, 512], BF16)
    for k_ in ("pKT", "pV", "sKT", "sV"):
        mkp["r_" + k_] = Res(multi=True)

    for layer in range(nlayers):
        if layer % 2 == 0:
            phaseA_nsa(layer, mkp)
            nsa_mix_prompt(layer)
            nsa_mix_sample(layer)
        else:
            phaseA_ret(layer, mkp)
            ret_mix(layer)
        mem_mix(layer, mkp)
        out_proj(layer)
        ffn(layer, layer == nlayers - 1)

    S.finish()
    return nc, S.n_inst


_CACHE = {}


def kernel(**inp):
    if "nc" not in _CACHE:
        _CACHE["nc"] = build_program()
    nc, _ = _CACHE["nc"]
    f = lambda a: np.ascontiguousarray(np.asarray(a, dtype=np.float32))
    x_prompt = f(inp["x_prompt"])
    x_sample = f(inp["x_sample"])
    mem_prompt = f(inp["mem_prompt"])
    state_nsa_win = f(inp["state_nsa_win"])
    state_ret = f(inp["state_ret"])
    state_ffn_conv = f(inp["state_ffn_conv"])
    cache_mem_kv = f(inp["cache_mem_kv"])
    page_table = np.ascontiguousarray(np.asarray(inp["page_table"], dtype=np.int32))
    conv_w = f(inp["ffn_conv_w"])
    conv_b = f(inp["ffn_conv_b"])
    cpar = np.concatenate([conv_w, conv_b[:, None, :]], axis=1)
    cpar = np.ascontiguousarray(cpar.reshape(DEPTH, 4, FC, 128).transpose(0, 3, 1, 2))
    peT = np.ascontiguousarray(f(inp["nsa_cmp_pe"]).transpose(0, 1, 3, 2))
    shared = {
        "cache": f(inp["cache_nsa_kv"]).reshape(2, 1280 * 128, 2048),
        "norm1_g": f(inp["norm1_g"]), "norm2_g": f(inp["norm2_g"]), "mem_norm_g": f(inp["mem_norm_g"]),
        "final_g": f(inp["final_norm_g"]).reshape(1, D),
        "nsa_w_in": f(inp["nsa_w_in"]), "ret_w_in": f(inp["ret_w_in"]), "ret_gn": f(inp["ret_gn_g"]),
        "w_mem_kv": f(inp["w_mem_kv"]), "w_o": f(inp["w_o"]),
        "ffn_w_in": f(inp["ffn_w_in"]), "ffn_w_out": f(inp["ffn_w_out"]),
        "convp": cpar, "cmp_peT": peT, "cmp_w1": f(inp["nsa_cmp_w1"]), "cmp_w2": f(inp["nsa_cmp_w2"]),
    }
    for k_, v_ in make_tables().items():
        shared["t_" + k_] = v_
    in_maps = []
    for c in range(8):
        b = c // 4
        m = dict(shared)
        m["xp"] = x_prompt[b]
        m["xs"] = x_sample[c]
        m["memp"] = mem_prompt[b]
        m["win_state"] = np.ascontiguousarray(state_nsa_win[:, c].reshape(2, 512, 1024))
        m["ret_state"] = np.ascontiguousarray(state_ret[:, c].reshape(2, 1536, 256))
        m["conv_state"] = np.ascontiguousarray(
            state_ffn_conv[:, c].reshape(DEPTH, 2, FC, 128).transpose(0, 3, 2, 1))
        m["mem_cache"] = np.ascontiguousarray(cache_mem_kv[:, c].reshape(DEPTH, 256, 1024))
        m["page_tab"] = page_table[c:c + 1]
        in_maps.append(m)
    res = run_bass_kernel_spmd(nc, in_maps, core_ids=list(range(8)))
    R = res.results
    _CACHE["raw"] = R
    pc = [0, 4]
    y_prompt = np.stack([R[c]["o_y_p"] for c in pc])
    y_sample = np.stack([R[c]["o_y_s"] for c in range(8)])
    kv_p = np.stack([R[c]["o_kv_p"] for c in pc], axis=1).reshape(2, 2, SEQ, 4, 4, 128)
    kv_s = np.stack([R[c]["o_kv_s"] for c in range(8)], axis=1).reshape(2, 8, T_S, 4, 4, 128)
    win_p = np.stack([R[c]["o_win_p"] for c in pc], axis=1).reshape(2, 2, 512, 2, 4, 128)
    win_s = np.stack([R[c]["o_win_s"] for c in range(8)], axis=1).reshape(2, 8, 512, 2, 4, 128)
    ret_p = np.stack([R[c]["o_ret_p"] for c in pc], axis=1).reshape(2, 2, 6, 256, 256)
    ret_s = np.stack([R[c]["o_ret_s"] for c in range(8)], axis=1).reshape(2, 8, 6, 256, 256)

    def conv_out(a):
        return np.ascontiguousarray(a.transpose(0, 3, 2, 1).reshape(DEPTH, 2, FFN))
    conv_p = np.stack([conv_out(R[c]["o_conv_p"]) for c in pc], axis=1)
    conv_s = np.stack([conv_out(R[c]["o_conv_s"]) for c in range(8)], axis=1)
    mem_p = np.stack([R[c]["o_mem_p"] for c in pc], axis=1).reshape(DEPTH, 2, 256, 2, 4, 128)
    return (y_prompt, y_sample, kv_p, kv_s, win_p, win_s, ret_p, ret_s, conv_p, conv_s, mem_p)
```

```python
from contextlib import ExitStack
import numpy as np
import concourse.bass as bass
import concourse.mybir as mybir
from concourse.bass_utils import run_bass_kernel_spmd

F32 = mybir.dt.float32
BF16 = mybir.dt.bfloat16
I32 = mybir.dt.int32
AF = mybir.ActivationFunctionType
ALU = mybir.AluOpType
AX = mybir.AxisListType

D = 2048
SEQ = 4096
DEPTH = 4
T_S = 8
NTOK = SEQ + T_S
KC = 16
NSA_IN = 5156
RET_IN = 6656
FFN = 5632
FC = FFN // 128
EPS = 1e-6
NDMA = 24
NEG = -30000.0
PAST = 16384
NPAGE = 128
SCALE = 128 ** -0.5
LAYERS = list(range(DEPTH))


class Res:
    __slots__ = ("w", "r", "multi")

    def __init__(self, multi=False):
        self.w = {}
        self.r = {}
        self.multi = multi


class Sched:
    def __init__(self, nc):
        self.nc = nc
        self.eng = {"pe": nc.tensor, "act": nc.scalar, "dve": nc.vector,
                    "pool": nc.gpsimd, "sp": nc.sync}
        self.sem, self.cnt, self.known = {}, {}, {}
        for k in self.eng:
            self.sem[k] = nc.alloc_semaphore(name="sem_" + k)
            self.cnt[k] = 0
            self.known[k] = {}
        for i in range(NDMA):
            k = "d%d" % i
            self.sem[k] = nc.alloc_semaphore(name="sem_" + k)
            self.cnt[k] = 0
        self.rr = 0
        self.n_inst = 0

    def _wait(self, e, s, v):
        if s == "pe" and e == "pe":
            return
        if self.known[e].get(s, 0) >= v:
            return
        self.known[e][s] = v
        self.eng[e].wait_ge(self.sem[s], v)
        self.n_inst += 1

    def _deps(self, e, reads, writes):
        for r in reads:
            for s, v in r.w.items():
                self._wait(e, s, v)
        for w in writes:
            for s, v in w.r.items():
                self._wait(e, s, v)
            if not w.multi:
                for s, v in w.w.items():
                    self._wait(e, s, v)

    def _record(self, s, v, reads, writes):
        for r in reads:
            if r.r.get(s, 0) < v:
                r.r[s] = v
        for w in writes:
            if w.multi:
                if w.w.get(s, 0) < v:
                    w.w[s] = v
            else:
                w.w = {s: v}
            w.r = {}

    def op(self, e, fn, reads=(), writes=()):
        self._deps(e, reads, writes)
        inst = fn(self.eng[e])
        self.cnt[e] += 1
        inst.then_inc(self.sem[e], 1)
        self.n_inst += 1
        self._record(e, self.cnt[e], reads, writes)

    def mm_group(self, fns, reads=(), writes=()):
        self._deps("pe", reads, writes)
        inst = None
        for fn in fns:
            inst = fn(self.eng["pe"])
            self.n_inst += 1
        self.cnt["pe"] += 1
        inst.then_inc(self.sem["pe"], 1)
        self._record("pe", self.cnt["pe"], reads, writes)

    def dma(self, q, out, in_, reads=(), writes=(), fn=None):
        self._deps(q, reads, writes)
        i = self.rr
        self.rr = (i + 1) % NDMA
        k = "d%d" % i
        if self.cnt[k] > 0:
            self._wait(q, k, 16 * self.cnt[k])
        self.cnt[k] += 1
        if fn is None:
            inst = self.eng[q].dma_start(out=out, in_=in_)
        else:
            inst = fn(self.eng[q])
        inst.then_inc(self.sem[k], 16)
        self.n_inst += 1
        self._record(k, 16 * self.cnt[k], reads, writes)

    def barrier(self):
        for e in ("pe", "act", "dve", "pool", "sp"):
            for k, c in self.cnt.items():
                if c > 0 and k != e:
                    self._wait(e, k, 16 * c if k[1:].isdigit() else c)

    def finish(self):
        for i in range(NDMA):
            k = "d%d" % i
            if self.cnt[k] > 0:
                self._wait("sp", k, 16 * self.cnt[k])
        for e in ("pe", "act", "dve", "pool"):
            if self.cnt[e] > 0:
                self._wait("sp", e, self.cnt[e])


class Pool:
    def __init__(self, tiles):
        self.tiles = [(t, Res()) for t in tiles]
        self.i = 0

    def get(self):
        t = self.tiles[self.i]
        self.i = (self.i + 1) % len(self.tiles)
        return t


def gamma(h):
    return 1.0 - 2.0 ** (-5.0 - h)


def make_tables():
    T = {}
    T["ident"] = np.eye(128, dtype=np.float32)
    q = np.arange(SEQ)[:, None]
    n = np.arange(128)[None, :]
    T["cmpmask"] = np.where(n * 32 + 31 <= q, 0.0, NEG).astype(np.float32)
    blk = np.arange(64)[None, :]
    cur = q // 64
    valid = blk * 64 <= q
    forced = (blk == 0) | (blk == cur) | (blk == cur - 1)
    T["bonus"] = np.where(valid, np.where(forced, 1.0e4, 0.0), -1.0e30).astype(np.float32)
    p = np.arange(128)[:, None]
    k = np.arange(128)[None, :]
    T["tri_le"] = np.where(k <= p, 0.0, NEG).astype(np.float32)
    T["tri_gt"] = np.where(k > p, 0.0, NEG).astype(np.float32)
    sb = np.zeros((128, 264), np.float32)
    sb[:, [0, 255, 256]] = 1.0e4
    sb[:, 257:] = -1.0e30
    T["s_bonus"] = sb
    t = np.arange(128)[:, None]
    j = np.arange(8)[None, :]
    T["s_tri8"] = np.where(j <= t, 0.0, NEG).astype(np.float32)
    r = np.arange(512)[None, :]
    T["s_winmask"] = np.where(r > t, 0.0, NEG).astype(np.float32)
    pos = np.concatenate([np.arange(SEQ), PAST + np.arange(T_S)]).astype(np.float32)
    inv = (10000.0 ** (-np.arange(128, dtype=np.float32) / 128.0)).astype(np.float32)
    ang = (pos[None, :] * inv[:, None]).astype(np.float32)
    T["cosT"] = np.cos(ang).astype(np.float32)
    T["sinT"] = np.sin(ang).astype(np.float32)
    for C, tag in ((128, "128"), (8, "8")):
        i = np.arange(C, dtype=np.float64)
        dm = np.zeros((6, 128, C), np.float32)
        qd = np.zeros((6, 128, C), np.float32)
        kd = np.zeros((128, 6), np.float32)
        for h in range(6):
            lg = np.log1p(-2.0 ** (-5.0 - h))
            diff = i[None, :] - i[:, None]
            dm[h, :C, :] = np.where(diff >= 0, np.exp(np.maximum(diff, 0.0) * lg), 0.0)
            qd[h, :, :] = np.exp((i + 1.0) * lg)[None, :]
            kd[:C, h] = np.exp((C - 1.0 - i) * lg)
        T["dmT" + tag] = dm
        T["qd" + tag] = qd
        T["kd" + tag] = kd
    return T


def cdec(h, C):
    return float(np.exp(C * np.log1p(-2.0 ** (-5.0 - h))))


def build_program(nlayers=DEPTH, debug=False):
    nc = bass.Bass("TRN2", target_bir_lowering=False)
    S = Sched(nc)
    uid = [0]

    def nm(p):
        uid[0] += 1
        return "%s_%d" % (p, uid[0])

    def din(name, shape, dt=F32):
        return nc.dram_tensor(name, list(shape), dt, kind="ExternalInput").ap()

    def dout(name, shape, dt=F32):
        return nc.dram_tensor(name, list(shape), dt, kind="ExternalOutput").ap()

    def dscr(name, shape, dt):
        kind = "ExternalOutput" if (debug and name in ("xbuf", "tokb", "xmid_dbg")) else "Internal"
        return nc.dram_tensor(name, list(shape), dt, kind=kind).ap()

    xp = din("xp", [SEQ, D])
    xs = din("xs", [T_S, D])
    memp = din("memp", [256, D])
    win_state = din("win_state", [2, 512, 1024])
    ret_state = din("ret_state", [2, 1536, 256])
    conv_state = din("conv_state", [DEPTH, 128, FC, 2])
    mem_cache = din("mem_cache", [DEPTH, 256, 1024])
    cache = din("cache", [2, 1280 * 128, 2048])
    page_tab = din("page_tab", [1, NPAGE], I32)
    norm1_g = din("norm1_g", [DEPTH, D])
    norm2_g = din("norm2_g", [DEPTH, D])
    mem_norm_g = din("mem_norm_g", [DEPTH, D])
    final_g = din("final_g", [1, D])
    nsa_w_in = din("nsa_w_in", [2, D, NSA_IN])
    ret_w_in = din("ret_w_in", [2, D, RET_IN])
    ret_gn = din("ret_gn", [2, 1536])
    w_mem_kv = din("w_mem_kv", [DEPTH, D, 1024])
    w_o = din("w_o", [DEPTH, D, D])
    ffn_w_in = din("ffn_w_in", [DEPTH, D, 2 * FFN])
    ffn_w_out = din("ffn_w_out", [DEPTH, FFN, D])
    convp = din("convp", [DEPTH, 128, 4, FC])
    cmp_peT = din("cmp_peT", [2, 2, 128, 32])
    cmp_w1 = din("cmp_w1", [2, 2, 4096, 128])
    cmp_w2 = din("cmp_w2", [2, 2, 128, 128])
    tabs = {}
    TS = make_tables()
    for k_, v_ in TS.items():
        tabs[k_] = din("t_" + k_, list(v_.shape))

    o_y_p = dout("o_y_p", [SEQ, D])
    o_y_s = dout("o_y_s", [T_S, D])
    o_kv_p = dout("o_kv_p", [2, SEQ, 2048])
    o_kv_s = dout("o_kv_s", [2, T_S, 2048])
    o_win_p = dout("o_win_p", [2, 512, 1024])
    o_win_s = dout("o_win_s", [2, 512, 1024])
    o_ret_p = dout("o_ret_p", [2, 1536, 256])
    o_ret_s = dout("o_ret_s", [2, 1536, 256])
    o_conv_p = dout("o_conv_p", [DEPTH, 128, FC, 2])
    o_conv_s = dout("o_conv_s", [DEPTH, 128, FC, 2])
    o_mem_p = dout("o_mem_p", [DEPTH, 256, 1024])

    xbuf = dscr("xbuf", [NTOK, D], F32)
    r_xb = [Res(multi=True) for _ in range(9)]

    def xres(r0):
        return r_xb[min(r0 // 512, 8)]
    xmid_dbg = dscr("xmid_dbg", [NTOK, D], F32) if debug else None
    tokb = dscr("tokb", [NTOK, D], BF16)
    r_tokb = Res(multi=True)
    QT = dscr("QT", [12, 128, NTOK], BF16)
    r_QT = Res(multi=True)
    KT12 = dscr("KT12", [12, 128, NTOK], BF16)
    r_KT12 = Res(multi=True)
    rcT = dscr("rcT", [8, 128, SEQ], BF16)
    r_rcT = Res(multi=True)
    ksT = dscr("ksT", [4, 128, NTOK], BF16)
    r_ksT = Res(multi=True)
    kwT = dscr("kwT", [4, 128, NTOK], BF16)
    r_kwT = Res(multi=True)
    qmT = dscr("qmT", [4, 128, NTOK], BF16)
    r_qmT = Res(multi=True)
    gates = dscr("gates", [NTOK, 36], F32)
    r_gates = Res(multi=True)
    winr = dscr("winr", [NTOK, 1024], F32)
    r_winr = Res(multi=True)
    vtm = dscr("vtm", [NTOK, 1536], BF16)
    r_vtm = Res(multi=True)
    sgate = dscr("sgate", [NTOK, 1536], F32)
    r_sgate = Res(multi=True)
    r_okv = Res(multi=True)
    ksT_s = dscr("ksT_s", [4, 128, PAST], BF16)
    r_ksT_s = Res(multi=True)
    vs_s = dscr("vs_s", [4, PAST, 128], BF16)
    r_vs_s = Res(multi=True)

    nsa_w_b = dscr("nsa_w_b", [2, D, NSA_IN], BF16)
    ret_w_b = dscr("ret_w_b", [2, D, RET_IN], BF16)
    mem_w_b = dscr("mem_w_b", [DEPTH, D, 1024], BF16)
    wo_b = dscr("wo_b", [DEPTH, D, D], BF16)
    fin_b = dscr("fin_b", [DEPTH, D, 2 * FFN], BF16)
    fout_b = dscr("fout_b", [DEPTH, FFN, D], BF16)

    def gsb(name, shape, dt):
        return nc.alloc_sbuf_tensor(name, list(shape), dt)

    ident_f = gsb("ident_f", [128, 128], F32)
    ident_b = gsb("ident_b", [128, 128], BF16)
    r_ident = Res()
    psA = Pool([nc.alloc_psum_tensor("psA%d" % i, [128, 512], F32) for i in range(4)])
    psB = Pool([nc.alloc_psum_tensor("psB%d" % i, [128, 512], F32) for i in range(2)])
    psT = Pool([nc.alloc_psum_tensor("psT%d" % i, [128, 1024], BF16) for i in range(2)])
    st_pool = Pool([gsb("stat%d" % i, [128, 8], F32) for i in range(6)])

    evac_rr = [0]

    def evac_engine():
        evac_rr[0] ^= 1
        return "act" if evac_rr[0] else "dve"

    def copy_op(e, out, in_, reads, writes):
        if e == "act":
            S.op("act", lambda a: a.copy(out, in_), reads, writes)
        else:
            S.op(e, lambda v: v.tensor_copy(out, in_), reads, writes)

    S.dma("sp", ident_f[:], tabs["ident"], writes=[r_ident])
    S.op("dve", lambda v: v.tensor_copy(ident_b[:], ident_f[:]), reads=[r_ident], writes=[r_ident])

    class Scope:
        def __init__(self):
            self.es = ExitStack()

        def __enter__(self):
            self.es.__enter__()
            return self

        def __exit__(self, *a):
            if a[0] is None:
                S.barrier()
            return self.es.__exit__(*a)

        def sb(self, name, shape, dt):
            return self.es.enter_context(nc.sbuf_tensor(nm(name), list(shape), dt))

        def pool(self, name, n, shape, dt):
            return Pool([self.sb(name, shape, dt) for _ in range(n)])

    BLOCKS = [(b * 512, 512) for b in range(SEQ // 512)] + [(SEQ, T_S)]

    def x_rows(layer, r0, n):
        if layer == 0:
            return xp[r0:r0 + n, :] if r0 < SEQ else xs[r0 - SEQ:r0 - SEQ + n, :]
        return xbuf[r0:r0 + n, :]

    class NormBufs:
        def __init__(self, sc):
            self.g_bc = sc.pool("g_bc", 2, [128, D], F32)
            self.xt = sc.pool("xt", 2, [128, D], F32)
            self.hb = sc.pool("hb", 2, [128, D], BF16)
            self.junk = sc.sb("junk", [128, D], BF16)
            self.r_junk = Res()

    def load_gain(nb, g_row_ap):
        t, r = nb.g_bc.get()
        S.dma("sp", t[:], g_row_ap.to_broadcast([128, D]), writes=[r])
        return t, r

    def rstd_of(xt, rx, P, nb):
        st, rs = st_pool.get()
        S.op("act", lambda a: a.activation(out=nb.junk[:P, :], in_=xt[:P, :], func=AF.Square,
                                           accum_out=st[:P, 0:1]),
             reads=[rx], writes=[nb.r_junk, rs])
        S.op("act", lambda a: a.activation(out=st[:P, 1:2], in_=st[:P, 0:1], func=AF.Sqrt,
                                           scale=1.0 / D, bias=EPS),
             reads=[rs], writes=[rs])
        S.op("dve", lambda v: v.reciprocal(st[:P, 2:3], st[:P, 1:2]), reads=[rs], writes=[rs])
        return st, rs

    def transpose_into(src_bf, rsrc, P, nchunks, dst_fn, rdst):
        for c0 in range(0, nchunks, 8):
            n = min(8, nchunks - c0)
            pt, rp = psT.get()
            fns = []
            for j in range(n):
                fns.append(lambda pe, j=j: pe.transpose(
                    pt[:, j * 128:j * 128 + P], src_bf[:P, (c0 + j) * 128:(c0 + j + 1) * 128],
                    ident_b[:P, :P]))
            S.mm_group(fns, reads=[rsrc, r_ident], writes=[rp])
            src = pt[:].rearrange("p (j q) -> p j q", q=128)[:, :n, :P]
            copy_op(evac_engine(), dst_fn(c0, n), src, [rp], [rdst])

    def norm_tile(nb, x_src_ap, xsrc_res, P, gt, gr, col0, hT_t, r_hT_t):
        xt, rx = nb.xt.get()
        S.dma("sp", xt[:P, :], x_src_ap, reads=xsrc_res, writes=[rx])
        st, rs = rstd_of(xt, rx, P, nb)
        hb, rh = nb.hb.get()
        S.op("dve", lambda v: v.scalar_tensor_tensor(out=hb[:P, :], in0=xt[:P, :], scalar=st[:P, 2:3],
                                                     in1=gt[:P, :], op0=ALU.mult, op1=ALU.mult),
             reads=[rx, rs, gr], writes=[rh])
        transpose_into(hb, rh, P, KC, lambda c0, n: hT_t[:, c0:c0 + n, col0:col0 + P], r_hT_t)

    def load_w(w_pool, W2d, col0, ncols, dst_col=0, tile=None):
        if tile is None:
            wt, rw = w_pool.get()
        else:
            wt, rw = tile
        src = W2d[:, col0:col0 + ncols].rearrange("(kc p) n -> p kc n", p=128)
        S.dma("act", wt[:, :, dst_col:dst_col + ncols], src, writes=[rw])
        return wt, rw

    def linear_tm(w_pool, hT_t, r_hT_t, n, W2d, col0, ncols, sink):
        for cb0 in range(0, ncols, 512):
            cw = min(512, ncols - cb0)
            wt, rw = load_w(w_pool, W2d, col0 + cb0, cw)
            for t0 in range(0, n, 128):
                P = min(128, n - t0)
                ps, rp = psA.get()
                fns = []
                for kc in range(KC):
                    fns.append(lambda pe, kc=kc: pe.matmul(
                        ps[:P, :cw], hT_t[:, kc, t0:t0 + P], wt[:, kc, :cw],
                        start=(kc == 0), stop=(kc == KC - 1)))
                S.mm_group(fns, reads=[r_hT_t, rw], writes=[rp])
                sink(t0, P, cb0, cw, ps, rp)

    def linear_fm(w_pool, hT_t, r_hT_t, n, W2d, col0, nchunks, sink):
        for j0 in range(0, nchunks, 4):
            nj = min(4, nchunks - j0)
            wt, rw = load_w(w_pool, W2d, col0 + j0 * 128, nj * 128)
            for jj in range(nj):
                ps, rp = psA.get()
                fns = []
                for kc in range(KC):
                    fns.append(lambda pe, kc=kc: pe.matmul(
                        ps[:, :n], wt[:, kc, jj * 128:(jj + 1) * 128], hT_t[:, kc, :n],
                        start=(kc == 0), stop=(kc == KC - 1)))
                S.mm_group(fns, reads=[r_hT_t, rw], writes=[rp])
                sink(j0 + jj, ps, rp)

    def tm_sink_dram(stage_pool, dst2d, rdst, func=None, dt_stage=F32):
        def f(t0, P, cb0, cw, ps, rp):
            stg, rs = stage_pool.get()
            if func is None:
                copy_op(evac_engine(), stg[:P, :cw], ps[:P, :cw], [rp], [rs])
            else:
                S.op("act", lambda a: a.activation(out=stg[:P, :cw], in_=ps[:P, :cw], func=func),
                     reads=[rp], writes=[rs])
            S.dma("sp", dst2d[t0:t0 + P, cb0:cb0 + cw], stg[:P, :cw], reads=[rs], writes=rdst)
        return f

    def fm_sink_dram(stage_pool, dst3d, rdst, r0, n, add_tab=None):
        def f(j, ps, rp):
            stg, rs = stage_pool.get()
            if add_tab is None:
                copy_op(evac_engine(), stg[:, :n], ps[:, :n], [rp], [rs])
            else:
                tab, rt = add_tab(j)
                S.op("dve", lambda v: v.tensor_tensor(
                    out=stg[:, :n].rearrange("p (b j) -> p b j", j=32),
                    in0=ps[:, :n].rearrange("p (b j) -> p b j", j=32),
                    in1=tab.unsqueeze(1).to_broadcast([128, n // 32, 32]), op=ALU.add),
                    reads=[rp, rt], writes=[rs])
            S.dma("sp", dst3d[j, :, r0:r0 + n], stg[:, :n], reads=[rs], writes=rdst)
        return f

    def softmax_pv(ab, nq, nk, S_sb, rS, vchunk, out_fn):
        st, rst = st_pool.get()
        S.op("dve", lambda v: v.reduce_max(out=st[:nq, 0:1], in_=S_sb[:nq, :nk], axis=AX.X),
             reads=[rS], writes=[rst])
        S.op("dve", lambda v: v.tensor_scalar(out=st[:nq, 1:2], in0=st[:nq, 0:1], scalar1=-1.0e4,
                                              scalar2=-1.0, op0=ALU.max, op1=ALU.mult),
             reads=[rst], writes=[rst])
        P, rP = ab.P.get()
        S.op("act", lambda a: a.activation(out=P[:nq, :nk], in_=S_sb[:nq, :nk], func=AF.Exp,
                                           bias=st[:nq, 1:2], scale=1.0, accum_out=st[:nq, 2:3]),
             reads=[rS, rst], writes=[rP, rst])
        S.op("dve", lambda v: v.tensor_scalar(out=st[:nq, 3:4], in0=st[:nq, 2:3], scalar1=1.0e-30,
                                              scalar2=None, op0=ALU.max),
             reads=[rst], writes=[rst])
        S.op("dve", lambda v: v.reciprocal(st[:nq, 3:4], st[:nq, 3:4]), reads=[rst], writes=[rst])
        nch = (nk + 127) // 128
        PT, rPT = ab.PT.get()
        per_bank = 1024 // nq
        for c0 in range(0, nch, per_bank):
            n = min(per_bank, nch - c0)
            pt, rp = psT.get()
            fns = []
            for j in range(n):
                c = c0 + j
                kk = min(128, nk - c * 128)
                fns.append(lambda pe, j=j, c=c, kk=kk: pe.transpose(
                    pt[:kk, j * nq:(j + 1) * nq], P[:nq, c * 128:c * 128 + kk], ident_b[:nq, :nq]))
            S.mm_group(fns, reads=[rP, r_ident], writes=[rp])
            copy_op(evac_engine(), PT[:, c0 * nq:(c0 + n) * nq], pt[:, :n * nq], [rp], [rPT])
        ps, rp2 = psB.get()
        fns = []
        vres = []
        for c in range(nch):
            vap, vr, kk = vchunk(c)
            if vr not in vres:
                vres.append(vr)
            fns.append(lambda pe, c=c, vap=vap, kk=kk: pe.matmul(
                ps[:nq, :128], PT[:kk, c * nq:(c + 1) * nq], vap,
                start=(c == 0), stop=(c == nch - 1)))
        S.mm_group(fns, reads=[rPT] + vres, writes=[rp2])
        out_fn(ps, rp2, st, rst)

    class AttnBufs:
        def __init__(self, sc, nq, nkmax):
            self.S = sc.pool("S_sb", 2 if nq > 8 else 1, [nq, nkmax], F32)
            self.P = sc.pool("P_bf", 2 if nq > 8 else 1, [nq, nkmax], BF16)
            nch = (nkmax + 127) // 128
            self.PT = sc.pool("PT", 2 if nq > 8 else 1, [128, nch * nq], BF16)

    def topk_selneg(sc_pool, score, rscore, nq, nblk, selneg, rsel):
        st, rst = st_pool.get()
        m8, rm8 = sc_pool.get()
        S.op("dve", lambda v: v.max(out=m8[:nq, 0:8], in_=score[:nq, :nblk]),
             reads=[rscore], writes=[rm8])
        S.op("dve", lambda v: v.match_replace(out=m8[:nq, 16:16 + nblk], in_to_replace=m8[:nq, 0:8],
                                              in_values=score[:nq, :nblk], imm_value=-3.0e38),
             reads=[rscore, rm8], writes=[rm8])
        S.op("dve", lambda v: v.max(out=m8[:nq, 8:16], in_=m8[:nq, 16:16 + nblk]),
             reads=[rm8], writes=[rm8])
        S.op("dve", lambda v: v.tensor_scalar(out=selneg[:nq, :nblk], in0=score[:nq, :nblk],
                                              scalar1=m8[:nq, 15:16], scalar2=None, op0=ALU.is_ge),
             reads=[rscore, rm8], writes=[rsel])
        S.op("dve", lambda v: v.tensor_scalar(out=selneg[:nq, :nblk], in0=selneg[:nq, :nblk],
                                              scalar1=-1.0, scalar2=-NEG, op0=ALU.add, op1=ALU.mult),
             reads=[rsel], writes=[rsel])

    def mem_prepare(layer, mk, w_pool, nb, stage_pool, hT, r_hT):
        gt, gr = load_gain(nb, mem_norm_g[layer:layer + 1, :])
        for t in range(2):
            norm_tile(nb, memp[t * 128:(t + 1) * 128, :], [], 128, gt, gr, t * 128, hT, r_hT)

        def sink_tm(t0, P, cb0, cw, ps, rp):
            stg, rs = stage_pool.get()
            copy_op(evac_engine(), stg[:P, :cw], ps[:P, :cw], [rp], [rs])
            S.dma("sp", o_mem_p[layer][t0:t0 + P, cb0:cb0 + cw], stg[:P, :cw], reads=[rs])
            if cb0 == 512:
                S.op("pool", lambda g: g.tensor_copy(mk["pV"][:, t0 // 128, :], stg[:, :512]),
                     reads=[rs], writes=[mk["r_pV"]])
        linear_tm(w_pool, hT, r_hT, 256, mem_w_b[layer], 0, 1024, sink_tm)

        def sink_fm(j, ps, rp):
            copy_op(evac_engine(), mk["pKT"][:, j, :], ps[:, :256], [rp], [mk["r_pKT"]])
        linear_fm(w_pool, hT, r_hT, 256, mem_w_b[layer], 0, 4, sink_fm)
        for t in range(2):
            xt, rx = nb.xt.get()
            S.dma("sp", xt[:, :1024], mem_cache[layer][t * 128:(t + 1) * 128, :], writes=[rx])
            S.op("pool", lambda g: g.tensor_copy(mk["sV"][:, t, :], xt[:, 512:1024]),
                 reads=[rx], writes=[mk["r_sV"]])
            hb, rh = nb.hb.get()
            S.op("dve", lambda v: v.tensor_copy(hb[:, :512], xt[:, :512]), reads=[rx], writes=[rh])
            transpose_into(hb, rh, 128, 4,
                           lambda c0, n: mk["sKT"][:, c0:c0 + n, t * 128:(t + 1) * 128], mk["r_sKT"])

    def phaseA_nsa(layer, mk):
        a = layer // 2
        W = nsa_w_b[a]
        with Scope() as sc:
            nb = NormBufs(sc)
            hT = sc.sb("hT", [128, KC, 512], BF16)
            r_hT = Res()
            w_pool = sc.pool("wbuf", 3, [128, KC, 512], BF16)
            stage = sc.pool("stage", 4, [128, 512], F32)
            stage_b = sc.pool("stageb", 4, [128, 512], BF16)
            peT = sc.sb("peT", [128, 2, 32], F32)
            r_peT = Res()
            S.dma("sp", peT[:], cmp_peT[a].rearrange("t d j -> d t j"), writes=[r_peT])
            mem_prepare(layer, mk, w_pool, nb, stage, hT, r_hT)
            gt, gr = load_gain(nb, norm1_g[layer:layer + 1, :])
            for (r0, n) in BLOCKS:
                samp = r0 >= SEQ
                for t0 in range(0, n, 128):
                    P = min(128, n - t0)
                    norm_tile(nb, x_rows(layer, r0 + t0, P), [xres(r0)] if layer > 0 else [], P, gt, gr,
                              t0, hT, r_hT)
                okv = o_kv_s[a] if samp else o_kv_p[a][r0:r0 + n, :]
                linear_tm(w_pool, hT, r_hT, n, W, 1536, 2048, tm_sink_dram(stage, okv, [r_okv]))
                linear_tm(w_pool, hT, r_hT, n, W, 3584, 1024,
                          tm_sink_dram(stage, winr[r0:r0 + n, :], [r_winr]))
                linear_tm(w_pool, hT, r_hT, n, W, 4608, 36,
                          tm_sink_dram(stage, gates[r0:r0 + n, :], [r_gates], func=AF.Sigmoid))
                linear_fm(w_pool, hT, r_hT, n, W, 0, 12, fm_sink_dram(stage_b, QT, [r_QT], r0, n))
                if not samp:
                    linear_fm(w_pool, hT, r_hT, n, W, 1536, 8,
                              fm_sink_dram(stage_b, rcT, [r_rcT], r0, n,
                                           add_tab=lambda j: (peT[:, j // 4, :], r_peT)))
                linear_fm(w_pool, hT, r_hT, n, W, 2560, 4, fm_sink_dram(stage_b, ksT, [r_ksT], r0, n))
                linear_fm(w_pool, hT, r_hT, n, W, 3584, 4, fm_sink_dram(stage_b, kwT, [r_kwT], r0, n))
                linear_fm(w_pool, hT, r_hT, n, W, 4644, 4, fm_sink_dram(stage_b, qmT, [r_qmT], r0, n))
            S.dma("sp", o_win_p[a], winr[SEQ - 512:SEQ, :], reads=[r_winr])
            S.dma("sp", o_win_s[a][504:512, :], winr[SEQ:SEQ + T_S, :], reads=[r_winr])
            S.dma("sp", o_win_s[a][0:504, :], win_state[a][8:512, :])

    def phaseA_ret(layer, mk):
        bl = layer // 2
        W = ret_w_b[bl]
        with Scope() as sc:
            nb = NormBufs(sc)
            hT = sc.sb("hT", [128, KC, 512], BF16)
            r_hT = Res()
            w_pool = sc.pool("wbuf", 3, [128, KC, 512], BF16)
            stage = sc.pool("stage", 4, [128, 512], F32)
            stage_b = sc.pool("stageb", 4, [128, 512], BF16)
            tmp = sc.pool("rot", 4, [128, 512], F32)
            cosb = sc.sb("cosb", [128, 512], F32)
            sinb = sc.sb("sinb", [128, 512], F32)
            r_cs = Res()
            mem_prepare(layer, mk, w_pool, nb, stage, hT, r_hT)
            gt, gr = load_gain(nb, norm1_g[layer:layer + 1, :])
            for (r0, n) in BLOCKS:
                for t0 in range(0, n, 128):
                    P = min(128, n - t0)
                    norm_tile(nb, x_rows(layer, r0 + t0, P), [xres(r0)], P, gt, gr, t0, hT, r_hT)
                S.dma("sp", cosb[:, :n], tabs["cosT"][:, r0:r0 + n], writes=[r_cs])
                S.dma("sp", sinb[:, :n], tabs["sinT"][:, r0:r0 + n], writes=[r_cs])

                def rot_sink(dst, rdst, scl):
                    held = {}

                    def f(j, ps, rp):
                        if j % 2 == 0:
                            held["x1"] = (ps, rp)
                            return
                        p1, r1 = held["x1"]
                        p2, r2 = ps, rp
                        ta, ra = tmp.get()
                        tb, rb = tmp.get()
                        o1, ro1 = stage_b.get()
                        o2, ro2 = stage_b.get()
                        S.op("dve", lambda v: v.scalar_tensor_tensor(
                            out=ta[:, :n], in0=p1[:, :n], scalar=scl, in1=cosb[:, :n],
                            op0=ALU.mult, op1=ALU.mult), reads=[r1, r_cs], writes=[ra])
                        S.op("dve", lambda v: v.scalar_tensor_tensor(
                            out=tb[:, :n], in0=p2[:, :n], scalar=scl, in1=sinb[:, :n],
                            op0=ALU.mult, op1=ALU.mult), reads=[r2, r_cs], writes=[rb])
                        S.op("pool", lambda g: g.tensor_tensor(out=o1[:, :n], in0=ta[:, :n], in1=tb[:, :n],
                                                               op=ALU.subtract),
                             reads=[ra, rb], writes=[ro1])
                        tc_, rc_ = tmp.get()
                        td, rd = tmp.get()
                        S.op("dve", lambda v: v.scalar_tensor_tensor(
                            out=tc_[:, :n], in0=p1[:, :n], scalar=scl, in1=sinb[:, :n],
                            op0=ALU.mult, op1=ALU.mult), reads=[r1, r_cs], writes=[rc_])
                        S.op("dve", lambda v: v.scalar_tensor_tensor(
                            out=td[:, :n], in0=p2[:, :n], scalar=scl, in1=cosb[:, :n],
                            op0=ALU.mult, op1=ALU.mult), reads=[r2, r_cs], writes=[rd])
                        S.op("pool", lambda g: g.tensor_tensor(out=o2[:, :n], in0=tc_[:, :n], in1=td[:, :n],
                                                               op=ALU.add),
                             reads=[rc_, rd], writes=[ro2])
                        S.dma("sp", dst[j - 1, :, r0:r0 + n], o1[:, :n], reads=[ro1], writes=rdst)
                        S.dma("sp", dst[j, :, r0:r0 + n], o2[:, :n], reads=[ro2], writes=rdst)
                    return f
                linear_fm(w_pool, hT, r_hT, n, W, 0, 12, rot_sink(QT, [r_QT], 1.0))
                linear_fm(w_pool, hT, r_hT, n, W, 1536, 12, rot_sink(KT12, [r_KT12], 1.0 / 16.0))
                linear_fm(w_pool, hT, r_hT, n, W, 6144, 4, fm_sink_dram(stage_b, qmT, [r_qmT], r0, n))

                def v_sink(t0, P, cb0, cw, ps, rp):
                    stg, rs = stage_b.get()
                    copy_op(evac_engine(), stg[:P, :cw], ps[:P, :cw], [rp], [rs])
                    S.dma("sp", vtm[r0 + t0:r0 + t0 + P, cb0:cb0 + cw], stg[:P, :cw], reads=[rs],
                          writes=[r_vtm])
                linear_tm(w_pool, hT, r_hT, n, W, 3072, 1536, v_sink)
                linear_tm(w_pool, hT, r_hT, n, W, 4608, 1536,
                          tm_sink_dram(stage, sgate[r0:r0 + n, :], [r_sgate], func=AF.Silu))

    def gelu_tanh(sc_tmp, ps, rp, n, dst, rdst):
        x, rx = sc_tmp.get()
        u, ru = sc_tmp.get()
        copy_op("act", x[:, :n], ps[:, :n], [rp], [rx])
        S.op("dve", lambda v: v.tensor_tensor(out=u[:, :n], in0=x[:, :n], in1=x[:, :n], op=ALU.mult),
             reads=[rx], writes=[ru])
        S.op("dve", lambda v: v.tensor_scalar(out=u[:, :n], in0=u[:, :n], scalar1=0.044715, scalar2=1.0,
                                              op0=ALU.mult, op1=ALU.add), reads=[ru], writes=[ru])
        S.op("dve", lambda v: v.tensor_tensor(out=u[:, :n], in0=u[:, :n], in1=x[:, :n], op=ALU.mult),
             reads=[ru, rx], writes=[ru])
        S.op("act", lambda a: a.activation(out=u[:, :n], in_=u[:, :n], func=AF.Sigmoid,
                                           scale=2.0 * 0.7978845608028654),
             reads=[ru], writes=[ru])
        S.op("dve", lambda v: v.tensor_tensor(out=dst, in0=u[:, :n], in1=x[:, :n], op=ALU.mult),
             reads=[ru, rx], writes=[rdst])

    def compress(sc, a, ty, rc_ap, rrc, nblk, kcT_dst, vc_dst, rdst, w1t, w2t, rw, tmp):
        ps, rp = psA.get()
        rc3 = rc_ap.rearrange("p (b j) -> p j b", j=32)
        fns = []
        for j in range(32):
            fns.append(lambda pe, j=j: pe.matmul(ps[:, :nblk], w1t[:, ty, j, :], rc3[:, j, :],
                                                 start=(j == 0), stop=(j == 31)))
        S.mm_group(fns, reads=[rrc, rw], writes=[rp])
        gT, rg = tmp["g"].get()
        gelu_tanh(tmp["f"], ps, rp, nblk, gT[:, :nblk], rg)
        if ty == 0:
            ps2, rp2 = psA.get()
            S.mm_group([lambda pe: pe.matmul(ps2[:, :nblk], w2t[:, 0, :], gT[:, :nblk], start=True, stop=True)],
                       reads=[rg, rw], writes=[rp2])
            copy_op(evac_engine(), kcT_dst, ps2[:, :nblk], [rp2], [rdst])
        else:
            for c in range(nblk // 128):
                ps2, rp2 = psA.get()
                S.mm_group([lambda pe, c=c: pe.matmul(ps2[:, :128], gT[:, c * 128:(c + 1) * 128], w2t[:, 1, :],
                                                      start=True, stop=True)],
                           reads=[rg, rw], writes=[rp2])
                copy_op(evac_engine(), vc_dst(c), ps2[:, :128], [rp2], [rdst])

    def load_cmp_weights(sc, a):
        w1t = sc.sb("w1t", [128, 2, 32, 128], BF16)
        w2t = sc.sb("w2t", [128, 2, 128], BF16)
        rw = Res()
        for ty in range(2):
            S.dma("pool", w1t[:, ty, :, :], cmp_w1[a][ty].rearrange("(j d) o -> d j o", d=128), writes=[rw])
            S.dma("pool", w2t[:, ty, :], cmp_w2[a][ty], writes=[rw])
        return w1t, w2t, rw

    def nsa_mix_prompt(layer):
        a = layer // 2
        with Scope() as sc:
            ab = AttnBufs(sc, 128, 4096)
            cmpmask_t = sc.sb("cmpmask", [128, 32, 128], F32)
            bonus_t = sc.sb("bonus", [128, 32, 64], F32)
            tri_le = sc.sb("tri_le", [128, 128], F32)
            tri_gt = sc.sb("tri_gt", [128, 128], F32)
            gates_t = sc.sb("gates_t", [128, 32, 36], F32)
            r_tab = Res()
            S.dma("sp", cmpmask_t[:], tabs["cmpmask"].rearrange("(t p) n -> p t n", p=128), writes=[r_tab])
            S.dma("sp", bonus_t[:], tabs["bonus"].rearrange("(t p) n -> p t n", p=128), writes=[r_tab])
            S.dma("sp", tri_le[:], tabs["tri_le"], writes=[r_tab])
            S.dma("sp", tri_gt[:], tabs["tri_gt"], writes=[r_tab])
            S.dma("sp", gates_t[:], gates[0:SEQ, :].rearrange("(t p) c -> p t c", p=128),
                  reads=[r_gates], writes=[r_tab])
            kcT = sc.sb("kcT", [128, 4, 128], BF16)
            vc = sc.sb("vc", [128, 4, 128], BF16)
            r_kv = Res()
            with Scope() as scc:
                w1t, w2t, rw = load_cmp_weights(scc, a)
                tmp = {"g": scc.pool("gT", 2, [128, 512], BF16), "f": scc.pool("gf", 4, [128, 512], F32)}
                rcb = scc.pool("rcb", 2, [128, SEQ], BF16)
                for g in range(4):
                    for ty in range(2):
                        rc, rrc = rcb.get()
                        S.dma("sp", rc[:], rcT[ty * 4 + g], reads=[r_rcT], writes=[rrc])
                        compress(scc, a, ty, rc[:], rrc, 128, kcT[:, g, :], lambda c, g=g: vc[:, g, :], r_kv,
                                 w1t, w2t, rw, tmp)
            QTg = sc.sb("QTg", [128, 3, SEQ], BF16)
            ksTg = sc.sb("ksTg", [128, SEQ], BF16)
            kwTg = sc.sb("kwTg", [128, SEQ], BF16)
            vsg = sc.sb("vsg", [128, 32, 128], BF16)
            vwg = sc.sb("vwg", [128, 32, 128], BF16)
            r_grp = Res()
            small = sc.pool("small", 6, [128, 128], F32)
            pbf = sc.pool("pbf", 2, [128, 128], BF16)
            ptc = sc.pool("ptc", 2, [128, 128], BF16)
            sel_pool = sc.pool("selp", 2, [128, 64], F32)
            m8_pool = sc.pool("m8", 2, [128, 16 + 64], F32)
            ocomb_pool = sc.pool("ocomb", 2, [128, 384], F32)
            ob_pool = sc.pool("ob", 2, [128, 384], BF16)
            for g in range(4):
                S.dma("sp", QTg[:], QT[3 * g:3 * g + 3, :, 0:SEQ].rearrange("h p n -> p h n"),
                      reads=[r_QT], writes=[r_grp])
                S.dma("sp", ksTg[:], ksT[g, :, 0:SEQ], reads=[r_ksT], writes=[r_grp])
                S.dma("sp", kwTg[:], kwT[g, :, 0:SEQ], reads=[r_kwT], writes=[r_grp])
                S.dma("pool", vsg[:], o_kv_p[a][:, 1536 + g * 128:1536 + (g + 1) * 128].rearrange(
                    "(c p) d -> p c d", p=128), reads=[r_okv], writes=[r_grp])
                S.dma("pool", vwg[:], winr[0:SEQ, 512 + g * 128:512 + (g + 1) * 128].rearrange(
                    "(c p) d -> p c d", p=128), reads=[r_winr], writes=[r_grp])
                for i in range(32):
                    qs = slice(i * 128, (i + 1) * 128)
                    oc, roc = ocomb_pool.get()
                    pgrp, rpg = small.get()
                    first = [True]

                    def gated_out(col, hh, oc=oc, roc=roc, i=i):
                        h = 3 * g + hh
                        gcol = gates_t[:, i, h * 3 + col:h * 3 + col + 1]

                        def f(ps, rp, st, rst):
                            if st is not None:
                                S.op("dve", lambda v: v.tensor_tensor(out=st[:, 4:5], in0=st[:, 3:4], in1=gcol,
                                                                      op=ALU.mult),
                                     reads=[rst, r_tab], writes=[rst])
                                sc_ap, rr = st[:, 4:5], [rst]
                            else:
                                sc_ap, rr = gcol, [r_tab]
                            dst = oc[:, hh * 128:(hh + 1) * 128]
                            if col == 0:
                                S.op("dve", lambda v: v.tensor_scalar(out=dst, in0=ps[:, :128], scalar1=sc_ap,
                                                                      scalar2=None, op0=ALU.mult),
                                     reads=[rp] + rr, writes=[roc])
                            else:
                                S.op("dve", lambda v: v.scalar_tensor_tensor(
                                    out=dst, in0=ps[:, :128], scalar=sc_ap, in1=dst, op0=ALU.mult, op1=ALU.add),
                                    reads=[rp] + rr, writes=[roc])
                        return f
                    for hh in range(3):
                        ps, rp = psA.get()
                        S.mm_group([lambda pe, hh=hh: pe.matmul(ps[:, :128], QTg[:, hh, qs], kcT[:, g, :],
                                                                start=True, stop=True)],
                                   reads=[r_grp, r_kv], writes=[rp])
                        sc_t, rsc = small.get()
                        S.op("dve", lambda v: v.scalar_tensor_tensor(
                            out=sc_t[:], in0=ps[:, :128], scalar=SCALE, in1=cmpmask_t[:, i, :],
                            op0=ALU.mult, op1=ALU.add), reads=[rp, r_tab], writes=[rsc])
                        st, rst = st_pool.get()
                        S.op("dve", lambda v: v.reduce_max(out=st[:, 0:1], in_=sc_t[:], axis=AX.X),
                             reads=[rsc], writes=[rst])
                        S.op("dve", lambda v: v.tensor_scalar(out=st[:, 1:2], in0=st[:, 0:1], scalar1=-1.0e4,
                                                              scalar2=-1.0, op0=ALU.max, op1=ALU.mult),
                             reads=[rst], writes=[rst])
                        S.op("act", lambda a_: a_.activation(out=sc_t[:], in_=sc_t[:], func=AF.Exp,
                                                             bias=st[:, 1:2], scale=1.0, accum_out=st[:, 2:3]),
                             reads=[rsc, rst], writes=[rsc, rst])
                        S.op("dve", lambda v: v.tensor_scalar(out=st[:, 3:4], in0=st[:, 2:3], scalar1=1.0e-30,
                                                              scalar2=None, op0=ALU.max), reads=[rst], writes=[rst])
                        S.op("dve", lambda v: v.reciprocal(st[:, 3:4], st[:, 3:4]), reads=[rst], writes=[rst])
                        S.op("dve", lambda v: v.tensor_scalar(out=sc_t[:], in0=sc_t[:], scalar1=st[:, 3:4],
                                                              scalar2=None, op0=ALU.mult),
                             reads=[rsc, rst], writes=[rsc])
                        if hh == 0:
                            S.op("pool", lambda g_: g_.tensor_copy(pgrp[:], sc_t[:]), reads=[rsc], writes=[rpg])
                        else:
                            S.op("pool", lambda g_: g_.tensor_tensor(out=pgrp[:], in0=pgrp[:], in1=sc_t[:],
                                                                     op=ALU.add), reads=[rsc], writes=[rpg])
                        pb, rpb = pbf.get()
                        S.op("act", lambda a_: a_.copy(pb[:], sc_t[:]), reads=[rsc], writes=[rpb])
                        pt, rpt = psT.get()
                        S.mm_group([lambda pe: pe.transpose(pt[:, :128], pb[:], ident_b[:])],
                                   reads=[rpb, r_ident], writes=[rpt])
                        pc, rpc = ptc.get()
                        copy_op("act", pc[:], pt[:, :128], [rpt], [rpc])
                        ps2, rp2 = psB.get()
                        S.mm_group([lambda pe: pe.matmul(ps2[:, :128], pc[:], vc[:, g, :], start=True, stop=True)],
                                   reads=[rpc, r_kv], writes=[rp2])
                        gated_out(0, hh)(ps2, rp2, None, None)
                    score, rscore = sel_pool.get()
                    pg3 = pgrp[:].rearrange("p (b two) -> p b two", two=2)
                    S.op("dve", lambda v: v.tensor_tensor(out=score[:], in0=pg3[:, :, 0], in1=pg3[:, :, 1],
                                                          op=ALU.add), reads=[rpg], writes=[rscore])
                    S.op("dve", lambda v: v.tensor_tensor(out=score[:], in0=score[:], in1=bonus_t[:, i, :],
                                                          op=ALU.add), reads=[rscore, r_tab], writes=[rscore])
                    selneg, rsel = sel_pool.get()
                    topk_selneg(m8_pool, score, rscore, 128, 64, selneg, rsel)
                    for hh in range(3):
                        nk = (i + 1) * 128
                        Ssb, rS = ab.S.get()
                        for kb in range(0, nk, 512):
                            w = min(512, nk - kb)
                            ps, rp = psA.get()
                            S.mm_group([lambda pe, hh=hh, kb=kb, w=w: pe.matmul(
                                ps[:, :w], QTg[:, hh, qs], ksTg[:, kb:kb + w], start=True, stop=True)],
                                reads=[r_grp], writes=[rp])
                            nb_ = w // 64
                            S.op("dve", lambda v, kb=kb, w=w, nb_=nb_: v.scalar_tensor_tensor(
                                out=Ssb[:, kb:kb + w].rearrange("p (b k) -> p b k", k=64),
                                in0=ps[:, :w].rearrange("p (b k) -> p b k", k=64), scalar=SCALE,
                                in1=selneg[:, kb // 64:kb // 64 + nb_].unsqueeze(2).to_broadcast([128, nb_, 64]),
                                op0=ALU.mult, op1=ALU.add), reads=[rp, rsel], writes=[rS])
                        S.op("pool", lambda g_: g_.tensor_tensor(out=Ssb[:, i * 128:(i + 1) * 128],
                                                                 in0=Ssb[:, i * 128:(i + 1) * 128], in1=tri_le[:],
                                                                 op=ALU.add), reads=[r_tab], writes=[rS])
                        softmax_pv(ab, 128, nk, Ssb, rS, lambda c: (vsg[:, c, :], r_grp, 128), gated_out(1, hh))
                        c0 = max(0, i - 4)
                        nk = (i + 1 - c0) * 128
                        Ssb, rS = ab.S.get()
                        for kb in range(0, nk, 512):
                            w = min(512, nk - kb)
                            ps, rp = psA.get()
                            S.mm_group([lambda pe, hh=hh, kb=kb, w=w, c0=c0: pe.matmul(
                                ps[:, :w], QTg[:, hh, qs], kwTg[:, c0 * 128 + kb:c0 * 128 + kb + w],
                                start=True, stop=True)], reads=[r_grp], writes=[rp])
                            S.op("act", lambda a_, kb=kb, w=w: a_.activation(
                                out=Ssb[:, kb:kb + w], in_=ps[:, :w], func=AF.Identity, scale=SCALE),
                                reads=[rp], writes=[rS])
                        if i >= 4:
                            S.op("pool", lambda g_: g_.tensor_tensor(out=Ssb[:, 0:128], in0=Ssb[:, 0:128],
                                                                     in1=tri_gt[:], op=ALU.add),
                                 reads=[r_tab], writes=[rS])
                        S.op("pool", lambda g_, nk=nk: g_.tensor_tensor(out=Ssb[:, nk - 128:nk], in0=Ssb[:, nk - 128:nk],
                                                                        in1=tri_le[:], op=ALU.add),
                             reads=[r_tab], writes=[rS])
                        softmax_pv(ab, 128, nk, Ssb, rS, lambda c, c0=c0: (vwg[:, c0 + c, :], r_grp, 128),
                                   gated_out(2, hh))
                    ob, rob = ob_pool.get()
                    S.op("act", lambda a_: a_.copy(ob[:], oc[:]), reads=[roc], writes=[rob])
                    S.dma("sp", tokb[i * 128:(i + 1) * 128, g * 384:(g + 1) * 384], ob[:], reads=[rob],
                          writes=[r_tokb])

    def nsa_mix_sample(layer):
        a = layer // 2
        cache2d = cache.rearrange("a r c -> (a r) c")
        with Scope() as sc:
            kcT = sc.sb("kcT_s", [128, 4, 512], BF16)
            vc = sc.sb("vc_s", [128, 4, 4, 128], BF16)
            r_kv = Res()
            peT = sc.sb("peT_s", [128, 2, 32], F32)
            r_peT = Res()
            S.dma("sp", peT[:], cmp_peT[a].rearrange("t d j -> d t j"), writes=[r_peT])
            idx = sc.sb("idx", [128, NPAGE], I32)
            idxf = sc.sb("idxf", [128, NPAGE], F32)
            iot = sc.sb("iot", [128, 1], F32)
            r_idx = Res()
            S.dma("sp", idx[:], page_tab[0:1, :].to_broadcast([128, NPAGE]), writes=[r_idx])
            S.op("pool", lambda g_: g_.iota(iot[:], pattern=[[0, 1]], base=0, channel_multiplier=1,
                                            allow_small_or_imprecise_dtypes=True), writes=[r_idx])
            S.op("dve", lambda v: v.tensor_copy(idxf[:], idx[:]), reads=[r_idx], writes=[r_idx])
            S.op("dve", lambda v: v.tensor_scalar(out=idxf[:], in0=idxf[:], scalar1=128.0, scalar2=iot[:, 0:1],
                                                  op0=ALU.mult, op1=ALU.add), reads=[r_idx], writes=[r_idx])
            if a > 0:
                S.op("dve", lambda v: v.tensor_scalar(out=idxf[:], in0=idxf[:], scalar1=float(a * 1280 * 128),
                                                      scalar2=None, op0=ALU.add), reads=[r_idx], writes=[r_idx])
            S.op("dve", lambda v: v.tensor_copy(idx[:], idxf[:]), reads=[r_idx], writes=[r_idx])
            with Scope() as sc1:
                w1t, w2t, rw = load_cmp_weights(sc1, a)
                tmp = {"g": sc1.pool("gT", 2, [128, 512], BF16), "f": sc1.pool("gf", 4, [128, 512], F32)}
                page_pool = sc1.pool("page", 3, [128, 2048], F32)
                rcb = sc1.sb("rcb_s", [128, 8, 4096], BF16)
                r_rcb = Res()
                ksst = sc1.pool("ksst", 2, [128, 4, 512], BF16)
                vsst = sc1.pool("vsst", 2, [128, 4, 4, 128], BF16)
                psF = psB
                for batch in range(4):
                    for pq in range(8):
                        kst, rks = ksst.get()
                        vst, rvs = vsst.get()
                        for pp in range(4):
                            pg = batch * 32 + pq * 4 + pp
                            pt_, rpg = page_pool.get()
                            S.dma("pool", None, None, reads=[r_idx], writes=[rpg],
                                  fn=lambda g_, pt_=pt_, pg=pg: g_.indirect_dma_start(
                                      out=pt_[:, :], out_offset=None, in_=cache2d[:, :],
                                      in_offset=bass.IndirectOffsetOnAxis(ap=idx[:, pg:pg + 1], axis=0)))
                            S.op("pool", lambda g_, pt_=pt_, pp=pp: g_.tensor_copy(
                                vst[:, pp, :, :], pt_[:, 1536:2048].rearrange("p (g d) -> p g d", d=128)),
                                reads=[rpg], writes=[rvs])
                            for ty in range(3):
                                ps, rp = psF.get()
                                fns = []
                                for gg in range(4):
                                    col = (ty * 4 + gg) * 128
                                    fns.append(lambda pe, gg=gg, col=col, pt_=pt_: pe.transpose(
                                        ps[:, gg * 128:(gg + 1) * 128], pt_[:, col:col + 128], ident_f[:]))
                                S.mm_group(fns, reads=[rpg, r_ident], writes=[rp])
                                src = ps[:].rearrange("p (g r) -> p g r", r=128)
                                loc = (pq * 4 + pp) * 128
                                if ty < 2:
                                    S.op("dve", lambda v, ty=ty, loc=loc, src=src: v.tensor_tensor(
                                        out=rcb[:, ty * 4:(ty + 1) * 4, loc:loc + 128].rearrange(
                                            "p g (b j) -> p g b j", j=32),
                                        in0=src.rearrange("p g (b j) -> p g b j", j=32),
                                        in1=peT[:, ty, :].unsqueeze(1).unsqueeze(1).to_broadcast([128, 4, 4, 32]),
                                        op=ALU.add), reads=[rp, r_peT], writes=[r_rcb])
                                else:
                                    copy_op("act", kst[:, :, pp * 128:(pp + 1) * 128], src, [rp], [rks])
                        p0 = (batch * 32 + pq * 4) * 128
                        S.dma("sp", ksT_s[:, :, p0:p0 + 512].rearrange("g p n -> p g n"), kst[:], reads=[rks],
                              writes=[r_ksT_s])
                        for gg in range(4):
                            S.dma("sp", vs_s[gg, p0:p0 + 512, :].rearrange("(q r) d -> r q d", r=128),
                                  vst[:, :, gg, :], reads=[rvs], writes=[r_vs_s])
                    for g in range(4):
                        for ty in range(2):
                            compress(sc1, a, ty, rcb[:, ty * 4 + g, :], r_rcb, 128,
                                     kcT[:, g, batch * 128:(batch + 1) * 128],
                                     lambda c, g=g, batch=batch: vc[:, g, batch, :], r_kv, w1t, w2t, rw, tmp)
            with Scope() as sc2:
                NK = PAST + T_S
                ab = AttnBufs(sc2, T_S, NK)
                s_bonus = sc2.sb("s_bonus", [T_S, 264], F32)
                s_tri8 = sc2.sb("s_tri8", [T_S, 8], F32)
                s_winm = sc2.sb("s_winm", [T_S, 512], F32)
                gates_t = sc2.sb("gates_s", [T_S, 36], F32)
                r_tab = Res()
                S.dma("sp", s_bonus[:], tabs["s_bonus"][0:T_S, :], writes=[r_tab])
                S.dma("sp", s_tri8[:], tabs["s_tri8"][0:T_S, :], writes=[r_tab])
                S.dma("sp", s_winm[:], tabs["s_winmask"][0:T_S, :], writes=[r_tab])
                S.dma("sp", gates_t[:], gates[SEQ:NTOK, :], reads=[r_gates], writes=[r_tab])
                QTg = sc2.sb("QTg_s", [128, 3, T_S], BF16)
                ksTg = sc2.sb("ksTg_s", [128, NK], BF16)
                vsg = sc2.sb("vsg_s", [128, 129, 128], BF16)
                kwTg = sc2.sb("kwTg_s", [128, 520], BF16)
                vwg = sc2.sb("vwg_s", [128, 5, 128], BF16)
                wst = sc2.sb("wst", [128, 4, 128], F32)
                wsb = sc2.sb("wsb", [128, 512], BF16)
                r_ws = Res()
                r_grp = Res()
                small = sc2.pool("small_s", 3, [T_S, 512], F32)
                pgrp_t = sc2.sb("pgrp_s", [T_S, 512], F32)
                r_pgrp = Res()
                pbf = sc2.pool("pbf_s", 2, [T_S, 512], BF16)
                ptc = sc2.pool("ptc_s", 2, [128, 4 * T_S], BF16)
                sel_pool = sc2.pool("selp_s", 2, [T_S, 264], F32)
                m8_pool = sc2.pool("m8_s", 2, [T_S, 16 + 264], F32)
                oc = sc2.sb("ocomb_s", [T_S, 384], F32)
                roc = Res()
                ob = sc2.sb("ob_s", [T_S, 384], BF16)
                for g in range(4):
                    S.dma("sp", QTg[:], QT[3 * g:3 * g + 3, :, SEQ:NTOK].rearrange("h p n -> p h n"),
                          reads=[r_QT], writes=[r_grp])
                    S.dma("sp", ksTg[:, 0:PAST], ksT_s[g], reads=[r_ksT_s], writes=[r_grp])
                    S.dma("sp", ksTg[:, PAST:NK], ksT[g, :, SEQ:NTOK], reads=[r_ksT], writes=[r_grp])
                    S.dma("sp", vsg[:, 0:128, :], vs_s[g].rearrange("(c p) d -> p c d", p=128),
                          reads=[r_vs_s], writes=[r_grp])
                    S.dma("pool", vsg[:T_S, 128, :], o_kv_s[a][:, 1536 + g * 128:1536 + (g + 1) * 128],
                          reads=[r_okv], writes=[r_grp])
                    S.dma("sp", wst[:], win_state[a][:, g * 128:(g + 1) * 128].rearrange("(c p) d -> p c d", p=128),
                          writes=[r_ws])
                    S.op("dve", lambda v: v.tensor_copy(wsb[:], wst[:].rearrange("p c d -> p (c d)")),
                         reads=[r_ws], writes=[r_ws])
                    transpose_into(wsb, r_ws, 128, 4,
                                   lambda c0, n: kwTg[:, 0:512].rearrange("p (c r) -> p c r", r=128)[:, c0:c0 + n, :],
                                   r_grp)
                    S.dma("sp", kwTg[:, 512:520], kwT[g, :, SEQ:NTOK], reads=[r_kwT], writes=[r_grp])
                    S.dma("pool", vwg[:, 0:4, :],
                          win_state[a][:, 512 + g * 128:512 + (g + 1) * 128].rearrange("(c p) d -> p c d", p=128),
                          writes=[r_grp])
                    S.dma("pool", vwg[:T_S, 4, :], winr[SEQ:NTOK, 512 + g * 128:512 + (g + 1) * 128],
                          reads=[r_winr], writes=[r_grp])
                    pgrp, rpg = pgrp_t, r_pgrp

                    def gated_out(col, hh):
                        h = 3 * g + hh
                        gcol = gates_t[:, h * 3 + col:h * 3 + col + 1]

                        def f(ps, rp, st, rst):
                            if st is not None:
                                S.op("dve", lambda v: v.tensor_tensor(out=st[:T_S, 4:5], in0=st[:T_S, 3:4], in1=gcol,
                                                                      op=ALU.mult), reads=[rst, r_tab], writes=[rst])
                                sc_ap, rr = st[:T_S, 4:5], [rst]
                            else:
                                sc_ap, rr = gcol, [r_tab]
                            dst = oc[:, hh * 128:(hh + 1) * 128]
                            if col == 0:
                                S.op("dve", lambda v: v.tensor_scalar(out=dst, in0=ps[:T_S, :128], scalar1=sc_ap,
                                                                      scalar2=None, op0=ALU.mult),
                                     reads=[rp] + rr, writes=[roc])
                            else:
                                S.op("dve", lambda v: v.scalar_tensor_tensor(
                                    out=dst, in0=ps[:T_S, :128], scalar=sc_ap, in1=dst, op0=ALU.mult, op1=ALU.add),
                                    reads=[rp] + rr, writes=[roc])
                        return f
                    for hh in range(3):
                        ps, rp = psA.get()
                        S.mm_group([lambda pe, hh=hh: pe.matmul(ps[:T_S, :512], QTg[:, hh, :], kcT[:, g, :],
                                                                start=True, stop=True)],
                                   reads=[r_grp, r_kv], writes=[rp])
                        sc_t, rsc = small.get()
                        S.op("act", lambda a_: a_.activation(out=sc_t[:], in_=ps[:T_S, :512], func=AF.Identity,
                                                             scale=SCALE), reads=[rp], writes=[rsc])
                        st, rst = st_pool.get()
                        S.op("dve", lambda v: v.reduce_max(out=st[:T_S, 0:1], in_=sc_t[:], axis=AX.X),
                             reads=[rsc], writes=[rst])
                        S.op("dve", lambda v: v.tensor_scalar(out=st[:T_S, 1:2], in0=st[:T_S, 0:1], scalar1=-1.0,
                                                              scalar2=None, op0=ALU.mult), reads=[rst], writes=[rst])
                        S.op("act", lambda a_: a_.activation(out=sc_t[:], in_=sc_t[:], func=AF.Exp,
                                                             bias=st[:T_S, 1:2], scale=1.0, accum_out=st[:T_S, 2:3]),
                             reads=[rsc, rst], writes=[rsc, rst])
                        S.op("dve", lambda v: v.reciprocal(st[:T_S, 3:4], st[:T_S, 2:3]), reads=[rst], writes=[rst])
                        S.op("dve", lambda v: v.tensor_scalar(out=sc_t[:], in0=sc_t[:], scalar1=st[:T_S, 3:4],
                                                              scalar2=None, op0=ALU.mult),
                             reads=[rsc, rst], writes=[rsc])
                        if hh == 0:
                            S.op("pool", lambda g_: g_.tensor_copy(pgrp[:], sc_t[:]), reads=[rsc], writes=[rpg])
                        else:
                            S.op("pool", lambda g_: g_.tensor_tensor(out=pgrp[:], in0=pgrp[:], in1=sc_t[:],
                                                                     op=ALU.add), reads=[rsc], writes=[rpg])
                        pb, rpb = pbf.get()
                        S.op("act", lambda a_: a_.copy(pb[:], sc_t[:]), reads=[rsc], writes=[rpb])
                        pt, rpt = psT.get()
                        S.mm_group([lambda pe, c=c: pe.transpose(pt[:, c * T_S:(c + 1) * T_S],
                                                                 pb[:, c * 128:(c + 1) * 128], ident_b[:T_S, :T_S])
                                    for c in range(4)], reads=[rpb, r_ident], writes=[rpt])
                        pc, rpc = ptc.get()
                        copy_op("act", pc[:], pt[:, :4 * T_S], [rpt], [rpc])
                        ps2, rp2 = psB.get()
                        S.mm_group([lambda pe, c=c: pe.matmul(ps2[:T_S, :128], pc[:, c * T_S:(c + 1) * T_S],
                                                              vc[:, g, c, :], start=(c == 0), stop=(c == 3))
                                    for c in range(4)], reads=[rpc, r_kv], writes=[rp2])
                        gated_out(0, hh)(ps2, rp2, None, None)
                    score, rscore = sel_pool.get()
                    pg3 = pgrp[:].rearrange("p (b two) -> p b two", two=2)
                    S.op("dve", lambda v: v.tensor_copy(score[:], s_bonus[:]), reads=[r_tab], writes=[rscore])
                    S.op("dve", lambda v: v.tensor_tensor(out=score[:, 0:256], in0=score[:, 0:256], in1=pg3[:, :, 0],
                                                          op=ALU.add), reads=[rpg], writes=[rscore])
                    S.op("dve", lambda v: v.tensor_tensor(out=score[:, 0:256], in0=score[:, 0:256], in1=pg3[:, :, 1],
                                                          op=ALU.add), reads=[rpg], writes=[rscore])
                    selneg, rsel = sel_pool.get()
                    topk_selneg(m8_pool, score, rscore, T_S, 264, selneg, rsel)
                    for hh in range(3):
                        Ssb, rS = ab.S.get()
                        for kb in range(0, PAST, 512):
                            ps, rp = psA.get()
                            S.mm_group([lambda pe, hh=hh, kb=kb: pe.matmul(
                                ps[:T_S, :512], QTg[:, hh, :], ksTg[:, kb:kb + 512], start=True, stop=True)],
                                reads=[r_grp], writes=[rp])
                            S.op("dve", lambda v, kb=kb: v.scalar_tensor_tensor(
                                out=Ssb[:, kb:kb + 512].rearrange("p (b k) -> p b k", k=64),
                                in0=ps[:T_S, :512].rearrange("p (b k) -> p b k", k=64), scalar=SCALE,
                                in1=selneg[:, kb // 64:kb // 64 + 8].unsqueeze(2).to_broadcast([T_S, 8, 64]),
                                op0=ALU.mult, op1=ALU.add), reads=[rp, rsel], writes=[rS])
                        ps, rp = psA.get()
                        S.mm_group([lambda pe, hh=hh: pe.matmul(ps[:T_S, :T_S], QTg[:, hh, :], ksTg[:, PAST:NK],
                                                                start=True, stop=True)], reads=[r_grp], writes=[rp])
                        S.op("dve", lambda v: v.scalar_tensor_tensor(
                            out=Ssb[:, PAST:NK], in0=ps[:T_S, :T_S], scalar=SCALE, in1=s_tri8[:],
                            op0=ALU.mult, op1=ALU.add), reads=[rp, r_tab], writes=[rS])
                        S.op("dve", lambda v: v.tensor_scalar(out=Ssb[:, PAST:NK], in0=Ssb[:, PAST:NK],
                                                              scalar1=selneg[:, 256:257], scalar2=None, op0=ALU.add),
                             reads=[rsel], writes=[rS])
                        softmax_pv(ab, T_S, NK, Ssb, rS,
                                   lambda c: (vsg[:, c, :], r_grp, 128) if c < 128 else (vsg[:T_S, 128, :], r_grp, T_S),
                                   gated_out(1, hh))
                        Ssb, rS = ab.S.get()
                        ps, rp = psA.get()
                        S.mm_group([lambda pe, hh=hh: pe.matmul(ps[:T_S, :512], QTg[:, hh, :], kwTg[:, 0:512],
                                                                start=True, stop=True)], reads=[r_grp], writes=[rp])
                        S.op("dve", lambda v: v.scalar_tensor_tensor(
                            out=Ssb[:, 0:512], in0=ps[:T_S, :512], scalar=SCALE, in1=s_winm[:],
                            op0=ALU.mult, op1=ALU.add), reads=[rp, r_tab], writes=[rS])
                        ps, rp = psA.get()
                        S.mm_group([lambda pe, hh=hh: pe.matmul(ps[:T_S, :T_S], QTg[:, hh, :], kwTg[:, 512:520],
                                                                start=True, stop=True)], reads=[r_grp], writes=[rp])
                        S.op("dve", lambda v: v.scalar_tensor_tensor(
                            out=Ssb[:, 512:520], in0=ps[:T_S, :T_S], scalar=SCALE, in1=s_tri8[:],
                            op0=ALU.mult, op1=ALU.add), reads=[rp, r_tab], writes=[rS])
                        softmax_pv(ab, T_S, 520, Ssb, rS,
                                   lambda c: (vwg[:, c, :], r_grp, 128) if c < 4 else (vwg[:T_S, 4, :], r_grp, T_S),
                                   gated_out(2, hh))
                    S.op("act", lambda a_: a_.copy(ob[:], oc[:]), reads=[roc], writes=[roc])
                    S.dma("sp", tokb[SEQ:NTOK, g * 384:(g + 1) * 384], ob[:], reads=[roc], writes=[r_tokb])

    def ret_mix(layer):
        bl = layer // 2
        with Scope() as sc:
            S32 = sc.sb("S32", [128, 6, 2, 256], F32)
            Sb = sc.sb("Sb", [128, 6, 2, 256], BF16)
            r_S = [Res() for _ in range(6)]
            gn_bc = sc.sb("gn_bc", [128, 1536], F32)
            r_gn = Res()
            S.dma("sp", gn_bc[:], ret_gn[bl:bl + 1, :].to_broadcast([128, 1536]), writes=[r_gn])
            qc_pool = sc.pool("qc", 2, [128, 12, 128], BF16)
            kc_pool = sc.pool("kc", 2, [128, 12, 128], BF16)
            v_pool = sc.pool("vch", 2, [128, 1536], BF16)
            sg_pool = sc.pool("sgch", 2, [128, 1536], F32)
            qd_pool = sc.pool("qdT", 2, [128, 2, 128], BF16)
            in_pool = sc.pool("inT", 2, [128, 128], BF16)
            kd_pool = sc.pool("kd", 2, [128, 256], BF16)
            y_pool = sc.pool("yf", 2, [128, 256], F32)
            tok_pool = sc.pool("tokc", 2, [128, 1536], BF16)
            bn_pool = sc.pool("bn", 4, [128, 8], F32)
            for mode in ("prompt", "sample"):
                C = 128 if mode == "prompt" else T_S
                tag = "128" if mode == "prompt" else "8"
                dmT = sc.sb("dmT" + tag, [128, 6, C], F32)
                qdt = sc.sb("qd" + tag, [128, 6, C], F32)
                kdt = sc.sb("kd" + tag, [128, 6], F32)
                r_dt = Res()
                S.dma("sp", dmT[:], tabs["dmT" + tag].rearrange("h j i -> j h i"), writes=[r_dt])
                S.dma("sp", qdt[:], tabs["qd" + tag].rearrange("h j i -> j h i"), writes=[r_dt])
                S.dma("sp", kdt[:], tabs["kd" + tag], writes=[r_dt])
                if mode == "prompt":
                    for h in range(6):
                        S.op("pool", lambda g_, h=h: g_.memset(S32[:, h, :, :], 0.0), writes=[r_S[h]])
                        S.op("pool", lambda g_, h=h: g_.memset(Sb[:, h, :, :], 0.0), writes=[r_S[h]])
                    chunks = [(c * 128, 128) for c in range(32)]
                else:
                    for h in range(6):
                        S.dma("sp", S32[:, h, :, :],
                              ret_state[bl][h * 256:(h + 1) * 256, :].rearrange("(dc p) v -> p dc v", p=128),
                              writes=[r_S[h]])
                        S.op("act", lambda a_, h=h: a_.copy(Sb[:, h, :, :], S32[:, h, :, :]), reads=[r_S[h]],
                             writes=[r_S[h]])
                    chunks = [(SEQ, T_S)]
                for (r0, Cn) in chunks:
                    qc, rq = qc_pool.get()
                    kc, rk = kc_pool.get()
                    vch, rv = v_pool.get()
                    sg, rsg = sg_pool.get()
                    S.dma("sp", qc[:, :, :Cn], QT[:, :, r0:r0 + Cn].rearrange("j p n -> p j n"), reads=[r_QT],
                          writes=[rq])
                    S.dma("sp", kc[:, :, :Cn], KT12[:, :, r0:r0 + Cn].rearrange("j p n -> p j n"), reads=[r_KT12],
                          writes=[rk])
                    S.dma("sp", vch[:Cn, :], vtm[r0:r0 + Cn, :], reads=[r_vtm], writes=[rv])
                    S.dma("sp", sg[:Cn, :], sgate[r0:r0 + Cn, :], reads=[r_sgate], writes=[rsg])
                    tk, rtk = tok_pool.get()
                    for h in range(6):
                        ps, rp = psA.get()
                        S.mm_group([lambda pe, dc=dc, h=h: pe.matmul(ps[:Cn, :Cn], kc[:, 2 * h + dc, :Cn],
                                                                     qc[:, 2 * h + dc, :Cn], start=(dc == 0),
                                                                     stop=(dc == 1)) for dc in range(2)],
                                   reads=[rq, rk], writes=[rp])
                        inT, rin = in_pool.get()
                        S.op("dve", lambda v, h=h: v.tensor_tensor(out=inT[:Cn, :Cn], in0=ps[:Cn, :Cn],
                                                                   in1=dmT[:Cn, h, :Cn], op=ALU.mult),
                             reads=[rp, r_dt], writes=[rin])
                        qd, rqd = qd_pool.get()
                        S.op("pool", lambda g_, h=h: g_.tensor_tensor(
                            out=qd[:, :, :Cn], in0=qc[:, 2 * h:2 * h + 2, :Cn],
                            in1=qdt[:, h, :Cn].unsqueeze(1).to_broadcast([128, 2, Cn]), op=ALU.mult),
                            reads=[rq, r_dt], writes=[rqd])
                        po, rpo = psB.get()
                        fns = [lambda pe, h=h: pe.matmul(po[:Cn, :256], inT[:Cn, :Cn], vch[:Cn, h * 256:(h + 1) * 256],
                                                         start=True, stop=False)]
                        for dc in range(2):
                            fns.append(lambda pe, dc=dc, h=h: pe.matmul(po[:Cn, :256], qd[:, dc, :Cn], Sb[:, h, dc, :],
                                                                        start=False, stop=(dc == 1)))
                        S.mm_group(fns, reads=[rin, rv, rqd, r_S[h]], writes=[rpo])
                        pt, rpt = psT.get()
                        S.mm_group([lambda pe, dc=dc, h=h: pe.transpose(pt[:Cn, dc * 128:(dc + 1) * 128],
                                                                        kc[:, 2 * h + dc, :Cn], ident_b[:, :])
                                    for dc in range(2)], reads=[rk, r_ident], writes=[rpt])
                        kd, rkd = kd_pool.get()
                        S.op("dve", lambda v, h=h: v.tensor_scalar(out=kd[:Cn, :], in0=pt[:Cn, :256],
                                                                   scalar1=kdt[:Cn, h:h + 1], scalar2=None,
                                                                   op0=ALU.mult), reads=[rpt, r_dt], writes=[rkd])
                        for dc in range(2):
                            pss, rps = psA.get()
                            S.mm_group([lambda pe, dc=dc, h=h: pe.matmul(
                                pss[:, :256], kd[:Cn, dc * 128:(dc + 1) * 128], vch[:Cn, h * 256:(h + 1) * 256],
                                start=True, stop=True)], reads=[rkd, rv], writes=[rps])
                            S.op("dve", lambda v, dc=dc, h=h: v.scalar_tensor_tensor(
                                out=S32[:, h, dc, :], in0=S32[:, h, dc, :], scalar=cdec(h, C), in1=pss[:, :256],
                                op0=ALU.mult, op1=ALU.add), reads=[rps], writes=[r_S[h]])
                        S.op("act", lambda a_, h=h: a_.copy(Sb[:, h, :, :], S32[:, h, :, :]), reads=[],
                             writes=[r_S[h]])
                        bn, rbn = bn_pool.get()
                        S.op("dve", lambda v: v.bn_stats(out=bn[:Cn, 0:6], in_=po[:Cn, :256]), reads=[rpo],
                             writes=[rbn])
                        S.op("dve", lambda v: v.bn_aggr(out=bn[:Cn, 6:8], in_=bn[:Cn, 0:6]), reads=[rbn],
                             writes=[rbn])
                        S.op("act", lambda a_: a_.activation(out=bn[:Cn, 0:1], in_=bn[:Cn, 7:8], func=AF.Sqrt,
                                                             scale=1.0, bias=EPS), reads=[rbn], writes=[rbn])
                        S.op("dve", lambda v: v.reciprocal(bn[:Cn, 1:2], bn[:Cn, 0:1]), reads=[rbn], writes=[rbn])
                        yf, ry = y_pool.get()
                        S.op("dve", lambda v: v.tensor_scalar(out=yf[:Cn, :], in0=po[:Cn, :256], scalar1=bn[:Cn, 6:7],
                                                              scalar2=bn[:Cn, 1:2], op0=ALU.subtract, op1=ALU.mult),
                             reads=[rpo, rbn], writes=[ry])
                        S.op("pool", lambda g_, h=h: g_.tensor_tensor(out=yf[:Cn, :], in0=yf[:Cn, :],
                                                                      in1=gn_bc[:Cn, h * 256:(h + 1) * 256],
                                                                      op=ALU.mult), reads=[r_gn], writes=[ry])
                        S.op("pool", lambda g_, h=h: g_.tensor_tensor(out=tk[:Cn, h * 256:(h + 1) * 256],
                                                                      in0=yf[:Cn, :],
                                                                      in1=sg[:Cn, h * 256:(h + 1) * 256], op=ALU.mult),
                             reads=[ry, rsg], writes=[rtk])
                    S.dma("sp", tokb[r0:r0 + Cn, 0:1536], tk[:Cn, :], reads=[rtk], writes=[r_tokb])
                dst = o_ret_p[bl] if mode == "prompt" else o_ret_s[bl]
                for h in range(6):
                    S.dma("sp", dst[h * 256:(h + 1) * 256, :].rearrange("(dc p) v -> p dc v", p=128),
                          S32[:, h, :, :], reads=[r_S[h]])

    def mem_mix(layer, mk):
        with Scope() as sc:
            ab = AttnBufs(sc, 128, 256)
            qm_pool = sc.pool("qmt", 2, [128, 4, 128], BF16)
            om_pool = sc.pool("om", 2, [128, 512], BF16)
            tiles = [(t * 128, 128) for t in range(32)] + [(SEQ, T_S)]
            for (r0, nq) in tiles:
                samp = r0 >= SEQ
                KTt, rKT = (mk["sKT"], mk["r_sKT"]) if samp else (mk["pKT"], mk["r_pKT"])
                Vt, rV = (mk["sV"], mk["r_sV"]) if samp else (mk["pV"], mk["r_pV"])
                qm, rqm = qm_pool.get()
                S.dma("sp", qm[:, :, :nq], qmT[:, :, r0:r0 + nq].rearrange("h p n -> p h n"), reads=[r_qmT],
                      writes=[rqm])
                om, rom = om_pool.get()
                for h in range(4):
                    ps, rp = psA.get()
                    S.mm_group([lambda pe, h=h: pe.matmul(ps[:nq, :256], qm[:, h, :nq], KTt[:, h, :],
                                                          start=True, stop=True)], reads=[rqm, rKT], writes=[rp])
                    Ssb, rS = ab.S.get()
                    S.op("act", lambda a_: a_.activation(out=Ssb[:nq, :256], in_=ps[:nq, :256], func=AF.Identity,
                                                         scale=SCALE), reads=[rp], writes=[rS])

                    def out_fn(ps2, rp2, st, rst, h=h):
                        S.op("dve", lambda v: v.tensor_scalar(out=om[:nq, h * 128:(h + 1) * 128], in0=ps2[:nq, :128],
                                                              scalar1=st[:nq, 3:4], scalar2=None, op0=ALU.mult),
                             reads=[rp2, rst], writes=[rom])
                    softmax_pv(ab, nq, 256, Ssb, rS, lambda c, h=h: (Vt[:, c, h * 128:(h + 1) * 128], rV, 128), out_fn)
                S.dma("sp", tokb[r0:r0 + nq, 1536:2048], om[:nq, :], reads=[rom], writes=[r_tokb])

    def out_proj(layer):
        with Scope() as sc:
            Wo = sc.sb("Wo", [128, KC, D], BF16)
            r_Wo = Res()
            for cb in range(4):
                S.dma("act", Wo[:, :, cb * 512:(cb + 1) * 512],
                      wo_b[layer][:, cb * 512:(cb + 1) * 512].rearrange("(kc p) n -> p kc n", p=128), writes=[r_Wo])
            tk_pool = sc.pool("tkt", 2, [128, D], BF16)
            tT_pool = sc.pool("tokT", 2, [128, KC, 128], BF16)
            x_pool = sc.pool("xo", 2, [128, D], F32)
            tiles = [(t * 128, 128) for t in range(32)] + [(SEQ, T_S)]
            for (r0, P) in tiles:
                tk, rtk = tk_pool.get()
                S.dma("sp", tk[:P, :], tokb[r0:r0 + P, :], reads=[r_tokb], writes=[rtk])
                tT, rtT = tT_pool.get()
                transpose_into(tk, rtk, P, KC, lambda c0, n: tT[:, c0:c0 + n, :P], rtT)
                xt, rx = x_pool.get()
                S.dma("sp", xt[:P, :], x_rows(layer, r0, P), reads=[xres(r0)] if layer > 0 else [], writes=[rx])
                for cb in range(4):
                    ps, rp = psA.get()
                    S.mm_group([lambda pe, kc=kc, cb=cb: pe.matmul(ps[:P, :512], tT[:, kc, :P],
                                                                   Wo[:, kc, cb * 512:(cb + 1) * 512],
                                                                   start=(kc == 0), stop=(kc == KC - 1))
                                for kc in range(KC)], reads=[rtT, r_Wo], writes=[rp])
                    S.op("dve", lambda v, cb=cb: v.tensor_tensor(out=xt[:P, cb * 512:(cb + 1) * 512],
                                                                 in0=xt[:P, cb * 512:(cb + 1) * 512], in1=ps[:P, :512],
                                                                 op=ALU.add), reads=[rp], writes=[rx])
                S.dma("sp", xbuf[r0:r0 + P, :], xt[:P, :], reads=[rx], writes=[xres(r0)])
                if debug and layer == 0:
                    S.dma("sp", xmid_dbg[r0:r0 + P, :], xt[:P, :], reads=[rx])

    def ffn(layer, last):
        with Scope() as sc:
            nb = NormBufs(sc)
            hT = sc.sb("hT", [128, KC, 512], BF16)
            r_hT = Res()
            uT = sc.sb("uT", [128, FC, 512], BF16)
            r_uT = Res()
            w_pool = sc.pool("wbuf", 2, [128, KC, 512], BF16)
            wo_pool = sc.pool("wobuf", 3, [128, 11, 512], BF16)
            cp = sc.sb("convp", [128, 4, FC], F32)
            carry = sc.sb("carry", [128, FC, 2], F32)
            r_cp = Res()
            r_carry = Res()
            S.dma("sp", cp[:], convp[layer], writes=[r_cp])
            a_pool = sc.pool("abuf", 2, [128, 514], F32)
            acc_pool = sc.pool("acc", 2, [128, 512], F32)
            gt, gr = load_gain(nb, norm2_g[layer:layer + 1, :])
            if last:
                gtf, grf = load_gain(nb, final_g[0:1, :])
            Win = fin_b[layer]
            Wout = fout_b[layer]
            for (r0, n) in BLOCKS:
                samp = r0 >= SEQ
                if r0 == 0:
                    S.op("pool", lambda g_: g_.memset(carry[:], 0.0), writes=[r_carry])
                if samp:
                    S.dma("sp", o_conv_p[layer], carry[:], reads=[r_carry])
                    S.dma("sp", carry[:], conv_state[layer], writes=[r_carry])
                for t0 in range(0, n, 128):
                    P = min(128, n - t0)
                    norm_tile(nb, xbuf[r0 + t0:r0 + t0 + P, :], [xres(r0)], P, gt, gr, t0, hT, r_hT)
                for j0 in range(0, FC, 2):
                    wt, rw = w_pool.get()
                    load_w(None, Win, j0 * 128, 256, dst_col=0, tile=(wt, rw))
                    load_w(None, Win, FFN + j0 * 128, 256, dst_col=256, tile=(wt, rw))
                    for jj in range(2):
                        j = j0 + jj
                        pa, rpa = psA.get()
                        S.mm_group([lambda pe, kc=kc, jj=jj: pe.matmul(pa[:, :n], wt[:, kc, jj * 128:(jj + 1) * 128],
                                                                       hT[:, kc, :n], start=(kc == 0),
                                                                       stop=(kc == KC - 1)) for kc in range(KC)],
                                   reads=[r_hT, rw], writes=[rpa])
                        pg, rpg = psA.get()
                        S.mm_group([lambda pe, kc=kc, jj=jj: pe.matmul(pg[:, :n],
                                                                       wt[:, kc, 256 + jj * 128:256 + (jj + 1) * 128],
                                                                       hT[:, kc, :n], start=(kc == 0),
                                                                       stop=(kc == KC - 1)) for kc in range(KC)],
                                   reads=[r_hT, rw], writes=[rpg])
                        ab_, rab = a_pool.get()
                        S.op("act", lambda a_, j=j: a_.copy(ab_[:, 2:2 + n], pa[:, :n]), reads=[rpa], writes=[rab])
                        S.op("act", lambda g_, j=j: g_.copy(ab_[:, 0:2], carry[:, j, :]), reads=[r_carry],
                             writes=[rab])
                        acc, racc = acc_pool.get()
                        S.op("dve", lambda v, j=j: v.tensor_scalar(out=acc[:, :n], in0=ab_[:, 2:2 + n],
                                                                   scalar1=cp[:, 2, j:j + 1], scalar2=cp[:, 3, j:j + 1],
                                                                   op0=ALU.mult, op1=ALU.add),
                             reads=[rab, r_cp], writes=[racc])
                        S.op("dve", lambda v, j=j: v.scalar_tensor_tensor(out=acc[:, :n], in0=ab_[:, 1:1 + n],
                                                                          scalar=cp[:, 1, j:j + 1], in1=acc[:, :n],
                                                                          op0=ALU.mult, op1=ALU.add),
                             reads=[rab, r_cp], writes=[racc])
                        S.op("dve", lambda v, j=j: v.scalar_tensor_tensor(out=acc[:, :n], in0=ab_[:, 0:n],
                                                                          scalar=cp[:, 0, j:j + 1], in1=acc[:, :n],
                                                                          op0=ALU.mult, op1=ALU.add),
                             reads=[rab, r_cp], writes=[racc])
                        S.op("act", lambda g_, j=j: g_.copy(carry[:, j, :], ab_[:, n:n + 2]), reads=[rab],
                             writes=[r_carry])
                        S.op("act", lambda a_: a_.activation(out=acc[:, :n], in_=acc[:, :n], func=AF.Silu),
                             reads=[racc], writes=[racc])
                        S.op("dve", lambda v, j=j: v.tensor_tensor(out=uT[:, j, :n], in0=acc[:, :n], in1=pg[:, :n],
                                                                   op=ALU.mult), reads=[racc, rpg], writes=[r_uT])
                if samp:
                    S.dma("sp", o_conv_s[layer], carry[:], reads=[r_carry])
                xtiles = [(t0, min(128, n - t0)) for t0 in range(0, n, 128)]
                for cb in range(4):
                    pss = [psA.get() for _ in xtiles]
                    for q4 in range(4):
                        wt, rw = wo_pool.get()
                        src = Wout[q4 * 11 * 128:(q4 + 1) * 11 * 128, cb * 512:(cb + 1) * 512].rearrange(
                            "(kc p) n -> p kc n", p=128)
                        S.dma("act", wt[:], src, writes=[rw])
                        for ti, (t0, P) in enumerate(xtiles):
                            ps, rp = pss[ti]
                            S.mm_group([lambda pe, kc=kc, q4=q4, t0=t0, P=P, ps=ps, wt=wt: pe.matmul(
                                ps[:P, :512], uT[:, q4 * 11 + kc, t0:t0 + P], wt[:, kc, :],
                                start=(q4 == 0 and kc == 0), stop=(q4 == 3 and kc == 10)) for kc in range(11)],
                                reads=[r_uT, rw], writes=[rp])
                    for ti, (t0, P) in enumerate(xtiles):
                        ps, rp = pss[ti]
                        stg, rs = acc_pool.get()
                        S.dma("sp", stg[:P, :], xbuf[r0 + t0:r0 + t0 + P, cb * 512:(cb + 1) * 512],
                              reads=[xres(r0)], writes=[rs])
                        S.op("dve", lambda v, ps=ps, P=P, stg=stg: v.tensor_tensor(out=stg[:P, :], in0=stg[:P, :],
                                                                                 in1=ps[:P, :512], op=ALU.add),
                             reads=[rp], writes=[rs])
                        S.dma("sp", xbuf[r0 + t0:r0 + t0 + P, cb * 512:(cb + 1) * 512], stg[:P, :], reads=[rs],
                              writes=[xres(r0)])
                if last:
                    for t0 in range(0, n, 128):
                        P = min(128, n - t0)
                        xt, rx = nb.xt.get()
                        S.dma("sp", xt[:P, :], xbuf[r0 + t0:r0 + t0 + P, :], reads=[xres(r0)], writes=[rx])
                        st, rs = rstd_of(xt, rx, P, nb)
                        S.op("dve", lambda v: v.scalar_tensor_tensor(out=xt[:P, :], in0=xt[:P, :], scalar=st[:P, 2:3],
                                                                     in1=gtf[:P, :], op0=ALU.mult, op1=ALU.mult),
                             reads=[rs, grf], writes=[rx])
                        dst = o_y_s[:, :] if samp else o_y_p[r0 + t0:r0 + t0 + P, :]
                        S.dma("sp", dst, xt[:P, :], reads=[rx])

    mkp = {}
    mkp["pKT"] = gsb("m_pKT", [128, 4, 256], BF16)
    mkp["pV"] = gsb("m_pV", [128, 2, 512], BF16)
    mkp["sKT"] = gsb("m_sKT", [128, 4, 256], BF16)
    mkp["sV"] = gsb("m_sV", [128, 2, 512], BF16)
    for k_ in ("pKT", "pV", "sKT", "sV"):
        mkp["r_" + k_] = Res(multi=True)

    def precast():
        with Scope() as sc:
            fpool = sc.pool("pc_f", 2, [128, 2 * FFN], F32)
            bpool = sc.pool("pc_b", 2, [128, 2 * FFN], BF16)
            k_ = [0]
            for (src, dst) in ((nsa_w_in, nsa_w_b), (w_mem_kv, mem_w_b), (w_o, wo_b), (ffn_w_in, fin_b),
                               (ffn_w_out, fout_b), (ret_w_in, ret_w_b)):
                s2 = src.rearrange("l r c -> (l r) c")
                d2 = dst.rearrange("l r c -> (l r) c")
                R_, C_ = s2.shape[0], s2.shape[1]
                for r0 in range(0, R_, 128):
                    ft, rf = fpool.get()
                    bt, rb = bpool.get()
                    S.dma("sp", ft[:, :C_], s2[r0:r0 + 128, :], writes=[rf])
                    k_[0] ^= 1
                    copy_op("act" if k_[0] else "dve", bt[:, :C_], ft[:, :C_], [rf], [rb])
                    S.dma("pool", d2[r0:r0 + 128, :], bt[:, :C_], reads=[rb])

    precast()
    for layer in range(nlayers):
        if layer % 2 == 0:
            phaseA_nsa(layer, mkp)
            nsa_mix_prompt(layer)
            nsa_mix_sample(layer)
        else:
            phaseA_ret(layer, mkp)
            ret_mix(layer)
        mem_mix(layer, mkp)
        out_proj(layer)
        ffn(layer, layer == nlayers - 1)

    S.finish()
    return nc, S.n_inst


_CACHE = {}


def kernel(**inp):
    if "nc" not in _CACHE:
        _CACHE["nc"] = build_program()
    nc, _ = _CACHE["nc"]
    f = lambda a: np.ascontiguousarray(np.asarray(a, dtype=np.float32))
    x_prompt = f(inp["x_prompt"])
    x_sample = f(inp["x_sample"])
    mem_prompt = f(inp["mem_prompt"])
    state_nsa_win = f(inp["state_nsa_win"])
    state_ret = f(inp["state_ret"])
    state_ffn_conv = f(inp["state_ffn_conv"])
    cache_mem_kv = f(inp["cache_mem_kv"])
    page_table = np.ascontiguousarray(np.asarray(inp["page_table"], dtype=np.int32))
    conv_w = f(inp["ffn_conv_w"])
    conv_b = f(inp["ffn_conv_b"])
    cpar = np.concatenate([conv_w, conv_b[:, None, :]], axis=1)
    cpar = np.ascontiguousarray(cpar.reshape(DEPTH, 4, FC, 128).transpose(0, 3, 1, 2))
    peT = np.ascontiguousarray(f(inp["nsa_cmp_pe"]).transpose(0, 1, 3, 2))
    shared = {
        "cache": f(inp["cache_nsa_kv"]).reshape(2, 1280 * 128, 2048),
        "norm1_g": f(inp["norm1_g"]), "norm2_g": f(inp["norm2_g"]), "mem_norm_g": f(inp["mem_norm_g"]),
        "final_g": f(inp["final_norm_g"]).reshape(1, D),
        "nsa_w_in": f(inp["nsa_w_in"]), "ret_w_in": f(inp["ret_w_in"]), "ret_gn": f(inp["ret_gn_g"]),
        "w_mem_kv": f(inp["w_mem_kv"]), "w_o": f(inp["w_o"]),
        "ffn_w_in": f(inp["ffn_w_in"]), "ffn_w_out": f(inp["ffn_w_out"]),
        "convp": cpar, "cmp_peT": peT, "cmp_w1": f(inp["nsa_cmp_w1"]), "cmp_w2": f(inp["nsa_cmp_w2"]),
    }
    for k_, v_ in make_tables().items():
        shared["t_" + k_] = v_
    in_maps = []
    for c in range(8):
        b = c // 4
        m = dict(shared)
        m["xp"] = x_prompt[b]
        m["xs"] = x_sample[c]
        m["memp"] = mem_prompt[b]
        m["win_state"] = np.ascontiguousarray(state_nsa_win[:, c].reshape(2, 512, 1024))
        m["ret_state"] = np.ascontiguousarray(state_ret[:, c].reshape(2, 1536, 256))
        m["conv_state"] = np.ascontiguousarray(
            state_ffn_conv[:, c].reshape(DEPTH, 2, FC, 128).transpose(0, 3, 2, 1))
        m["mem_cache"] = np.ascontiguousarray(cache_mem_kv[:, c].reshape(DEPTH, 256, 1024))
        m["page_tab"] = page_table[c:c + 1]
        in_maps.append(m)
    res = run_bass_kernel_spmd(nc, in_maps, core_ids=list(range(8)))
    R = res.results
    _CACHE["raw"] = R
    pc = [0, 4]
    y_prompt = np.stack([R[c]["o_y_p"] for c in pc])
    y_sample = np.stack([R[c]["o_y_s"] for c in range(8)])
    kv_p = np.stack([R[c]["o_kv_p"] for c in pc], axis=1).reshape(2, 2, SEQ, 4, 4, 128)
    kv_s = np.stack([R[c]["o_kv_s"] for c in range(8)], axis=1).reshape(2, 8, T_S, 4, 4, 128)
    win_p = np.stack([R[c]["o_win_p"] for c in pc], axis=1).reshape(2, 2, 512, 2, 4, 128)
    win_s = np.stack([R[c]["o_win_s"] for c in range(8)], axis=1).reshape(2, 8, 512, 2, 4, 128)
    ret_p = np.stack([R[c]["o_ret_p"] for c in pc], axis=1).reshape(2, 2, 6, 256, 256)
    ret_s = np.stack([R[c]["o_ret_s"] for c in range(8)], axis=1).reshape(2, 8, 6, 256, 256)

    def conv_out(a):
        return np.ascontiguousarray(a.transpose(0, 3, 2, 1).reshape(DEPTH, 2, FFN))
    conv_p = np.stack([conv_out(R[c]["o_conv_p"]) for c in pc], axis=1)
    conv_s = np.stack([conv_out(R[c]["o_conv_s"]) for c in range(8)], axis=1)
    mem_p = np.stack([R[c]["o_mem_p"] for c in pc], axis=1).reshape(DEPTH, 2, 256, 2, 4, 128)
    return (y_prompt, y_sample, kv_p, kv_s, win_p, win_s, ret_p, ret_s, conv_p, conv_s, mem_p)
```

```python
from contextlib import ExitStack
import numpy as np
import concourse.bass as bass
import concourse.mybir as mybir
from concourse.bass_utils import run_bass_kernel_spmd

F32 = mybir.dt.float32
BF16 = mybir.dt.bfloat16
I32 = mybir.dt.int32
AF = mybir.ActivationFunctionType
ALU = mybir.AluOpType
AX = mybir.AxisListType

D = 2048
SEQ = 4096
DEPTH = 4
T_S = 8
NTOK = SEQ + T_S
KC = 16
NSA_IN = 5156
RET_IN = 6656
FFN = 5632
FC = FFN // 128
EPS = 1e-6
NDMA = 24
NEG = -30000.0
PAST = 16384
NPAGE = 128
SCALE = 128 ** -0.5
LAYERS = list(range(DEPTH))


class Res:
    __slots__ = ("w", "r", "multi")

    def __init__(self, multi=False):
        self.w = {}
        self.r = {}
        self.multi = multi


class Sched:
    def __init__(self, nc):
        self.nc = nc
        self.eng = {"pe": nc.tensor, "act": nc.scalar, "dve": nc.vector,
                    "pool": nc.gpsimd, "sp": nc.sync}
        self.sem, self.cnt, self.known = {}, {}, {}
        for k in self.eng:
            self.sem[k] = nc.alloc_semaphore(name="sem_" + k)
            self.cnt[k] = 0
            self.known[k] = {}
        for i in range(NDMA):
            k = "d%d" % i
            self.sem[k] = nc.alloc_semaphore(name="sem_" + k)
            self.cnt[k] = 0
        self.rr = 0
        self.n_inst = 0
        self.qmap = {}

    def _wait(self, e, s, v):
        if s == "pe" and e == "pe":
            return
        if self.known[e].get(s, 0) >= v:
            return
        self.known[e][s] = v
        self.eng[e].wait_ge(self.sem[s], v)
        self.n_inst += 1

    def _deps(self, e, reads, writes):
        for r in reads:
            for s, v in r.w.items():
                self._wait(e, s, v)
        for w in writes:
            for s, v in w.r.items():
                self._wait(e, s, v)
            if not w.multi:
                for s, v in w.w.items():
                    self._wait(e, s, v)

    def _record(self, s, v, reads, writes):
        for r in reads:
            if r.r.get(s, 0) < v:
                r.r[s] = v
        for w in writes:
            if w.multi:
                if w.w.get(s, 0) < v:
                    w.w[s] = v
            else:
                w.w = {s: v}
            w.r = {}

    def op(self, e, fn, reads=(), writes=()):
        self._deps(e, reads, writes)
        inst = fn(self.eng[e])
        self.cnt[e] += 1
        inst.then_inc(self.sem[e], 1)
        self.n_inst += 1
        self._record(e, self.cnt[e], reads, writes)

    def mm_group(self, fns, reads=(), writes=()):
        self._deps("pe", reads, writes)
        inst = None
        for fn in fns:
            inst = fn(self.eng["pe"])
            self.n_inst += 1
        self.cnt["pe"] += 1
        inst.then_inc(self.sem["pe"], 1)
        self._record("pe", self.cnt["pe"], reads, writes)

    def dma(self, q, out, in_, reads=(), writes=(), fn=None):
        q = self.qmap.get(q, q)
        self._deps(q, reads, writes)
        i = self.rr
        self.rr = (i + 1) % NDMA
        k = "d%d" % i
        if self.cnt[k] > 0:
            self._wait(q, k, 16 * self.cnt[k])
        self.cnt[k] += 1
        if fn is None:
            inst = self.eng[q].dma_start(out=out, in_=in_)
        else:
            inst = fn(self.eng[q])
        inst.then_inc(self.sem[k], 16)
        self.n_inst += 1
        self._record(k, 16 * self.cnt[k], reads, writes)

    def barrier(self):
        for e in ("pe", "act", "dve", "pool", "sp"):
            for k, c in self.cnt.items():
                if c > 0 and k != e:
                    self._wait(e, k, 16 * c if k[1:].isdigit() else c)

    def finish(self):
        for i in range(NDMA):
            k = "d%d" % i
            if self.cnt[k] > 0:
                self._wait("sp", k, 16 * self.cnt[k])
        for e in ("pe", "act", "dve", "pool"):
            if self.cnt[e] > 0:
                self._wait("sp", e, self.cnt[e])


class Pool:
    def __init__(self, tiles):
        self.tiles = [(t, Res()) for t in tiles]
        self.i = 0

    def get(self):
        t = self.tiles[self.i]
        self.i = (self.i + 1) % len(self.tiles)
        return t


def gamma(h):
    return 1.0 - 2.0 ** (-5.0 - h)


def make_tables():
    T = {}
    T["ident"] = np.eye(128, dtype=np.float32)
    q = np.arange(SEQ)[:, None]
    n = np.arange(128)[None, :]
    T["cmpmask"] = np.where(n * 32 + 31 <= q, 0.0, NEG).astype(np.float32)
    blk = np.arange(64)[None, :]
    cur = q // 64
    valid = blk * 64 <= q
    forced = (blk == 0) | (blk == cur) | (blk == cur - 1)
    T["bonus"] = np.where(valid, np.where(forced, 1.0e4, 0.0), -1.0e30).astype(np.float32)
    p = np.arange(128)[:, None]
    k = np.arange(128)[None, :]
    T["tri_le"] = np.where(k <= p, 0.0, NEG).astype(np.float32)
    T["tri_gt"] = np.where(k > p, 0.0, NEG).astype(np.float32)
    sb = np.zeros((128, 264), np.float32)
    sb[:, [0, 255, 256]] = 1.0e4
    sb[:, 257:] = -1.0e30
    T["s_bonus"] = sb
    t = np.arange(128)[:, None]
    j = np.arange(8)[None, :]
    T["s_tri8"] = np.where(j <= t, 0.0, NEG).astype(np.float32)
    r = np.arange(512)[None, :]
    T["s_winmask"] = np.where(r > t, 0.0, NEG).astype(np.float32)
    pos = np.concatenate([np.arange(SEQ), PAST + np.arange(T_S)]).astype(np.float32)
    inv = (10000.0 ** (-np.arange(128, dtype=np.float32) / 128.0)).astype(np.float32)
    ang = (pos[None, :] * inv[:, None]).astype(np.float32)
    T["cosT"] = np.cos(ang).astype(np.float32)
    T["sinT"] = np.sin(ang).astype(np.float32)
    for C, tag in ((128, "128"), (8, "8")):
        i = np.arange(C, dtype=np.float64)
        dm = np.zeros((6, 128, C), np.float32)
        qd = np.zeros((6, 128, C), np.float32)
        kd = np.zeros((128, 6), np.float32)
        for h in range(6):
            lg = np.log1p(-2.0 ** (-5.0 - h))
            diff = i[None, :] - i[:, None]
            dm[h, :C, :] = np.where(diff >= 0, np.exp(np.maximum(diff, 0.0) * lg), 0.0)
            qd[h, :, :] = np.exp((i + 1.0) * lg)[None, :]
            kd[:C, h] = np.exp((C - 1.0 - i) * lg)
        T["dmT" + tag] = dm
        T["qd" + tag] = qd
        T["kd" + tag] = kd
    return T


def cdec(h, C):
    return float(np.exp(C * np.log1p(-2.0 ** (-5.0 - h))))


def build_program(nlayers=DEPTH, debug=False):
    nc = bass.Bass("TRN2", target_bir_lowering=False)
    S = Sched(nc)
    uid = [0]

    def nm(p):
        uid[0] += 1
        return "%s_%d" % (p, uid[0])

    def din(name, shape, dt=F32):
        return nc.dram_tensor(name, list(shape), dt, kind="ExternalInput").ap()

    def dout(name, shape, dt=F32):
        return nc.dram_tensor(name, list(shape), dt, kind="ExternalOutput").ap()

    def dscr(name, shape, dt):
        kind = "ExternalOutput" if (debug and name in ("xbuf", "tokb", "xmid_dbg")) else "Internal"
        return nc.dram_tensor(name, list(shape), dt, kind=kind).ap()

    xp = din("xp", [SEQ, D])
    xs = din("xs", [T_S, D])
    memp = din("memp", [256, D])
    win_state = din("win_state", [2, 512, 1024])
    ret_state = din("ret_state", [2, 1536, 256])
    conv_state = din("conv_state", [DEPTH, 128, FC, 2])
    mem_cache = din("mem_cache", [DEPTH, 256, 1024])
    cache = din("cache", [2, 1280 * 128, 2048])
    page_tab = din("page_tab", [1, NPAGE], I32)
    norm1_g = din("norm1_g", [DEPTH, D])
    norm2_g = din("norm2_g", [DEPTH, D])
    mem_norm_g = din("mem_norm_g", [DEPTH, D])
    final_g = din("final_g", [1, D])
    nsa_w_in = din("nsa_w_in", [2, D, NSA_IN])
    ret_w_in = din("ret_w_in", [2, D, RET_IN])
    ret_gn = din("ret_gn", [2, 1536])
    w_mem_kv = din("w_mem_kv", [DEPTH, D, 1024])
    w_o = din("w_o", [DEPTH, D, D])
    ffn_w_in = din("ffn_w_in", [DEPTH, D, 2 * FFN])
    ffn_w_out = din("ffn_w_out", [DEPTH, FFN, D])
    convp = din("convp", [DEPTH, 128, 4, FC])
    cmp_peT = din("cmp_peT", [2, 2, 128, 32])
    cmp_w1 = din("cmp_w1", [2, 2, 4096, 128])
    cmp_w2 = din("cmp_w2", [2, 2, 128, 128])
    tabs = {}
    TS = make_tables()
    for k_, v_ in TS.items():
        tabs[k_] = din("t_" + k_, list(v_.shape))

    o_y_p = dout("o_y_p", [SEQ, D])
    o_y_s = dout("o_y_s", [T_S, D])
    o_kv_p = dout("o_kv_p", [2, SEQ, 2048])
    o_kv_s = dout("o_kv_s", [2, T_S, 2048])
    o_win_p = dout("o_win_p", [2, 512, 1024])
    o_win_s = dout("o_win_s", [2, 512, 1024])
    o_ret_p = dout("o_ret_p", [2, 1536, 256])
    o_ret_s = dout("o_ret_s", [2, 1536, 256])
    o_conv_p = dout("o_conv_p", [DEPTH, 128, FC, 2])
    o_conv_s = dout("o_conv_s", [DEPTH, 128, FC, 2])
    o_mem_p = dout("o_mem_p", [DEPTH, 256, 1024])

    xbuf = dscr("xbuf", [NTOK, D], F32)
    r_xb = [Res(multi=True) for _ in range(9)]

    def xres(r0):
        return r_xb[min(r0 // 512, 8)]
    xmid_dbg = dscr("xmid_dbg", [NTOK, D], F32) if debug else None
    tokb = dscr("tokb", [NTOK, D], BF16)
    r_tokb = Res(multi=True)
    QT = dscr("QT", [12, 128, NTOK], BF16)
    r_QT = Res(multi=True)
    KT12 = dscr("KT12", [12, 128, NTOK], BF16)
    r_KT12 = Res(multi=True)
    rcT = dscr("rcT", [8, 128, SEQ], BF16)
    r_rcT = Res(multi=True)
    ksT = dscr("ksT", [4, 128, NTOK], BF16)
    r_ksT = Res(multi=True)
    kwT = dscr("kwT", [4, 128, NTOK], BF16)
    r_kwT = Res(multi=True)
    qmT = dscr("qmT", [4, 128, NTOK], BF16)
    r_qmT = Res(multi=True)
    gates = dscr("gates", [NTOK, 36], F32)
    r_gates = Res(multi=True)
    winr = dscr("winr", [NTOK, 1024], F32)
    r_winr = Res(multi=True)
    vtm = dscr("vtm", [NTOK, 1536], BF16)
    r_vtm = Res(multi=True)
    sgate = dscr("sgate", [NTOK, 1536], F32)
    r_sgate = Res(multi=True)
    r_okv = Res(multi=True)
    ksT_s = dscr("ksT_s", [4, 128, PAST], BF16)
    r_ksT_s = Res(multi=True)
    vs_s = dscr("vs_s", [4, PAST, 128], BF16)
    r_vs_s = Res(multi=True)

    nsa_w_b = dscr("nsa_w_b", [2, D, NSA_IN], BF16)
    ret_w_b = dscr("ret_w_b", [2, D, RET_IN], BF16)
    mem_w_b = dscr("mem_w_b", [DEPTH, D, 1024], BF16)
    wo_b = dscr("wo_b", [DEPTH, D, D], BF16)
    fin_b = dscr("fin_b", [DEPTH, D, 2 * FFN], BF16)
    fout_b = dscr("fout_b", [DEPTH, FFN, D], BF16)

    def gsb(name, shape, dt):
        return nc.alloc_sbuf_tensor(name, list(shape), dt)

    ident_f = gsb("ident_f", [128, 128], F32)
    ident_b = gsb("ident_b", [128, 128], BF16)
    r_ident = Res()
    psA = Pool([nc.alloc_psum_tensor("psA%d" % i, [128, 512], F32) for i in range(4)])
    psB = Pool([nc.alloc_psum_tensor("psB%d" % i, [128, 512], F32) for i in range(2)])
    psT = Pool([nc.alloc_psum_tensor("psT%d" % i, [128, 1024], BF16) for i in range(2)])
    st_pool = Pool([gsb("stat%d" % i, [128, 8], F32) for i in range(6)])

    evac_rr = [0]

    def evac_engine():
        evac_rr[0] ^= 1
        return "act" if evac_rr[0] else "dve"

    def copy_op(e, out, in_, reads, writes):
        if e == "act":
            S.op("act", lambda a: a.copy(out, in_), reads, writes)
        else:
            S.op(e, lambda v: v.tensor_copy(out, in_), reads, writes)

    S.dma("sp", ident_f[:], tabs["ident"], writes=[r_ident])
    S.op("dve", lambda v: v.tensor_copy(ident_b[:], ident_f[:]), reads=[r_ident], writes=[r_ident])

    class Scope:
        def __init__(self):
            self.es = ExitStack()

        def __enter__(self):
            self.es.__enter__()
            return self

        def __exit__(self, *a):
            if a[0] is None:
                S.barrier()
            return self.es.__exit__(*a)

        def sb(self, name, shape, dt):
            return self.es.enter_context(nc.sbuf_tensor(nm(name), list(shape), dt))

        def pool(self, name, n, shape, dt):
            return Pool([self.sb(name, shape, dt) for _ in range(n)])

    BLOCKS = [(b * 512, 512) for b in range(SEQ // 512)] + [(SEQ, T_S)]

    def x_rows(layer, r0, n):
        if layer == 0:
            return xp[r0:r0 + n, :] if r0 < SEQ else xs[r0 - SEQ:r0 - SEQ + n, :]
        return xbuf[r0:r0 + n, :]

    class NormBufs:
        def __init__(self, sc):
            self.g_bc = sc.pool("g_bc", 2, [128, D], F32)
            self.xt = sc.pool("xt", 2, [128, D], F32)
            self.hb = sc.pool("hb", 2, [128, D], BF16)
            self.junk = sc.sb("junk", [128, D], BF16)
            self.r_junk = Res()

    def load_gain(nb, g_row_ap):
        t, r = nb.g_bc.get()
        S.dma("sp", t[:], g_row_ap.to_broadcast([128, D]), writes=[r])
        return t, r

    def rstd_of(xt, rx, P, nb):
        st, rs = st_pool.get()
        S.op("act", lambda a: a.activation(out=nb.junk[:P, :], in_=xt[:P, :], func=AF.Square,
                                           accum_out=st[:P, 0:1]),
             reads=[rx], writes=[nb.r_junk, rs])
        S.op("act", lambda a: a.activation(out=st[:P, 1:2], in_=st[:P, 0:1], func=AF.Sqrt,
                                           scale=1.0 / D, bias=EPS),
             reads=[rs], writes=[rs])
        S.op("dve", lambda v: v.reciprocal(st[:P, 2:3], st[:P, 1:2]), reads=[rs], writes=[rs])
        return st, rs

    def transpose_into(src_bf, rsrc, P, nchunks, dst_fn, rdst):
        for c0 in range(0, nchunks, 8):
            n = min(8, nchunks - c0)
            pt, rp = psT.get()
            fns = []
            for j in range(n):
                fns.append(lambda pe, j=j: pe.transpose(
                    pt[:, j * 128:j * 128 + P], src_bf[:P, (c0 + j) * 128:(c0 + j + 1) * 128],
                    ident_b[:P, :P]))
            S.mm_group(fns, reads=[rsrc, r_ident], writes=[rp])
            src = pt[:].rearrange("p (j q) -> p j q", q=128)[:, :n, :P]
            copy_op(evac_engine(), dst_fn(c0, n), src, [rp], [rdst])

    def norm_tile(nb, x_src_ap, xsrc_res, P, gt, gr, col0, hT_t, r_hT_t):
        xt, rx = nb.xt.get()
        S.dma("sp", xt[:P, :], x_src_ap, reads=xsrc_res, writes=[rx])
        st, rs = rstd_of(xt, rx, P, nb)
        hb, rh = nb.hb.get()
        S.op("dve", lambda v: v.scalar_tensor_tensor(out=hb[:P, :], in0=xt[:P, :], scalar=st[:P, 2:3],
                                                     in1=gt[:P, :], op0=ALU.mult, op1=ALU.mult),
             reads=[rx, rs, gr], writes=[rh])
        transpose_into(hb, rh, P, KC, lambda c0, n: hT_t[:, c0:c0 + n, col0:col0 + P], r_hT_t)

    def load_w(w_pool, W2d, col0, ncols, dst_col=0, tile=None):
        if tile is None:
            wt, rw = w_pool.get()
        else:
            wt, rw = tile
        src = W2d[:, col0:col0 + ncols].rearrange("(kc p) n -> p kc n", p=128)
        S.dma("act", wt[:, :, dst_col:dst_col + ncols], src, writes=[rw])
        return wt, rw

    def linear_tm(w_pool, hT_t, r_hT_t, n, W2d, col0, ncols, sink):
        for cb0 in range(0, ncols, 512):
            cw = min(512, ncols - cb0)
            wt, rw = load_w(w_pool, W2d, col0 + cb0, cw)
            for t0 in range(0, n, 128):
                P = min(128, n - t0)
                ps, rp = psA.get()
                fns = []
                for kc in range(KC):
                    fns.append(lambda pe, kc=kc: pe.matmul(
                        ps[:P, :cw], hT_t[:, kc, t0:t0 + P], wt[:, kc, :cw],
                        start=(kc == 0), stop=(kc == KC - 1)))
                S.mm_group(fns, reads=[r_hT_t, rw], writes=[rp])
                sink(t0, P, cb0, cw, ps, rp)

    def linear_fm(w_pool, hT_t, r_hT_t, n, W2d, col0, nchunks, sink):
        for j0 in range(0, nchunks, 4):
            nj = min(4, nchunks - j0)
            wt, rw = load_w(w_pool, W2d, col0 + j0 * 128, nj * 128)
            for jj in range(nj):
                ps, rp = psA.get()
                fns = []
                for kc in range(KC):
                    fns.append(lambda pe, kc=kc: pe.matmul(
                        ps[:, :n], wt[:, kc, jj * 128:(jj + 1) * 128], hT_t[:, kc, :n],
                        start=(kc == 0), stop=(kc == KC - 1)))
                S.mm_group(fns, reads=[r_hT_t, rw], writes=[rp])
                sink(j0 + jj, ps, rp)

    def tm_sink_dram(stage_pool, dst2d, rdst, func=None, dt_stage=F32):
        def f(t0, P, cb0, cw, ps, rp):
            stg, rs = stage_pool.get()
            if func is None:
                copy_op(evac_engine(), stg[:P, :cw], ps[:P, :cw], [rp], [rs])
            else:
                S.op("act", lambda a: a.activation(out=stg[:P, :cw], in_=ps[:P, :cw], func=func),
                     reads=[rp], writes=[rs])
            S.dma("sp", dst2d[t0:t0 + P, cb0:cb0 + cw], stg[:P, :cw], reads=[rs], writes=rdst)
        return f

    def fm_sink_dram(stage_pool, dst3d, rdst, r0, n, add_tab=None):
        def f(j, ps, rp):
            stg, rs = stage_pool.get()
            if add_tab is None:
                copy_op(evac_engine(), stg[:, :n], ps[:, :n], [rp], [rs])
            else:
                tab, rt = add_tab(j)
                S.op("dve", lambda v: v.tensor_tensor(
                    out=stg[:, :n].rearrange("p (b j) -> p b j", j=32),
                    in0=ps[:, :n].rearrange("p (b j) -> p b j", j=32),
                    in1=tab.unsqueeze(1).to_broadcast([128, n // 32, 32]), op=ALU.add),
                    reads=[rp, rt], writes=[rs])
            S.dma("sp", dst3d[j, :, r0:r0 + n], stg[:, :n], reads=[rs], writes=rdst)
        return f

    def softmax_pv(ab, nq, nk, S_sb, rS, vchunk, out_fn):
        st, rst = st_pool.get()
        S.op("dve", lambda v: v.reduce_max(out=st[:nq, 0:1], in_=S_sb[:nq, :nk], axis=AX.X),
             reads=[rS], writes=[rst])
        S.op("dve", lambda v: v.tensor_scalar(out=st[:nq, 1:2], in0=st[:nq, 0:1], scalar1=-1.0e4,
                                              scalar2=-1.0, op0=ALU.max, op1=ALU.mult),
             reads=[rst], writes=[rst])
        P, rP = ab.P.get()
        S.op("act", lambda a: a.activation(out=P[:nq, :nk], in_=S_sb[:nq, :nk], func=AF.Exp,
                                           bias=st[:nq, 1:2], scale=1.0, accum_out=st[:nq, 2:3]),
             reads=[rS, rst], writes=[rP, rst])
        S.op("dve", lambda v: v.tensor_scalar(out=st[:nq, 3:4], in0=st[:nq, 2:3], scalar1=1.0e-30,
                                              scalar2=None, op0=ALU.max),
             reads=[rst], writes=[rst])
        S.op("dve", lambda v: v.reciprocal(st[:nq, 3:4], st[:nq, 3:4]), reads=[rst], writes=[rst])
        nch = (nk + 127) // 128
        PT, rPT = ab.PT.get()
        per_bank = 1024 // nq
        for c0 in range(0, nch, per_bank):
            n = min(per_bank, nch - c0)
            pt, rp = psT.get()
            fns = []
            for j in range(n):
                c = c0 + j
                kk = min(128, nk - c * 128)
                fns.append(lambda pe, j=j, c=c, kk=kk: pe.transpose(
                    pt[:kk, j * nq:(j + 1) * nq], P[:nq, c * 128:c * 128 + kk], ident_b[:nq, :nq]))
            S.mm_group(fns, reads=[rP, r_ident], writes=[rp])
            copy_op(evac_engine(), PT[:, c0 * nq:(c0 + n) * nq], pt[:, :n * nq], [rp], [rPT])
        ps, rp2 = psB.get()
        fns = []
        vres = []
        for c in range(nch):
            vap, vr, kk = vchunk(c)
            if vr not in vres:
                vres.append(vr)
            fns.append(lambda pe, c=c, vap=vap, kk=kk: pe.matmul(
                ps[:nq, :128], PT[:kk, c * nq:(c + 1) * nq], vap,
                start=(c == 0), stop=(c == nch - 1)))
        S.mm_group(fns, reads=[rPT] + vres, writes=[rp2])
        out_fn(ps, rp2, st, rst)

    class AttnBufs:
        def __init__(self, sc, nq, nkmax):
            self.S = sc.pool("S_sb", 2 if nq > 8 else 1, [nq, nkmax], F32)
            self.P = sc.pool("P_bf", 2 if nq > 8 else 1, [nq, nkmax], BF16)
            nch = (nkmax + 127) // 128
            self.PT = sc.pool("PT", 2 if nq > 8 else 1, [128, nch * nq], BF16)

    def topk_selneg(sc_pool, score, rscore, nq, nblk, selneg, rsel):
        st, rst = st_pool.get()
        m8, rm8 = sc_pool.get()
        S.op("dve", lambda v: v.max(out=m8[:nq, 0:8], in_=score[:nq, :nblk]),
             reads=[rscore], writes=[rm8])
        S.op("dve", lambda v: v.match_replace(out=m8[:nq, 16:16 + nblk], in_to_replace=m8[:nq, 0:8],
                                              in_values=score[:nq, :nblk], imm_value=-3.0e38),
             reads=[rscore, rm8], writes=[rm8])
        S.op("dve", lambda v: v.max(out=m8[:nq, 8:16], in_=m8[:nq, 16:16 + nblk]),
             reads=[rm8], writes=[rm8])
        S.op("dve", lambda v: v.tensor_scalar(out=selneg[:nq, :nblk], in0=score[:nq, :nblk],
                                              scalar1=m8[:nq, 15:16], scalar2=None, op0=ALU.is_ge),
             reads=[rscore, rm8], writes=[rsel])
        S.op("dve", lambda v: v.tensor_scalar(out=selneg[:nq, :nblk], in0=selneg[:nq, :nblk],
                                              scalar1=-1.0, scalar2=-NEG, op0=ALU.add, op1=ALU.mult),
             reads=[rsel], writes=[rsel])

    def mem_prepare(layer, mk, w_pool, nb, stage_pool, hT, r_hT):
        gt, gr = load_gain(nb, mem_norm_g[layer:layer + 1, :])
        for t in range(2):
            norm_tile(nb, memp[t * 128:(t + 1) * 128, :], [], 128, gt, gr, t * 128, hT, r_hT)

        def sink_tm(t0, P, cb0, cw, ps, rp):
            stg, rs = stage_pool.get()
            copy_op(evac_engine(), stg[:P, :cw], ps[:P, :cw], [rp], [rs])
            S.dma("sp", o_mem_p[layer][t0:t0 + P, cb0:cb0 + cw], stg[:P, :cw], reads=[rs])
            if cb0 == 512:
                S.op("pool", lambda g: g.tensor_copy(mk["pV"][:, t0 // 128, :], stg[:, :512]),
                     reads=[rs], writes=[mk["r_pV"]])
        linear_tm(w_pool, hT, r_hT, 256, mem_w_b[layer], 0, 1024, sink_tm)

        def sink_fm(j, ps, rp):
            copy_op(evac_engine(), mk["pKT"][:, j, :], ps[:, :256], [rp], [mk["r_pKT"]])
        linear_fm(w_pool, hT, r_hT, 256, mem_w_b[layer], 0, 4, sink_fm)
        for t in range(2):
            xt, rx = nb.xt.get()
            S.dma("sp", xt[:, :1024], mem_cache[layer][t * 128:(t + 1) * 128, :], writes=[rx])
            S.op("pool", lambda g: g.tensor_copy(mk["sV"][:, t, :], xt[:, 512:1024]),
                 reads=[rx], writes=[mk["r_sV"]])
            hb, rh = nb.hb.get()
            S.op("dve", lambda v: v.tensor_copy(hb[:, :512], xt[:, :512]), reads=[rx], writes=[rh])
            transpose_into(hb, rh, 128, 4,
                           lambda c0, n: mk["sKT"][:, c0:c0 + n, t * 128:(t + 1) * 128], mk["r_sKT"])

    WQ = {"act": "sp", "sp": "pool"}

    def phaseA_nsa(layer, mk):
        S.qmap = WQ
        try:
            _phaseA_nsa(layer, mk)
        finally:
            S.qmap = {}

    def phaseA_ret(layer, mk):
        S.qmap = WQ
        try:
            _phaseA_ret(layer, mk)
        finally:
            S.qmap = {}

    def ffn(layer, last):
        S.qmap = WQ
        try:
            _ffn(layer, last)
        finally:
            S.qmap = {}

    def _phaseA_nsa(layer, mk):
        a = layer // 2
        W = nsa_w_b[a]
        with Scope() as sc:
            nb = NormBufs(sc)
            hT = sc.sb("hT", [128, KC, 512], BF16)
            r_hT = Res()
            w_pool = sc.pool("wbuf", 3, [128, KC, 512], BF16)
            stage = sc.pool("stage", 4, [128, 512], F32)
            stage_b = sc.pool("stageb", 4, [128, 512], BF16)
            peT = sc.sb("peT", [128, 2, 32], F32)
            r_peT = Res()
            S.dma("sp", peT[:], cmp_peT[a].rearrange("t d j -> d t j"), writes=[r_peT])
            mem_prepare(layer, mk, w_pool, nb, stage, hT, r_hT)
            gt, gr = load_gain(nb, norm1_g[layer:layer + 1, :])
            for (r0, n) in BLOCKS:
                samp = r0 >= SEQ
                for t0 in range(0, n, 128):
                    P = min(128, n - t0)
                    norm_tile(nb, x_rows(layer, r0 + t0, P), [xres(r0)] if layer > 0 else [], P, gt, gr,
                              t0, hT, r_hT)
                okv = o_kv_s[a] if samp else o_kv_p[a][r0:r0 + n, :]
                linear_tm(w_pool, hT, r_hT, n, W, 1536, 2048, tm_sink_dram(stage, okv, [r_okv]))
                linear_tm(w_pool, hT, r_hT, n, W, 3584, 1024,
                          tm_sink_dram(stage, winr[r0:r0 + n, :], [r_winr]))
                linear_tm(w_pool, hT, r_hT, n, W, 4608, 36,
                          tm_sink_dram(stage, gates[r0:r0 + n, :], [r_gates], func=AF.Sigmoid))
                linear_fm(w_pool, hT, r_hT, n, W, 0, 12, fm_sink_dram(stage_b, QT, [r_QT], r0, n))
                if not samp:
                    linear_fm(w_pool, hT, r_hT, n, W, 1536, 8,
                              fm_sink_dram(stage_b, rcT, [r_rcT], r0, n,
                                           add_tab=lambda j: (peT[:, j // 4, :], r_peT)))
                linear_fm(w_pool, hT, r_hT, n, W, 2560, 4, fm_sink_dram(stage_b, ksT, [r_ksT], r0, n))
                linear_fm(w_pool, hT, r_hT, n, W, 3584, 4, fm_sink_dram(stage_b, kwT, [r_kwT], r0, n))
                linear_fm(w_pool, hT, r_hT, n, W, 4644, 4, fm_sink_dram(stage_b, qmT, [r_qmT], r0, n))
            S.dma("act", o_win_p[a], winr[SEQ - 512:SEQ, :], reads=[r_winr])
            S.dma("act", o_win_s[a][504:512, :], winr[SEQ:SEQ + T_S, :], reads=[r_winr])
            S.dma("act", o_win_s[a][0:504, :], win_state[a][8:512, :])

    def _phaseA_ret(layer, mk):
        bl = layer // 2
        W = ret_w_b[bl]
        with Scope() as sc:
            nb = NormBufs(sc)
            hT = sc.sb("hT", [128, KC, 512], BF16)
            r_hT = Res()
            w_pool = sc.pool("wbuf", 3, [128, KC, 512], BF16)
            stage = sc.pool("stage", 4, [128, 512], F32)
            stage_b = sc.pool("stageb", 4, [128, 512], BF16)
            tmp = sc.pool("rot", 4, [128, 512], F32)
            cosb = sc.sb("cosb", [128, 512], F32)
            sinb = sc.sb("sinb", [128, 512], F32)
            r_cs = Res()
            mem_prepare(layer, mk, w_pool, nb, stage, hT, r_hT)
            gt, gr = load_gain(nb, norm1_g[layer:layer + 1, :])
            for (r0, n) in BLOCKS:
                for t0 in range(0, n, 128):
                    P = min(128, n - t0)
                    norm_tile(nb, x_rows(layer, r0 + t0, P), [xres(r0)], P, gt, gr, t0, hT, r_hT)
                S.dma("sp", cosb[:, :n], tabs["cosT"][:, r0:r0 + n], writes=[r_cs])
                S.dma("sp", sinb[:, :n], tabs["sinT"][:, r0:r0 + n], writes=[r_cs])

                def rot_sink(dst, rdst, scl):
                    held = {}

                    def f(j, ps, rp):
                        if j % 2 == 0:
                            held["x1"] = (ps, rp)
                            return
                        p1, r1 = held["x1"]
                        p2, r2 = ps, rp
                        ta, ra = tmp.get()
                        tb, rb = tmp.get()
                        o1, ro1 = stage_b.get()
                        o2, ro2 = stage_b.get()
                        S.op("dve", lambda v: v.scalar_tensor_tensor(
                            out=ta[:, :n], in0=p1[:, :n], scalar=scl, in1=cosb[:, :n],
                            op0=ALU.mult, op1=ALU.mult), reads=[r1, r_cs], writes=[ra])
                        S.op("dve", lambda v: v.scalar_tensor_tensor(
                            out=tb[:, :n], in0=p2[:, :n], scalar=scl, in1=sinb[:, :n],
                            op0=ALU.mult, op1=ALU.mult), reads=[r2, r_cs], writes=[rb])
                        S.op("pool", lambda g: g.tensor_tensor(out=o1[:, :n], in0=ta[:, :n], in1=tb[:, :n],
                                                               op=ALU.subtract),
                             reads=[ra, rb], writes=[ro1])
                        tc_, rc_ = tmp.get()
                        td, rd = tmp.get()
                        S.op("dve", lambda v: v.scalar_tensor_tensor(
                            out=tc_[:, :n], in0=p1[:, :n], scalar=scl, in1=sinb[:, :n],
                            op0=ALU.mult, op1=ALU.mult), reads=[r1, r_cs], writes=[rc_])
                        S.op("dve", lambda v: v.scalar_tensor_tensor(
                            out=td[:, :n], in0=p2[:, :n], scalar=scl, in1=cosb[:, :n],
                            op0=ALU.mult, op1=ALU.mult), reads=[r2, r_cs], writes=[rd])
                        S.op("pool", lambda g: g.tensor_tensor(out=o2[:, :n], in0=tc_[:, :n], in1=td[:, :n],
                                                               op=ALU.add),
                             reads=[rc_, rd], writes=[ro2])
                        S.dma("sp", dst[j - 1, :, r0:r0 + n], o1[:, :n], reads=[ro1], writes=rdst)
                        S.dma("sp", dst[j, :, r0:r0 + n], o2[:, :n], reads=[ro2], writes=rdst)
                    return f
                linear_fm(w_pool, hT, r_hT, n, W, 0, 12, rot_sink(QT, [r_QT], 1.0))
                linear_fm(w_pool, hT, r_hT, n, W, 1536, 12, rot_sink(KT12, [r_KT12], 1.0 / 16.0))
                linear_fm(w_pool, hT, r_hT, n, W, 6144, 4, fm_sink_dram(stage_b, qmT, [r_qmT], r0, n))

                def v_sink(t0, P, cb0, cw, ps, rp):
                    stg, rs = stage_b.get()
                    copy_op(evac_engine(), stg[:P, :cw], ps[:P, :cw], [rp], [rs])
                    S.dma("sp", vtm[r0 + t0:r0 + t0 + P, cb0:cb0 + cw], stg[:P, :cw], reads=[rs],
                          writes=[r_vtm])
                linear_tm(w_pool, hT, r_hT, n, W, 3072, 1536, v_sink)
                linear_tm(w_pool, hT, r_hT, n, W, 4608, 1536,
                          tm_sink_dram(stage, sgate[r0:r0 + n, :], [r_sgate], func=AF.Silu))

    def gelu_tanh(sc_tmp, ps, rp, n, dst, rdst):
        x, rx = sc_tmp.get()
        u, ru = sc_tmp.get()
        copy_op("act", x[:, :n], ps[:, :n], [rp], [rx])
        S.op("dve", lambda v: v.tensor_tensor(out=u[:, :n], in0=x[:, :n], in1=x[:, :n], op=ALU.mult),
             reads=[rx], writes=[ru])
        S.op("dve", lambda v: v.tensor_scalar(out=u[:, :n], in0=u[:, :n], scalar1=0.044715, scalar2=1.0,
                                              op0=ALU.mult, op1=ALU.add), reads=[ru], writes=[ru])
        S.op("dve", lambda v: v.tensor_tensor(out=u[:, :n], in0=u[:, :n], in1=x[:, :n], op=ALU.mult),
             reads=[ru, rx], writes=[ru])
        S.op("act", lambda a: a.activation(out=u[:, :n], in_=u[:, :n], func=AF.Sigmoid,
                                           scale=2.0 * 0.7978845608028654),
             reads=[ru], writes=[ru])
        S.op("dve", lambda v: v.tensor_tensor(out=dst, in0=u[:, :n], in1=x[:, :n], op=ALU.mult),
             reads=[ru, rx], writes=[rdst])

    def compress(sc, a, ty, rc_ap, rrc, nblk, kcT_dst, vc_dst, rdst, w1t, w2t, rw, tmp):
        ps, rp = psA.get()
        rc3 = rc_ap.rearrange("p (b j) -> p j b", j=32)
        fns = []
        for j in range(32):
            fns.append(lambda pe, j=j: pe.matmul(ps[:, :nblk], w1t[:, ty, j, :], rc3[:, j, :],
                                                 start=(j == 0), stop=(j == 31)))
        S.mm_group(fns, reads=[rrc, rw], writes=[rp])
        gT, rg = tmp["g"].get()
        gelu_tanh(tmp["f"], ps, rp, nblk, gT[:, :nblk], rg)
        if ty == 0:
            ps2, rp2 = psA.get()
            S.mm_group([lambda pe: pe.matmul(ps2[:, :nblk], w2t[:, 0, :], gT[:, :nblk], start=True, stop=True)],
                       reads=[rg, rw], writes=[rp2])
            copy_op(evac_engine(), kcT_dst, ps2[:, :nblk], [rp2], [rdst])
        else:
            for c in range(nblk // 128):
                ps2, rp2 = psA.get()
                S.mm_group([lambda pe, c=c: pe.matmul(ps2[:, :128], gT[:, c * 128:(c + 1) * 128], w2t[:, 1, :],
                                                      start=True, stop=True)],
                           reads=[rg, rw], writes=[rp2])
                copy_op(evac_engine(), vc_dst(c), ps2[:, :128], [rp2], [rdst])

    def load_cmp_weights(sc, a):
        w1t = sc.sb("w1t", [128, 2, 32, 128], BF16)
        w2t = sc.sb("w2t", [128, 2, 128], BF16)
        rw = Res()
        for ty in range(2):
            S.dma("pool", w1t[:, ty, :, :], cmp_w1[a][ty].rearrange("(j d) o -> d j o", d=128), writes=[rw])
            S.dma("pool", w2t[:, ty, :], cmp_w2[a][ty], writes=[rw])
        return w1t, w2t, rw

    def nsa_mix_prompt(layer):
        a = layer // 2
        with Scope() as sc:
            ab = AttnBufs(sc, 128, 4096)
            cmpmask_t = sc.sb("cmpmask", [128, 32, 128], F32)
            bonus_t = sc.sb("bonus", [128, 32, 64], F32)
            tri_le = sc.sb("tri_le", [128, 128], F32)
            tri_gt = sc.sb("tri_gt", [128, 128], F32)
            gates_t = sc.sb("gates_t", [128, 32, 36], F32)
            r_tab = Res()
            S.dma("sp", cmpmask_t[:], tabs["cmpmask"].rearrange("(t p) n -> p t n", p=128), writes=[r_tab])
            S.dma("sp", bonus_t[:], tabs["bonus"].rearrange("(t p) n -> p t n", p=128), writes=[r_tab])
            S.dma("sp", tri_le[:], tabs["tri_le"], writes=[r_tab])
            S.dma("sp", tri_gt[:], tabs["tri_gt"], writes=[r_tab])
            S.dma("sp", gates_t[:], gates[0:SEQ, :].rearrange("(t p) c -> p t c", p=128),
                  reads=[r_gates], writes=[r_tab])
            kcT = sc.sb("kcT", [128, 4, 128], BF16)
            vc = sc.sb("vc", [128, 4, 128], BF16)
            r_kv = Res()
            with Scope() as scc:
                w1t, w2t, rw = load_cmp_weights(scc, a)
                tmp = {"g": scc.pool("gT", 2, [128, 512], BF16), "f": scc.pool("gf", 4, [128, 512], F32)}
                rcb = scc.pool("rcb", 2, [128, SEQ], BF16)
                for g in range(4):
                    for ty in range(2):
                        rc, rrc = rcb.get()
                        S.dma("sp", rc[:], rcT[ty * 4 + g], reads=[r_rcT], writes=[rrc])
                        compress(scc, a, ty, rc[:], rrc, 128, kcT[:, g, :], lambda c, g=g: vc[:, g, :], r_kv,
                                 w1t, w2t, rw, tmp)
            QTg = sc.sb("QTg", [128, 3, SEQ], BF16)
            ksTg = sc.sb("ksTg", [128, SEQ], BF16)
            kwTg = sc.sb("kwTg", [128, SEQ], BF16)
            vsg = sc.sb("vsg", [128, 32, 128], BF16)
            vwg = sc.sb("vwg", [128, 32, 128], BF16)
            r_grp = Res()
            small = sc.pool("small", 6, [128, 128], F32)
            pbf = sc.pool("pbf", 2, [128, 128], BF16)
            ptc = sc.pool("ptc", 2, [128, 128], BF16)
            sel_pool = sc.pool("selp", 2, [128, 64], F32)
            m8_pool = sc.pool("m8", 2, [128, 16 + 64], F32)
            ocomb_pool = sc.pool("ocomb", 2, [128, 384], F32)
            ob_pool = sc.pool("ob", 2, [128, 384], BF16)
            for g in range(4):
                S.dma("sp", QTg[:], QT[3 * g:3 * g + 3, :, 0:SEQ].rearrange("h p n -> p h n"),
                      reads=[r_QT], writes=[r_grp])
                S.dma("sp", ksTg[:], ksT[g, :, 0:SEQ], reads=[r_ksT], writes=[r_grp])
                S.dma("sp", kwTg[:], kwT[g, :, 0:SEQ], reads=[r_kwT], writes=[r_grp])
                S.dma("pool", vsg[:], o_kv_p[a][:, 1536 + g * 128:1536 + (g + 1) * 128].rearrange(
                    "(c p) d -> p c d", p=128), reads=[r_okv], writes=[r_grp])
                S.dma("pool", vwg[:], winr[0:SEQ, 512 + g * 128:512 + (g + 1) * 128].rearrange(
                    "(c p) d -> p c d", p=128), reads=[r_winr], writes=[r_grp])
                for i in range(32):
                    qs = slice(i * 128, (i + 1) * 128)
                    oc, roc = ocomb_pool.get()
                    pgrp, rpg = small.get()
                    first = [True]

                    def gated_out(col, hh, oc=oc, roc=roc, i=i):
                        h = 3 * g + hh
                        gcol = gates_t[:, i, h * 3 + col:h * 3 + col + 1]

                        def f(ps, rp, st, rst):
                            if st is not None:
                                S.op("dve", lambda v: v.tensor_tensor(out=st[:, 4:5], in0=st[:, 3:4], in1=gcol,
                                                                      op=ALU.mult),
                                     reads=[rst, r_tab], writes=[rst])
                                sc_ap, rr = st[:, 4:5], [rst]
                            else:
                                sc_ap, rr = gcol, [r_tab]
                            dst = oc[:, hh * 128:(hh + 1) * 128]
                            if col == 0:
                                S.op("dve", lambda v: v.tensor_scalar(out=dst, in0=ps[:, :128], scalar1=sc_ap,
                                                                      scalar2=None, op0=ALU.mult),
                                     reads=[rp] + rr, writes=[roc])
                            else:
                                S.op("dve", lambda v: v.scalar_tensor_tensor(
                                    out=dst, in0=ps[:, :128], scalar=sc_ap, in1=dst, op0=ALU.mult, op1=ALU.add),
                                    reads=[rp] + rr, writes=[roc])
                        return f
                    for hh in range(3):
                        ps, rp = psA.get()
                        S.mm_group([lambda pe, hh=hh: pe.matmul(ps[:, :128], QTg[:, hh, qs], kcT[:, g, :],
                                                                start=True, stop=True)],
                                   reads=[r_grp, r_kv], writes=[rp])
                        sc_t, rsc = small.get()
                        S.op("dve", lambda v: v.scalar_tensor_tensor(
                            out=sc_t[:], in0=ps[:, :128], scalar=SCALE, in1=cmpmask_t[:, i, :],
                            op0=ALU.mult, op1=ALU.add), reads=[rp, r_tab], writes=[rsc])
                        st, rst = st_pool.get()
                        S.op("dve", lambda v: v.reduce_max(out=st[:, 0:1], in_=sc_t[:], axis=AX.X),
                             reads=[rsc], writes=[rst])
                        S.op("dve", lambda v: v.tensor_scalar(out=st[:, 1:2], in0=st[:, 0:1], scalar1=-1.0e4,
                                                              scalar2=-1.0, op0=ALU.max, op1=ALU.mult),
                             reads=[rst], writes=[rst])
                        S.op("act", lambda a_: a_.activation(out=sc_t[:], in_=sc_t[:], func=AF.Exp,
                                                             bias=st[:, 1:2], scale=1.0, accum_out=st[:, 2:3]),
                             reads=[rsc, rst], writes=[rsc, rst])
                        S.op("dve", lambda v: v.tensor_scalar(out=st[:, 3:4], in0=st[:, 2:3], scalar1=1.0e-30,
                                                              scalar2=None, op0=ALU.max), reads=[rst], writes=[rst])
                        S.op("dve", lambda v: v.reciprocal(st[:, 3:4], st[:, 3:4]), reads=[rst], writes=[rst])
                        S.op("dve", lambda v: v.tensor_scalar(out=sc_t[:], in0=sc_t[:], scalar1=st[:, 3:4],
                                                              scalar2=None, op0=ALU.mult),
                             reads=[rsc, rst], writes=[rsc])
                        if hh == 0:
                            S.op("pool", lambda g_: g_.tensor_copy(pgrp[:], sc_t[:]), reads=[rsc], writes=[rpg])
                        else:
                            S.op("pool", lambda g_: g_.tensor_tensor(out=pgrp[:], in0=pgrp[:], in1=sc_t[:],
                                                                     op=ALU.add), reads=[rsc], writes=[rpg])
                        pb, rpb = pbf.get()
                        S.op("act", lambda a_: a_.copy(pb[:], sc_t[:]), reads=[rsc], writes=[rpb])
                        pt, rpt = psT.get()
                        S.mm_group([lambda pe: pe.transpose(pt[:, :128], pb[:], ident_b[:])],
                                   reads=[rpb, r_ident], writes=[rpt])
                        pc, rpc = ptc.get()
                        copy_op("act", pc[:], pt[:, :128], [rpt], [rpc])
                        ps2, rp2 = psB.get()
                        S.mm_group([lambda pe: pe.matmul(ps2[:, :128], pc[:], vc[:, g, :], start=True, stop=True)],
                                   reads=[rpc, r_kv], writes=[rp2])
                        gated_out(0, hh)(ps2, rp2, None, None)
                    score, rscore = sel_pool.get()
                    pg3 = pgrp[:].rearrange("p (b two) -> p b two", two=2)
                    S.op("dve", lambda v: v.tensor_tensor(out=score[:], in0=pg3[:, :, 0], in1=pg3[:, :, 1],
                                                          op=ALU.add), reads=[rpg], writes=[rscore])
                    S.op("dve", lambda v: v.tensor_tensor(out=score[:], in0=score[:], in1=bonus_t[:, i, :],
                                                          op=ALU.add), reads=[rscore, r_tab], writes=[rscore])
                    selneg, rsel = sel_pool.get()
                    topk_selneg(m8_pool, score, rscore, 128, 64, selneg, rsel)
                    for hh in range(3):
                        nk = (i + 1) * 128
                        Ssb, rS = ab.S.get()
                        for kb in range(0, nk, 512):
                            w = min(512, nk - kb)
                            ps, rp = psA.get()
                            S.mm_group([lambda pe, hh=hh, kb=kb, w=w: pe.matmul(
                                ps[:, :w], QTg[:, hh, qs], ksTg[:, kb:kb + w], start=True, stop=True)],
                                reads=[r_grp], writes=[rp])
                            nb_ = w // 64
                            S.op("dve", lambda v, kb=kb, w=w, nb_=nb_: v.scalar_tensor_tensor(
                                out=Ssb[:, kb:kb + w].rearrange("p (b k) -> p b k", k=64),
                                in0=ps[:, :w].rearrange("p (b k) -> p b k", k=64), scalar=SCALE,
                                in1=selneg[:, kb // 64:kb // 64 + nb_].unsqueeze(2).to_broadcast([128, nb_, 64]),
                                op0=ALU.mult, op1=ALU.add), reads=[rp, rsel], writes=[rS])
                        S.op("pool", lambda g_: g_.tensor_tensor(out=Ssb[:, i * 128:(i + 1) * 128],
                                                                 in0=Ssb[:, i * 128:(i + 1) * 128], in1=tri_le[:],
                                                                 op=ALU.add), reads=[r_tab], writes=[rS])
                        softmax_pv(ab, 128, nk, Ssb, rS, lambda c: (vsg[:, c, :], r_grp, 128), gated_out(1, hh))
                        c0 = max(0, i - 4)
                        nk = (i + 1 - c0) * 128
                        Ssb, rS = ab.S.get()
                        for kb in range(0, nk, 512):
                            w = min(512, nk - kb)
                            ps, rp = psA.get()
                            S.mm_group([lambda pe, hh=hh, kb=kb, w=w, c0=c0: pe.matmul(
                                ps[:, :w], QTg[:, hh, qs], kwTg[:, c0 * 128 + kb:c0 * 128 + kb + w],
                                start=True, stop=True)], reads=[r_grp], writes=[rp])
                            S.op("act", lambda a_, kb=kb, w=w: a_.activation(
                                out=Ssb[:, kb:kb + w], in_=ps[:, :w], func=AF.Identity, scale=SCALE),
                                reads=[rp], writes=[rS])
                        if i >= 4:
                            S.op("pool", lambda g_: g_.tensor_tensor(out=Ssb[:, 0:128], in0=Ssb[:, 0:128],
                                                                     in1=tri_gt[:], op=ALU.add),
                                 reads=[r_tab], writes=[rS])
                        S.op("pool", lambda g_, nk=nk: g_.tensor_tensor(out=Ssb[:, nk - 128:nk], in0=Ssb[:, nk - 128:nk],
                                                                        in1=tri_le[:], op=ALU.add),
                             reads=[r_tab], writes=[rS])
                        softmax_pv(ab, 128, nk, Ssb, rS, lambda c, c0=c0: (vwg[:, c0 + c, :], r_grp, 128),
                                   gated_out(2, hh))
                    ob, rob = ob_pool.get()
                    S.op("act", lambda a_: a_.copy(ob[:], oc[:]), reads=[roc], writes=[rob])
                    S.dma("sp", tokb[i * 128:(i + 1) * 128, g * 384:(g + 1) * 384], ob[:], reads=[rob],
                          writes=[r_tokb])

    def nsa_mix_sample(layer):
        a = layer // 2
        cache2d = cache.rearrange("a r c -> (a r) c")
        with Scope() as sc:
            kcT = sc.sb("kcT_s", [128, 4, 512], BF16)
            vc = sc.sb("vc_s", [128, 4, 4, 128], BF16)
            r_kv = Res()
            peT = sc.sb("peT_s", [128, 2, 32], F32)
            r_peT = Res()
            S.dma("sp", peT[:], cmp_peT[a].rearrange("t d j -> d t j"), writes=[r_peT])
            idx = sc.sb("idx", [128, NPAGE], I32)
            idxf = sc.sb("idxf", [128, NPAGE], F32)
            iot = sc.sb("iot", [128, 1], F32)
            r_idx = Res()
            S.dma("sp", idx[:], page_tab[0:1, :].to_broadcast([128, NPAGE]), writes=[r_idx])
            S.op("pool", lambda g_: g_.iota(iot[:], pattern=[[0, 1]], base=0, channel_multiplier=1,
                                            allow_small_or_imprecise_dtypes=True), writes=[r_idx])
            S.op("dve", lambda v: v.tensor_copy(idxf[:], idx[:]), reads=[r_idx], writes=[r_idx])
            S.op("dve", lambda v: v.tensor_scalar(out=idxf[:], in0=idxf[:], scalar1=128.0, scalar2=iot[:, 0:1],
                                                  op0=ALU.mult, op1=ALU.add), reads=[r_idx], writes=[r_idx])
            if a > 0:
                S.op("dve", lambda v: v.tensor_scalar(out=idxf[:], in0=idxf[:], scalar1=float(a * 1280 * 128),
                                                      scalar2=None, op0=ALU.add), reads=[r_idx], writes=[r_idx])
            S.op("dve", lambda v: v.tensor_copy(idx[:], idxf[:]), reads=[r_idx], writes=[r_idx])
            with Scope() as sc1:
                w1t, w2t, rw = load_cmp_weights(sc1, a)
                tmp = {"g": sc1.pool("gT", 2, [128, 512], BF16), "f": sc1.pool("gf", 4, [128, 512], F32)}
                page_pool = sc1.pool("page", 3, [128, 2048], F32)
                rcb = sc1.sb("rcb_s", [128, 8, 4096], BF16)
                r_rcb = Res()
                ksst = sc1.pool("ksst", 2, [128, 4, 512], BF16)
                vsst = sc1.pool("vsst", 2, [128, 4, 4, 128], BF16)
                psF = psB
                for batch in range(4):
                    for pq in range(8):
                        kst, rks = ksst.get()
                        vst, rvs = vsst.get()
                        for pp in range(4):
                            pg = batch * 32 + pq * 4 + pp
                            pt_, rpg = page_pool.get()
                            S.dma("pool", None, None, reads=[r_idx], writes=[rpg],
                                  fn=lambda g_, pt_=pt_, pg=pg: g_.indirect_dma_start(
                                      out=pt_[:, :], out_offset=None, in_=cache2d[:, :],
                                      in_offset=bass.IndirectOffsetOnAxis(ap=idx[:, pg:pg + 1], axis=0)))
                            S.op("pool", lambda g_, pt_=pt_, pp=pp: g_.tensor_copy(
                                vst[:, pp, :, :], pt_[:, 1536:2048].rearrange("p (g d) -> p g d", d=128)),
                                reads=[rpg], writes=[rvs])
                            for ty in range(3):
                                ps, rp = psF.get()
                                fns = []
                                for gg in range(4):
                                    col = (ty * 4 + gg) * 128
                                    fns.append(lambda pe, gg=gg, col=col, pt_=pt_: pe.transpose(
                                        ps[:, gg * 128:(gg + 1) * 128], pt_[:, col:col + 128], ident_f[:]))
                                S.mm_group(fns, reads=[rpg, r_ident], writes=[rp])
                                src = ps[:].rearrange("p (g r) -> p g r", r=128)
                                loc = (pq * 4 + pp) * 128
                                if ty < 2:
                                    S.op("dve", lambda v, ty=ty, loc=loc, src=src: v.tensor_tensor(
                                        out=rcb[:, ty * 4:(ty + 1) * 4, loc:loc + 128].rearrange(
                                            "p g (b j) -> p g b j", j=32),
                                        in0=src.rearrange("p g (b j) -> p g b j", j=32),
                                        in1=peT[:, ty, :].unsqueeze(1).unsqueeze(1).to_broadcast([128, 4, 4, 32]),
                                        op=ALU.add), reads=[rp, r_peT], writes=[r_rcb])
                                else:
                                    copy_op("act", kst[:, :, pp * 128:(pp + 1) * 128], src, [rp], [rks])
                        p0 = (batch * 32 + pq * 4) * 128
                        S.dma("sp", ksT_s[:, :, p0:p0 + 512].rearrange("g p n -> p g n"), kst[:], reads=[rks],
                              writes=[r_ksT_s])
                        for gg in range(4):
                            S.dma("sp", vs_s[gg, p0:p0 + 512, :].rearrange("(q r) d -> r q d", r=128),
                                  vst[:, :, gg, :], reads=[rvs], writes=[r_vs_s])
                    for g in range(4):
                        for ty in range(2):
                            compress(sc1, a, ty, rcb[:, ty * 4 + g, :], r_rcb, 128,
                                     kcT[:, g, batch * 128:(batch + 1) * 128],
                                     lambda c, g=g, batch=batch: vc[:, g, batch, :], r_kv, w1t, w2t, rw, tmp)
            with Scope() as sc2:
                NK = PAST + T_S
                ab = AttnBufs(sc2, T_S, NK)
                s_bonus = sc2.sb("s_bonus", [T_S, 264], F32)
                s_tri8 = sc2.sb("s_tri8", [T_S, 8], F32)
                s_winm = sc2.sb("s_winm", [T_S, 512], F32)
                gates_t = sc2.sb("gates_s", [T_S, 36], F32)
                r_tab = Res()
                S.dma("sp", s_bonus[:], tabs["s_bonus"][0:T_S, :], writes=[r_tab])
                S.dma("sp", s_tri8[:], tabs["s_tri8"][0:T_S, :], writes=[r_tab])
                S.dma("sp", s_winm[:], tabs["s_winmask"][0:T_S, :], writes=[r_tab])
                S.dma("sp", gates_t[:], gates[SEQ:NTOK, :], reads=[r_gates], writes=[r_tab])
                QTg = sc2.sb("QTg_s", [128, 3, T_S], BF16)
                ksTg = sc2.sb("ksTg_s", [128, NK], BF16)
                vsg = sc2.sb("vsg_s", [128, 129, 128], BF16)
                kwTg = sc2.sb("kwTg_s", [128, 520], BF16)
                vwg = sc2.sb("vwg_s", [128, 5, 128], BF16)
                wst = sc2.sb("wst", [128, 4, 128], F32)
                wsb = sc2.sb("wsb", [128, 512], BF16)
                r_ws = Res()
                r_grp = Res()
                small = sc2.pool("small_s", 3, [T_S, 512], F32)
                pgrp_t = sc2.sb("pgrp_s", [T_S, 512], F32)
                r_pgrp = Res()
                pbf = sc2.pool("pbf_s", 2, [T_S, 512], BF16)
                ptc = sc2.pool("ptc_s", 2, [128, 4 * T_S], BF16)
                sel_pool = sc2.pool("selp_s", 2, [T_S, 264], F32)
                m8_pool = sc2.pool("m8_s", 2, [T_S, 16 + 264], F32)
                oc = sc2.sb("ocomb_s", [T_S, 384], F32)
                roc = Res()
                ob = sc2.sb("ob_s", [T_S, 384], BF16)
                for g in range(4):
                    S.dma("sp", QTg[:], QT[3 * g:3 * g + 3, :, SEQ:NTOK].rearrange("h p n -> p h n"),
                          reads=[r_QT], writes=[r_grp])
                    S.dma("sp", ksTg[:, 0:PAST], ksT_s[g], reads=[r_ksT_s], writes=[r_grp])
                    S.dma("sp", ksTg[:, PAST:NK], ksT[g, :, SEQ:NTOK], reads=[r_ksT], writes=[r_grp])
                    S.dma("sp", vsg[:, 0:128, :], vs_s[g].rearrange("(c p) d -> p c d", p=128),
                          reads=[r_vs_s], writes=[r_grp])
                    S.dma("pool", vsg[:T_S, 128, :], o_kv_s[a][:, 1536 + g * 128:1536 + (g + 1) * 128],
                          reads=[r_okv], writes=[r_grp])
                    S.dma("sp", wst[:], win_state[a][:, g * 128:(g + 1) * 128].rearrange("(c p) d -> p c d", p=128),
                          writes=[r_ws])
                    S.op("dve", lambda v: v.tensor_copy(wsb[:], wst[:].rearrange("p c d -> p (c d)")),
                         reads=[r_ws], writes=[r_ws])
                    transpose_into(wsb, r_ws, 128, 4,
                                   lambda c0, n: kwTg[:, 0:512].rearrange("p (c r) -> p c r", r=128)[:, c0:c0 + n, :],
                                   r_grp)
                    S.dma("sp", kwTg[:, 512:520], kwT[g, :, SEQ:NTOK], reads=[r_kwT], writes=[r_grp])
                    S.dma("pool", vwg[:, 0:4, :],
                          win_state[a][:, 512 + g * 128:512 + (g + 1) * 128].rearrange("(c p) d -> p c d", p=128),
                          writes=[r_grp])
                    S.dma("pool", vwg[:T_S, 4, :], winr[SEQ:NTOK, 512 + g * 128:512 + (g + 1) * 128],
                          reads=[r_winr], writes=[r_grp])
                    pgrp, rpg = pgrp_t, r_pgrp

                    def gated_out(col, hh):
                        h = 3 * g + hh
                        gcol = gates_t[:, h * 3 + col:h * 3 + col + 1]

                        def f(ps, rp, st, rst):
                            if st is not None:
                                S.op("dve", lambda v: v.tensor_tensor(out=st[:T_S, 4:5], in0=st[:T_S, 3:4], in1=gcol,
                                                                      op=ALU.mult), reads=[rst, r_tab], writes=[rst])
                                sc_ap, rr = st[:T_S, 4:5], [rst]
                            else:
                                sc_ap, rr = gcol, [r_tab]
                            dst = oc[:, hh * 128:(hh + 1) * 128]
                            if col == 0:
                                S.op("dve", lambda v: v.tensor_scalar(out=dst, in0=ps[:T_S, :128], scalar1=sc_ap,
                                                                      scalar2=None, op0=ALU.mult),
                                     reads=[rp] + rr, writes=[roc])
                            else:
                                S.op("dve", lambda v: v.scalar_tensor_tensor(
                                    out=dst, in0=ps[:T_S, :128], scalar=sc_ap, in1=dst, op0=ALU.mult, op1=ALU.add),
                                    reads=[rp] + rr, writes=[roc])
                        return f
                    for hh in range(3):
                        ps, rp = psA.get()
                        S.mm_group([lambda pe, hh=hh: pe.matmul(ps[:T_S, :512], QTg[:, hh, :], kcT[:, g, :],
                                                                start=True, stop=True)],
                                   reads=[r_grp, r_kv], writes=[rp])
                        sc_t, rsc = small.get()
                        S.op("act", lambda a_: a_.activation(out=sc_t[:], in_=ps[:T_S, :512], func=AF.Identity,
                                                             scale=SCALE), reads=[rp], writes=[rsc])
                        st, rst = st_pool.get()
                        S.op("dve", lambda v: v.reduce_max(out=st[:T_S, 0:1], in_=sc_t[:], axis=AX.X),
                             reads=[rsc], writes=[rst])
                        S.op("dve", lambda v: v.tensor_scalar(out=st[:T_S, 1:2], in0=st[:T_S, 0:1], scalar1=-1.0,
                                                              scalar2=None, op0=ALU.mult), reads=[rst], writes=[rst])
                        S.op("act", lambda a_: a_.activation(out=sc_t[:], in_=sc_t[:], func=AF.Exp,
                                                             bias=st[:T_S, 1:2], scale=1.0, accum_out=st[:T_S, 2:3]),
                             reads=[rsc, rst], writes=[rsc, rst])
                        S.op("dve", lambda v: v.reciprocal(st[:T_S, 3:4], st[:T_S, 2:3]), reads=[rst], writes=[rst])
                        S.op("dve", lambda v: v.tensor_scalar(out=sc_t[:], in0=sc_t[:], scalar1=st[:T_S, 3:4],
                                                              scalar2=None, op0=ALU.mult),
                             reads=[rsc, rst], writes=[rsc])
                        if hh == 0:
                            S.op("pool", lambda g_: g_.tensor_copy(pgrp[:], sc_t[:]), reads=[rsc], writes=[rpg])
                        else:
                            S.op("pool", lambda g_: g_.tensor_tensor(out=pgrp[:], in0=pgrp[:], in1=sc_t[:],
                                                                     op=ALU.add), reads=[rsc], writes=[rpg])
                        pb, rpb = pbf.get()
                        S.op("act", lambda a_: a_.copy(pb[:], sc_t[:]), reads=[rsc], writes=[rpb])
                        pt, rpt = psT.get()
                        S.mm_group([lambda pe, c=c: pe.transpose(pt[:, c * T_S:(c + 1) * T_S],
                                                                 pb[:, c * 128:(c + 1) * 128], ident_b[:T_S, :T_S])
                                    for c in range(4)], reads=[rpb, r_ident], writes=[rpt])
                        pc, rpc = ptc.get()
                        copy_op("act", pc[:], pt[:, :4 * T_S], [rpt], [rpc])
                        ps2, rp2 = psB.get()
                        S.mm_group([lambda pe, c=c: pe.matmul(ps2[:T_S, :128], pc[:, c * T_S:(c + 1) * T_S],
                                                              vc[:, g, c, :], start=(c == 0), stop=(c == 3))
                                    for c in range(4)], reads=[rpc, r_kv], writes=[rp2])
                        gated_out(0, hh)(ps2, rp2, None, None)
                    score, rscore = sel_pool.get()
                    pg3 = pgrp[:].rearrange("p (b two) -> p b two", two=2)
                    S.op("dve", lambda v: v.tensor_copy(score[:], s_bonus[:]), reads=[r_tab], writes=[rscore])
                    S.op("dve", lambda v: v.tensor_tensor(out=score[:, 0:256], in0=score[:, 0:256], in1=pg3[:, :, 0],
                                                          op=ALU.add), reads=[rpg], writes=[rscore])
                    S.op("dve", lambda v: v.tensor_tensor(out=score[:, 0:256], in0=score[:, 0:256], in1=pg3[:, :, 1],
                                                          op=ALU.add), reads=[rpg], writes=[rscore])
                    selneg, rsel = sel_pool.get()
                    topk_selneg(m8_pool, score, rscore, T_S, 264, selneg, rsel)
                    for hh in range(3):
                        Ssb, rS = ab.S.get()
                        for kb in range(0, PAST, 512):
                            ps, rp = psA.get()
                            S.mm_group([lambda pe, hh=hh, kb=kb: pe.matmul(
                                ps[:T_S, :512], QTg[:, hh, :], ksTg[:, kb:kb + 512], start=True, stop=True)],
                                reads=[r_grp], writes=[rp])
                            S.op("dve", lambda v, kb=kb: v.scalar_tensor_tensor(
                                out=Ssb[:, kb:kb + 512].rearrange("p (b k) -> p b k", k=64),
                                in0=ps[:T_S, :512].rearrange("p (b k) -> p b k", k=64), scalar=SCALE,
                                in1=selneg[:, kb // 64:kb // 64 + 8].unsqueeze(2).to_broadcast([T_S, 8, 64]),
                                op0=ALU.mult, op1=ALU.add), reads=[rp, rsel], writes=[rS])
                        ps, rp = psA.get()
                        S.mm_group([lambda pe, hh=hh: pe.matmul(ps[:T_S, :T_S], QTg[:, hh, :], ksTg[:, PAST:NK],
                                                                start=True, stop=True)], reads=[r_grp], writes=[rp])
                        S.op("dve", lambda v: v.scalar_tensor_tensor(
                            out=Ssb[:, PAST:NK], in0=ps[:T_S, :T_S], scalar=SCALE, in1=s_tri8[:],
                            op0=ALU.mult, op1=ALU.add), reads=[rp, r_tab], writes=[rS])
                        S.op("dve", lambda v: v.tensor_scalar(out=Ssb[:, PAST:NK], in0=Ssb[:, PAST:NK],
                                                              scalar1=selneg[:, 256:257], scalar2=None, op0=ALU.add),
                             reads=[rsel], writes=[rS])
                        softmax_pv(ab, T_S, NK, Ssb, rS,
                                   lambda c: (vsg[:, c, :], r_grp, 128) if c < 128 else (vsg[:T_S, 128, :], r_grp, T_S),
                                   gated_out(1, hh))
                        Ssb, rS = ab.S.get()
                        ps, rp = psA.get()
                        S.mm_group([lambda pe, hh=hh: pe.matmul(ps[:T_S, :512], QTg[:, hh, :], kwTg[:, 0:512],
                                                                start=True, stop=True)], reads=[r_grp], writes=[rp])
                        S.op("dve", lambda v: v.scalar_tensor_tensor(
                            out=Ssb[:, 0:512], in0=ps[:T_S, :512], scalar=SCALE, in1=s_winm[:],
                            op0=ALU.mult, op1=ALU.add), reads=[rp, r_tab], writes=[rS])
                        ps, rp = psA.get()
                        S.mm_group([lambda pe, hh=hh: pe.matmul(ps[:T_S, :T_S], QTg[:, hh, :], kwTg[:, 512:520],
                                                                start=True, stop=True)], reads=[r_grp], writes=[rp])
                        S.op("dve", lambda v: v.scalar_tensor_tensor(
                            out=Ssb[:, 512:520], in0=ps[:T_S, :T_S], scalar=SCALE, in1=s_tri8[:],
                            op0=ALU.mult, op1=ALU.add), reads=[rp, r_tab], writes=[rS])
                        softmax_pv(ab, T_S, 520, Ssb, rS,
                                   lambda c: (vwg[:, c, :], r_grp, 128) if c < 4 else (vwg[:T_S, 4, :], r_grp, T_S),
                                   gated_out(2, hh))
                    S.op("act", lambda a_: a_.copy(ob[:], oc[:]), reads=[roc], writes=[roc])
                    S.dma("sp", tokb[SEQ:NTOK, g * 384:(g + 1) * 384], ob[:], reads=[roc], writes=[r_tokb])

    def ret_mix(layer):
        bl = layer // 2
        with Scope() as sc:
            S32 = sc.sb("S32", [128, 6, 2, 256], F32)
            Sb = sc.sb("Sb", [128, 6, 2, 256], BF16)
            r_S = [Res() for _ in range(6)]
            gn_bc = sc.sb("gn_bc", [128, 1536], F32)
            r_gn = Res()
            S.dma("sp", gn_bc[:], ret_gn[bl:bl + 1, :].to_broadcast([128, 1536]), writes=[r_gn])
            qc_pool = sc.pool("qc", 2, [128, 12, 128], BF16)
            kc_pool = sc.pool("kc", 2, [128, 12, 128], BF16)
            v_pool = sc.pool("vch", 2, [128, 1536], BF16)
            sg_pool = sc.pool("sgch", 2, [128, 1536], F32)
            qd_pool = sc.pool("qdT", 2, [128, 2, 128], BF16)
            in_pool = sc.pool("inT", 2, [128, 128], BF16)
            kd_pool = sc.pool("kd", 2, [128, 256], BF16)
            y_pool = sc.pool("yf", 2, [128, 256], F32)
            tok_pool = sc.pool("tokc", 2, [128, 1536], BF16)
            bn_pool = sc.pool("bn", 4, [128, 8], F32)
            for mode in ("prompt", "sample"):
                C = 128 if mode == "prompt" else T_S
                tag = "128" if mode == "prompt" else "8"
                dmT = sc.sb("dmT" + tag, [128, 6, C], F32)
                qdt = sc.sb("qd" + tag, [128, 6, C], F32)
                kdt = sc.sb("kd" + tag, [128, 6], F32)
                r_dt = Res()
                S.dma("sp", dmT[:], tabs["dmT" + tag].rearrange("h j i -> j h i"), writes=[r_dt])
                S.dma("sp", qdt[:], tabs["qd" + tag].rearrange("h j i -> j h i"), writes=[r_dt])
                S.dma("sp", kdt[:], tabs["kd" + tag], writes=[r_dt])
                if mode == "prompt":
                    for h in range(6):
                        S.op("pool", lambda g_, h=h: g_.memset(S32[:, h, :, :], 0.0), writes=[r_S[h]])
                        S.op("pool", lambda g_, h=h: g_.memset(Sb[:, h, :, :], 0.0), writes=[r_S[h]])
                    chunks = [(c * 128, 128) for c in range(32)]
                else:
                    for h in range(6):
                        S.dma("sp", S32[:, h, :, :],
                              ret_state[bl][h * 256:(h + 1) * 256, :].rearrange("(dc p) v -> p dc v", p=128),
                              writes=[r_S[h]])
                        S.op("act", lambda a_, h=h: a_.copy(Sb[:, h, :, :], S32[:, h, :, :]), reads=[r_S[h]],
                             writes=[r_S[h]])
                    chunks = [(SEQ, T_S)]
                for (r0, Cn) in chunks:
                    qc, rq = qc_pool.get()
                    kc, rk = kc_pool.get()
                    vch, rv = v_pool.get()
                    sg, rsg = sg_pool.get()
                    S.dma("sp", qc[:, :, :Cn], QT[:, :, r0:r0 + Cn].rearrange("j p n -> p j n"), reads=[r_QT],
                          writes=[rq])
                    S.dma("sp", kc[:, :, :Cn], KT12[:, :, r0:r0 + Cn].rearrange("j p n -> p j n"), reads=[r_KT12],
                          writes=[rk])
                    S.dma("sp", vch[:Cn, :], vtm[r0:r0 + Cn, :], reads=[r_vtm], writes=[rv])
                    S.dma("sp", sg[:Cn, :], sgate[r0:r0 + Cn, :], reads=[r_sgate], writes=[rsg])
                    tk, rtk = tok_pool.get()
                    for h in range(6):
                        ps, rp = psA.get()
                        S.mm_group([lambda pe, dc=dc, h=h: pe.matmul(ps[:Cn, :Cn], kc[:, 2 * h + dc, :Cn],
                                                                     qc[:, 2 * h + dc, :Cn], start=(dc == 0),
                                                                     stop=(dc == 1)) for dc in range(2)],
                                   reads=[rq, rk], writes=[rp])
                        inT, rin = in_pool.get()
                        S.op("dve", lambda v, h=h: v.tensor_tensor(out=inT[:Cn, :Cn], in0=ps[:Cn, :Cn],
                                                                   in1=dmT[:Cn, h, :Cn], op=ALU.mult),
                             reads=[rp, r_dt], writes=[rin])
                        qd, rqd = qd_pool.get()
                        S.op("pool", lambda g_, h=h: g_.tensor_tensor(
                            out=qd[:, :, :Cn], in0=qc[:, 2 * h:2 * h + 2, :Cn],
                            in1=qdt[:, h, :Cn].unsqueeze(1).to_broadcast([128, 2, Cn]), op=ALU.mult),
                            reads=[rq, r_dt], writes=[rqd])
                        po, rpo = psB.get()
                        fns = [lambda pe, h=h: pe.matmul(po[:Cn, :256], inT[:Cn, :Cn], vch[:Cn, h * 256:(h + 1) * 256],
                                                         start=True, stop=False)]
                        for dc in range(2):
                            fns.append(lambda pe, dc=dc, h=h: pe.matmul(po[:Cn, :256], qd[:, dc, :Cn], Sb[:, h, dc, :],
                                                                        start=False, stop=(dc == 1)))
                        S.mm_group(fns, reads=[rin, rv, rqd, r_S[h]], writes=[rpo])
                        pt, rpt = psT.get()
                        S.mm_group([lambda pe, dc=dc, h=h: pe.transpose(pt[:Cn, dc * 128:(dc + 1) * 128],
                                                                        kc[:, 2 * h + dc, :Cn], ident_b[:, :])
                                    for dc in range(2)], reads=[rk, r_ident], writes=[rpt])
                        kd, rkd = kd_pool.get()
                        S.op("dve", lambda v, h=h: v.tensor_scalar(out=kd[:Cn, :], in0=pt[:Cn, :256],
                                                                   scalar1=kdt[:Cn, h:h + 1], scalar2=None,
                                                                   op0=ALU.mult), reads=[rpt, r_dt], writes=[rkd])
                        for dc in range(2):
                            pss, rps = psA.get()
                            S.mm_group([lambda pe, dc=dc, h=h: pe.matmul(
                                pss[:, :256], kd[:Cn, dc * 128:(dc + 1) * 128], vch[:Cn, h * 256:(h + 1) * 256],
                                start=True, stop=True)], reads=[rkd, rv], writes=[rps])
                            S.op("dve", lambda v, dc=dc, h=h: v.scalar_tensor_tensor(
                                out=S32[:, h, dc, :], in0=S32[:, h, dc, :], scalar=cdec(h, C), in1=pss[:, :256],
                                op0=ALU.mult, op1=ALU.add), reads=[rps], writes=[r_S[h]])
                        S.op("act", lambda a_, h=h: a_.copy(Sb[:, h, :, :], S32[:, h, :, :]), reads=[],
                             writes=[r_S[h]])
                        bn, rbn = bn_pool.get()
                        S.op("dve", lambda v: v.bn_stats(out=bn[:Cn, 0:6], in_=po[:Cn, :256]), reads=[rpo],
                             writes=[rbn])
                        S.op("dve", lambda v: v.bn_aggr(out=bn[:Cn, 6:8], in_=bn[:Cn, 0:6]), reads=[rbn],
                             writes=[rbn])
                        S.op("act", lambda a_: a_.activation(out=bn[:Cn, 0:1], in_=bn[:Cn, 7:8], func=AF.Sqrt,
                                                             scale=1.0, bias=EPS), reads=[rbn], writes=[rbn])
                        S.op("dve", lambda v: v.reciprocal(bn[:Cn, 1:2], bn[:Cn, 0:1]), reads=[rbn], writes=[rbn])
                        yf, ry = y_pool.get()
                        S.op("dve", lambda v: v.tensor_scalar(out=yf[:Cn, :], in0=po[:Cn, :256], scalar1=bn[:Cn, 6:7],
                                                              scalar2=bn[:Cn, 1:2], op0=ALU.subtract, op1=ALU.mult),
                             reads=[rpo, rbn], writes=[ry])
                        S.op("pool", lambda g_, h=h: g_.tensor_tensor(out=yf[:Cn, :], in0=yf[:Cn, :],
                                                                      in1=gn_bc[:Cn, h * 256:(h + 1) * 256],
                                                                      op=ALU.mult), reads=[r_gn], writes=[ry])
                        S.op("pool", lambda g_, h=h: g_.tensor_tensor(out=tk[:Cn, h * 256:(h + 1) * 256],
                                                                      in0=yf[:Cn, :],
                                                                      in1=sg[:Cn, h * 256:(h + 1) * 256], op=ALU.mult),
                             reads=[ry, rsg], writes=[rtk])
                    S.dma("sp", tokb[r0:r0 + Cn, 0:1536], tk[:Cn, :], reads=[rtk], writes=[r_tokb])
                dst = o_ret_p[bl] if mode == "prompt" else o_ret_s[bl]
                for h in range(6):
                    S.dma("sp", dst[h * 256:(h + 1) * 256, :].rearrange("(dc p) v -> p dc v", p=128),
                          S32[:, h, :, :], reads=[r_S[h]])

    def mem_mix(layer, mk):
        with Scope() as sc:
            ab = AttnBufs(sc, 128, 256)
            qm_pool = sc.pool("qmt", 2, [128, 4, 128], BF16)
            om_pool = sc.pool("om", 2, [128, 512], BF16)
            tiles = [(t * 128, 128) for t in range(32)] + [(SEQ, T_S)]
            for (r0, nq) in tiles:
                samp = r0 >= SEQ
                KTt, rKT = (mk["sKT"], mk["r_sKT"]) if samp else (mk["pKT"], mk["r_pKT"])
                Vt, rV = (mk["sV"], mk["r_sV"]) if samp else (mk["pV"], mk["r_pV"])
                qm, rqm = qm_pool.get()
                S.dma("sp", qm[:, :, :nq], qmT[:, :, r0:r0 + nq].rearrange("h p n -> p h n"), reads=[r_qmT],
                      writes=[rqm])
                om, rom = om_pool.get()
                for h in range(4):
                    ps, rp = psA.get()
                    S.mm_group([lambda pe, h=h: pe.matmul(ps[:nq, :256], qm[:, h, :nq], KTt[:, h, :],
                                                          start=True, stop=True)], reads=[rqm, rKT], writes=[rp])
                    Ssb, rS = ab.S.get()
                    S.op("act", lambda a_: a_.activation(out=Ssb[:nq, :256], in_=ps[:nq, :256], func=AF.Identity,
                                                         scale=SCALE), reads=[rp], writes=[rS])

                    def out_fn(ps2, rp2, st, rst, h=h):
                        S.op("dve", lambda v: v.tensor_scalar(out=om[:nq, h * 128:(h + 1) * 128], in0=ps2[:nq, :128],
                                                              scalar1=st[:nq, 3:4], scalar2=None, op0=ALU.mult),
                             reads=[rp2, rst], writes=[rom])
                    softmax_pv(ab, nq, 256, Ssb, rS, lambda c, h=h: (Vt[:, c, h * 128:(h + 1) * 128], rV, 128), out_fn)
                S.dma("sp", tokb[r0:r0 + nq, 1536:2048], om[:nq, :], reads=[rom], writes=[r_tokb])

    def out_proj(layer):
        with Scope() as sc:
            Wo = sc.sb("Wo", [128, KC, D], BF16)
            r_Wo = Res()
            for cb in range(4):
                S.dma("act", Wo[:, :, cb * 512:(cb + 1) * 512],
                      wo_b[layer][:, cb * 512:(cb + 1) * 512].rearrange("(kc p) n -> p kc n", p=128), writes=[r_Wo])
            tk_pool = sc.pool("tkt", 2, [128, D], BF16)
            tT_pool = sc.pool("tokT", 2, [128, KC, 128], BF16)
            x_pool = sc.pool("xo", 2, [128, D], F32)
            tiles = [(t * 128, 128) for t in range(32)] + [(SEQ, T_S)]
            for (r0, P) in tiles:
                tk, rtk = tk_pool.get()
                S.dma("sp", tk[:P, :], tokb[r0:r0 + P, :], reads=[r_tokb], writes=[rtk])
                tT, rtT = tT_pool.get()
                transpose_into(tk, rtk, P, KC, lambda c0, n: tT[:, c0:c0 + n, :P], rtT)
                xt, rx = x_pool.get()
                S.dma("sp", xt[:P, :], x_rows(layer, r0, P), reads=[xres(r0)] if layer > 0 else [], writes=[rx])
                for cb in range(4):
                    ps, rp = psA.get()
                    S.mm_group([lambda pe, kc=kc, cb=cb: pe.matmul(ps[:P, :512], tT[:, kc, :P],
                                                                   Wo[:, kc, cb * 512:(cb + 1) * 512],
                                                                   start=(kc == 0), stop=(kc == KC - 1))
                                for kc in range(KC)], reads=[rtT, r_Wo], writes=[rp])
                    S.op("dve", lambda v, cb=cb: v.tensor_tensor(out=xt[:P, cb * 512:(cb + 1) * 512],
                                                                 in0=xt[:P, cb * 512:(cb + 1) * 512], in1=ps[:P, :512],
                                                                 op=ALU.add), reads=[rp], writes=[rx])
                S.dma("sp", xbuf[r0:r0 + P, :], xt[:P, :], reads=[rx], writes=[xres(r0)])
                if debug and layer == 0:
                    S.dma("sp", xmid_dbg[r0:r0 + P, :], xt[:P, :], reads=[rx])

    def _ffn(layer, last):
        with Scope() as sc:
            nb = NormBufs(sc)
            hT = sc.sb("hT", [128, KC, 512], BF16)
            r_hT = Res()
            uT = sc.sb("uT", [128, FC, 512], BF16)
            r_uT = Res()
            w_pool = sc.pool("wbuf", 2, [128, KC, 512], BF16)
            wo_pool = sc.pool("wobuf", 3, [128, 11, 512], BF16)
            cp = sc.sb("convp", [128, 4, FC], F32)
            carry = sc.sb("carry", [128, FC, 2], F32)
            r_cp = Res()
            r_carry = Res()
            S.dma("sp", cp[:], convp[layer], writes=[r_cp])
            a_pool = sc.pool("abuf", 2, [128, 514], F32)
            acc_pool = sc.pool("acc", 2, [128, 512], F32)
            gt, gr = load_gain(nb, norm2_g[layer:layer + 1, :])
            if last:
                gtf, grf = load_gain(nb, final_g[0:1, :])
            Win = fin_b[layer]
            Wout = fout_b[layer]
            for (r0, n) in BLOCKS:
                samp = r0 >= SEQ
                if r0 == 0:
                    S.op("pool", lambda g_: g_.memset(carry[:], 0.0), writes=[r_carry])
                if samp:
                    S.dma("sp", o_conv_p[layer], carry[:], reads=[r_carry])
                    S.dma("sp", carry[:], conv_state[layer], writes=[r_carry])
                for t0 in range(0, n, 128):
                    P = min(128, n - t0)
                    norm_tile(nb, xbuf[r0 + t0:r0 + t0 + P, :], [xres(r0)], P, gt, gr, t0, hT, r_hT)
                for j0 in range(0, FC, 2):
                    wt, rw = w_pool.get()
                    load_w(None, Win, j0 * 128, 256, dst_col=0, tile=(wt, rw))
                    load_w(None, Win, FFN + j0 * 128, 256, dst_col=256, tile=(wt, rw))
                    for jj in range(2):
                        j = j0 + jj
                        pa, rpa = psA.get()
                        S.mm_group([lambda pe, kc=kc, jj=jj: pe.matmul(pa[:, :n], wt[:, kc, jj * 128:(jj + 1) * 128],
                                                                       hT[:, kc, :n], start=(kc == 0),
                                                                       stop=(kc == KC - 1)) for kc in range(KC)],
                                   reads=[r_hT, rw], writes=[rpa])
                        pg, rpg = psA.get()
                        S.mm_group([lambda pe, kc=kc, jj=jj: pe.matmul(pg[:, :n],
                                                                       wt[:, kc, 256 + jj * 128:256 + (jj + 1) * 128],
                                                                       hT[:, kc, :n], start=(kc == 0),
                                                                       stop=(kc == KC - 1)) for kc in range(KC)],
                                   reads=[r_hT, rw], writes=[rpg])
                        ab_, rab = a_pool.get()
                        S.op("act", lambda a_, j=j: a_.copy(ab_[:, 2:2 + n], pa[:, :n]), reads=[rpa], writes=[rab])
                        S.op("act", lambda g_, j=j: g_.copy(ab_[:, 0:2], carry[:, j, :]), reads=[r_carry],
                             writes=[rab])
                        acc, racc = acc_pool.get()
                        S.op("dve", lambda v, j=j: v.tensor_scalar(out=acc[:, :n], in0=ab_[:, 2:2 + n],
                                                                   scalar1=cp[:, 2, j:j + 1], scalar2=cp[:, 3, j:j + 1],
                                                                   op0=ALU.mult, op1=ALU.add),
                             reads=[rab, r_cp], writes=[racc])
                        S.op("dve", lambda v, j=j: v.scalar_tensor_tensor(out=acc[:, :n], in0=ab_[:, 1:1 + n],
                                                                          scalar=cp[:, 1, j:j + 1], in1=acc[:, :n],
                                                                          op0=ALU.mult, op1=ALU.add),
                             reads=[rab, r_cp], writes=[racc])
                        S.op("dve", lambda v, j=j: v.scalar_tensor_tensor(out=acc[:, :n], in0=ab_[:, 0:n],
                                                                          scalar=cp[:, 0, j:j + 1], in1=acc[:, :n],
                                                                          op0=ALU.mult, op1=ALU.add),
                             reads=[rab, r_cp], writes=[racc])
                        S.op("act", lambda g_, j=j: g_.copy(carry[:, j, :], ab_[:, n:n + 2]), reads=[rab],
                             writes=[r_carry])
                        S.op("act", lambda a_: a_.activation(out=acc[:, :n], in_=acc[:, :n], func=AF.Silu),
                             reads=[racc], writes=[racc])
                        S.op("dve", lambda v, j=j: v.tensor_tensor(out=uT[:, j, :n], in0=acc[:, :n], in1=pg[:, :n],
                                                                   op=ALU.mult), reads=[racc, rpg], writes=[r_uT])
                if samp:
                    S.dma("sp", o_conv_s[layer], carry[:], reads=[r_carry])
                xtiles = [(t0, min(128, n - t0)) for t0 in range(0, n, 128)]
                for cb in range(4):
                    pss = [psA.get() for _ in xtiles]
                    for q4 in range(4):
                        wt, rw = wo_pool.get()
                        src = Wout[q4 * 11 * 128:(q4 + 1) * 11 * 128, cb * 512:(cb + 1) * 512].rearrange(
                            "(kc p) n -> p kc n", p=128)
                        S.dma("act", wt[:], src, writes=[rw])
                        for ti, (t0, P) in enumerate(xtiles):
                            ps, rp = pss[ti]
                            S.mm_group([lambda pe, kc=kc, q4=q4, t0=t0, P=P, ps=ps, wt=wt: pe.matmul(
                                ps[:P, :512], uT[:, q4 * 11 + kc, t0:t0 + P], wt[:, kc, :],
                                start=(q4 == 0 and kc == 0), stop=(q4 == 3 and kc == 10)) for kc in range(11)],
                                reads=[r_uT, rw], writes=[rp])
                    for ti, (t0, P) in enumerate(xtiles):
                        ps, rp = pss[ti]
                        stg, rs = acc_pool.get()
                        S.dma("sp", stg[:P, :], xbuf[r0 + t0:r0 + t0 + P, cb * 512:(cb + 1) * 512],
                              reads=[xres(r0)], writes=[rs])
                        S.op("dve", lambda v, ps=ps, P=P, stg=stg: v.tensor_tensor(out=stg[:P, :], in0=stg[:P, :],
                                                                                 in1=ps[:P, :512], op=ALU.add),
                             reads=[rp], writes=[rs])
                        S.dma("sp", xbuf[r0 + t0:r0 + t0 + P, cb * 512:(cb + 1) * 512], stg[:P, :], reads=[rs],
                              writes=[xres(r0)])
                if last:
                    for t0 in range(0, n, 128):
                        P = min(128, n - t0)
                        xt, rx = nb.xt.get()
                        S.dma("sp", xt[:P, :], xbuf[r0 + t0:r0 + t0 + P, :], reads=[xres(r0)], writes=[rx])
                        st, rs = rstd_of(xt, rx, P, nb)
                        S.op("dve", lambda v: v.scalar_tensor_tensor(out=xt[:P, :], in0=xt[:P, :], scalar=st[:P, 2:3],
                                                                     in1=gtf[:P, :], op0=ALU.mult, op1=ALU.mult),
                             reads=[rs, grf], writes=[rx])
                        dst = o_y_s[:, :] if samp else o_y_p[r0 + t0:r0 + t0 + P, :]
                        S.dma("sp", dst, xt[:P, :], reads=[rx])

    mkp = {}
    mkp["pKT"] = gsb("m_pKT", [128, 4, 256], BF16)
    mkp["pV"] = gsb("m_pV", [128, 2, 512], BF16)
    mkp["sKT"] = gsb("m_sKT", [128, 4, 256], BF16)
    mkp["sV"] = gsb("m_sV", [128, 2, 512], BF16)
    for k_ in ("pKT", "pV", "sKT", "sV"):
        mkp["r_" + k_] = Res(multi=True)

    def precast():
        with Scope() as sc:
            fpool = sc.pool("pc_f", 2, [128, 2 * FFN], F32)
            bpool = sc.pool("pc_b", 2, [128, 2 * FFN], BF16)
            k_ = [0]
            for (src, dst) in ((nsa_w_in, nsa_w_b), (w_mem_kv, mem_w_b), (w_o, wo_b), (ffn_w_in, fin_b),
                               (ffn_w_out, fout_b), (ret_w_in, ret_w_b)):
                s2 = src.rearrange("l r c -> (l r) c")
                d2 = dst.rearrange("l r c -> (l r) c")
                R_, C_ = s2.shape[0], s2.shape[1]
                for r0 in range(0, R_, 128):
                    ft, rf = fpool.get()
                    bt, rb = bpool.get()
                    S.dma("sp", ft[:, :C_], s2[r0:r0 + 128, :], writes=[rf])
                    k_[0] ^= 1
                    copy_op("act" if k_[0] else "dve", bt[:, :C_], ft[:, :C_], [rf], [rb])
                    S.dma("pool", d2[r0:r0 + 128, :], bt[:, :C_], reads=[rb])

    precast()
    for layer in range(nlayers):
        if layer % 2 == 0:
            phaseA_nsa(layer, mkp)
            nsa_mix_prompt(layer)
            nsa_mix_sample(layer)
        else:
            phaseA_ret(layer, mkp)
            ret_mix(layer)
        mem_mix(layer, mkp)
        out_proj(layer)
        ffn(layer, layer == nlayers - 1)

    S.finish()
    return nc, S.n_inst


_CACHE = {}


def kernel(**inp):
    if "nc" not in _CACHE:
        _CACHE["nc"] = build_program()
    nc, _ = _CACHE["nc"]
    f = lambda a: np.ascontiguousarray(np.asarray(a, dtype=np.float32))
    x_prompt = f(inp["x_prompt"])
    x_sample = f(inp["x_sample"])
    mem_prompt = f(inp["mem_prompt"])
    state_nsa_win = f(inp["state_nsa_win"])
    state_ret = f(inp["state_ret"])
    state_ffn_conv = f(inp["state_ffn_conv"])
    cache_mem_kv = f(inp["cache_mem_kv"])
    page_table = np.ascontiguousarray(np.asarray(inp["page_table"], dtype=np.int32))
    conv_w = f(inp["ffn_conv_w"])
    conv_b = f(inp["ffn_conv_b"])
    cpar = np.concatenate([conv_w, conv_b[:, None, :]], axis=1)
    cpar = np.ascontiguousarray(cpar.reshape(DEPTH, 4, FC, 128).transpose(0, 3, 1, 2))
    peT = np.ascontiguousarray(f(inp["nsa_cmp_pe"]).transpose(0, 1, 3, 2))
    shared = {
        "cache": f(inp["cache_nsa_kv"]).reshape(2, 1280 * 128, 2048),
        "norm1_g": f(inp["norm1_g"]), "norm2_g": f(inp["norm2_g"]), "mem_norm_g": f(inp["mem_norm_g"]),
        "final_g": f(inp["final_norm_g"]).reshape(1, D),
        "nsa_w_in": f(inp["nsa_w_in"]), "ret_w_in": f(inp["ret_w_in"]), "ret_gn": f(inp["ret_gn_g"]),
        "w_mem_kv": f(inp["w_mem_kv"]), "w_o": f(inp["w_o"]),
        "ffn_w_in": f(inp["ffn_w_in"]), "ffn_w_out": f(inp["ffn_w_out"]),
        "convp": cpar, "cmp_peT": peT, "cmp_w1": f(inp["nsa_cmp_w1"]), "cmp_w2": f(inp["nsa_cmp_w2"]),
    }
    for k_, v_ in make_tables().items():
        shared["t_" + k_] = v_
    in_maps = []
    for c in range(8):
        b = c // 4
        m = dict(shared)
        m["xp"] = x_prompt[b]
        m["xs"] = x_sample[c]
        m["memp"] = mem_prompt[b]
        m["win_state"] = np.ascontiguousarray(state_nsa_win[:, c].reshape(2, 512, 1024))
        m["ret_state"] = np.ascontiguousarray(state_ret[:, c].reshape(2, 1536, 256))
        m["conv_state"] = np.ascontiguousarray(
            state_ffn_conv[:, c].reshape(DEPTH, 2, FC, 128).transpose(0, 3, 2, 1))
        m["mem_cache"] = np.ascontiguousarray(cache_mem_kv[:, c].reshape(DEPTH, 256, 1024))
        m["page_tab"] = page_table[c:c + 1]
        in_maps.append(m)
    res = run_bass_kernel_spmd(nc, in_maps, core_ids=list(range(8)))
    R = res.results
    _CACHE["raw"] = R
    pc = [0, 4]
    y_prompt = np.stack([R[c]["o_y_p"] for c in pc])
    y_sample = np.stack([R[c]["o_y_s"] for c in range(8)])
    kv_p = np.stack([R[c]["o_kv_p"] for c in pc], axis=1).reshape(2, 2, SEQ, 4, 4, 128)
    kv_s = np.stack([R[c]["o_kv_s"] for c in range(8)], axis=1).reshape(2, 8, T_S, 4, 4, 128)
    win_p = np.stack([R[c]["o_win_p"] for c in pc], axis=1).reshape(2, 2, 512, 2, 4, 128)
    win_s = np.stack([R[c]["o_win_s"] for c in range(8)], axis=1).reshape(2, 8, 512, 2, 4, 128)
    ret_p = np.stack([R[c]["o_ret_p"] for c in pc], axis=1).reshape(2, 2, 6, 256, 256)
    ret_s = np.stack([R[c]["o_ret_s"] for c in range(8)], axis=1).reshape(2, 8, 6, 256, 256)

    def conv_out(a):
        return np.ascontiguousarray(a.transpose(0, 3, 2, 1).reshape(DEPTH, 2, FFN))
    conv_p = np.stack([conv_out(R[c]["o_conv_p"]) for c in pc], axis=1)
    conv_s = np.stack([conv_out(R[c]["o_conv_s"]) for c in range(8)], axis=1)
    mem_p = np.stack([R[c]["o_mem_p"] for c in pc], axis=1).reshape(DEPTH, 2, 256, 2, 4, 128)
    return (y_prompt, y_sample, kv_p, kv_s, win_p, win_s, ret_p, ret_s, conv_p, conv_s, mem_p)
```

```python
from contextlib import ExitStack
import numpy as np
import concourse.bass as bass
import concourse.mybir as mybir
from concourse.bass_utils import run_bass_kernel_spmd

F32 = mybir.dt.float32
BF16 = mybir.dt.bfloat16
I32 = mybir.dt.int32
AF = mybir.ActivationFunctionType
ALU = mybir.AluOpType
AX = mybir.AxisListType

D = 2048
SEQ = 4096
DEPTH = 4
T_S = 8
NTOK = SEQ + T_S
KC = 16
NSA_IN = 5156
RET_IN = 6656
FFN = 5632
FC = FFN // 128
EPS = 1e-6
NDMA = 24
NEG = -30000.0
PAST = 16384
NPAGE = 128
SCALE = 128 ** -0.5
LAYERS = list(range(DEPTH))


class Res:
    __slots__ = ("w", "r", "multi")

    def __init__(self, multi=False):
        self.w = {}
        self.r = {}
        self.multi = multi


class Sched:
    def __init__(self, nc):
        self.nc = nc
        self.eng = {"pe": nc.tensor, "act": nc.scalar, "dve": nc.vector,
                    "pool": nc.gpsimd, "sp": nc.sync}
        self.sem, self.cnt, self.known = {}, {}, {}
        for k in self.eng:
            self.sem[k] = nc.alloc_semaphore(name="sem_" + k)
            self.cnt[k] = 0
            self.known[k] = {}
        for i in range(NDMA):
            k = "d%d" % i
            self.sem[k] = nc.alloc_semaphore(name="sem_" + k)
            self.cnt[k] = 0
        self.rr = 0
        self.n_inst = 0
        self.qmap = {}

    def _wait(self, e, s, v):
        if s == "pe" and e == "pe":
            return
        if self.known[e].get(s, 0) >= v:
            return
        self.known[e][s] = v
        self.eng[e].wait_ge(self.sem[s], v)
        self.n_inst += 1

    def _deps(self, e, reads, writes):
        for r in reads:
            for s, v in r.w.items():
                self._wait(e, s, v)
        for w in writes:
            for s, v in w.r.items():
                self._wait(e, s, v)
            if not w.multi:
                for s, v in w.w.items():
                    self._wait(e, s, v)

    def _record(self, s, v, reads, writes):
        for r in reads:
            if r.r.get(s, 0) < v:
                r.r[s] = v
        for w in writes:
            if w.multi:
                if w.w.get(s, 0) < v:
                    w.w[s] = v
            else:
                w.w = {s: v}
            w.r = {}

    def op(self, e, fn, reads=(), writes=()):
        self._deps(e, reads, writes)
        inst = fn(self.eng[e])
        self.cnt[e] += 1
        inst.then_inc(self.sem[e], 1)
        self.n_inst += 1
        self._record(e, self.cnt[e], reads, writes)

    def mm_group(self, fns, reads=(), writes=()):
        self._deps("pe", reads, writes)
        inst = None
        for fn in fns:
            inst = fn(self.eng["pe"])
            self.n_inst += 1
        self.cnt["pe"] += 1
        inst.then_inc(self.sem["pe"], 1)
        self._record("pe", self.cnt["pe"], reads, writes)

    def dma(self, q, out, in_, reads=(), writes=(), fn=None):
        q = self.qmap.get(q, q)
        self._deps(q, reads, writes)
        i = self.rr
        self.rr = (i + 1) % NDMA
        k = "d%d" % i
        if self.cnt[k] > 0:
            self._wait(q, k, 16 * self.cnt[k])
        self.cnt[k] += 1
        if fn is None:
            inst = self.eng[q].dma_start(out=out, in_=in_)
        else:
            inst = fn(self.eng[q])
        inst.then_inc(self.sem[k], 16)
        self.n_inst += 1
        self._record(k, 16 * self.cnt[k], reads, writes)

    def barrier(self):
        for e in ("pe", "act", "dve", "pool", "sp"):
            for k, c in self.cnt.items():
                if c > 0 and k != e:
                    self._wait(e, k, 16 * c if k[1:].isdigit() else c)

    def finish(self):
        for i in range(NDMA):
            k = "d%d" % i
            if self.cnt[k] > 0:
                self._wait("sp", k, 16 * self.cnt[k])
        for e in ("pe", "act", "dve", "pool"):
            if self.cnt[e] > 0:
                self._wait("sp", e, self.cnt[e])


class Pool:
    def __init__(self, tiles):
        self.tiles = [(t, Res()) for t in tiles]
        self.i = 0

    def get(self):
        t = self.tiles[self.i]
        self.i = (self.i + 1) % len(self.tiles)
        return t


def gamma(h):
    return 1.0 - 2.0 ** (-5.0 - h)


def make_tables():
    T = {}
    T["ident"] = np.eye(128, dtype=np.float32)
    q = np.arange(SEQ)[:, None]
    n = np.arange(128)[None, :]
    T["cmpmask"] = np.where(n * 32 + 31 <= q, 0.0, NEG).astype(np.float32)
    blk = np.arange(64)[None, :]
    cur = q // 64
    valid = blk * 64 <= q
    forced = (blk == 0) | (blk == cur) | (blk == cur - 1)
    T["bonus"] = np.where(valid, np.where(forced, 1.0e4, 0.0), -1.0e30).astype(np.float32)
    p = np.arange(128)[:, None]
    k = np.arange(128)[None, :]
    T["tri_le"] = np.where(k <= p, 0.0, NEG).astype(np.float32)
    T["tri_gt"] = np.where(k > p, 0.0, NEG).astype(np.float32)
    sb = np.zeros((128, 264), np.float32)
    sb[:, [0, 255, 256]] = 1.0e4
    sb[:, 257:] = -1.0e30
    T["s_bonus"] = sb
    t = np.arange(128)[:, None]
    j = np.arange(8)[None, :]
    T["s_tri8"] = np.where(j <= t, 0.0, NEG).astype(np.float32)
    r = np.arange(512)[None, :]
    T["s_winmask"] = np.where(r > t, 0.0, NEG).astype(np.float32)
    pos = np.concatenate([np.arange(SEQ), PAST + np.arange(T_S)]).astype(np.float32)
    inv = (10000.0 ** (-np.arange(128, dtype=np.float32) / 128.0)).astype(np.float32)
    ang = (pos[None, :] * inv[:, None]).astype(np.float32)
    T["cosT"] = np.cos(ang).astype(np.float32)
    T["sinT"] = np.sin(ang).astype(np.float32)
    for C, tag in ((128, "128"), (8, "8")):
        i = np.arange(C, dtype=np.float64)
        dm = np.zeros((6, 128, C), np.float32)
        qd = np.zeros((6, 128, C), np.float32)
        kd = np.zeros((128, 6), np.float32)
        for h in range(6):
            lg = np.log1p(-2.0 ** (-5.0 - h))
            diff = i[None, :] - i[:, None]
            dm[h, :C, :] = np.where(diff >= 0, np.exp(np.maximum(diff, 0.0) * lg), 0.0)
            qd[h, :, :] = np.exp((i + 1.0) * lg)[None, :]
            kd[:C, h] = np.exp((C - 1.0 - i) * lg)
        T["dmT" + tag] = dm
        T["qd" + tag] = qd
        T["kd" + tag] = kd
    return T


def cdec(h, C):
    return float(np.exp(C * np.log1p(-2.0 ** (-5.0 - h))))


def build_program(nlayers=DEPTH, debug=False):
    nc = bass.Bass("TRN2", target_bir_lowering=False)
    S = Sched(nc)
    uid = [0]

    def nm(p):
        uid[0] += 1
        return "%s_%d" % (p, uid[0])

    def din(name, shape, dt=F32):
        return nc.dram_tensor(name, list(shape), dt, kind="ExternalInput").ap()

    def dout(name, shape, dt=F32):
        return nc.dram_tensor(name, list(shape), dt, kind="ExternalOutput").ap()

    def dscr(name, shape, dt):
        kind = "ExternalOutput" if (debug and name in ("xbuf", "tokb", "xmid_dbg")) else "Internal"
        return nc.dram_tensor(name, list(shape), dt, kind=kind).ap()

    xp = din("xp", [SEQ, D])
    xs = din("xs", [T_S, D])
    memp = din("memp", [256, D])
    win_state = din("win_state", [2, 512, 1024])
    ret_state = din("ret_state", [2, 1536, 256])
    conv_state = din("conv_state", [DEPTH, 128, FC, 2])
    mem_cache = din("mem_cache", [DEPTH, 256, 1024])
    cache = din("cache", [2, 1280 * 128, 2048])
    page_tab = din("page_tab", [1, NPAGE], I32)
    norm1_g = din("norm1_g", [DEPTH, D])
    norm2_g = din("norm2_g", [DEPTH, D])
    mem_norm_g = din("mem_norm_g", [DEPTH, D])
    final_g = din("final_g", [1, D])
    nsa_w_in = din("nsa_w_in", [2, D, NSA_IN])
    ret_w_in = din("ret_w_in", [2, D, RET_IN])
    ret_gn = din("ret_gn", [2, 1536])
    w_mem_kv = din("w_mem_kv", [DEPTH, D, 1024])
    w_o = din("w_o", [DEPTH, D, D])
    ffn_w_in = din("ffn_w_in", [DEPTH, D, 2 * FFN])
    ffn_w_out = din("ffn_w_out", [DEPTH, FFN, D])
    convp = din("convp", [DEPTH, 128, 4, FC])
    cmp_peT = din("cmp_peT", [2, 2, 128, 32])
    cmp_w1 = din("cmp_w1", [2, 2, 4096, 128])
    cmp_w2 = din("cmp_w2", [2, 2, 128, 128])
    tabs = {}
    TS = make_tables()
    for k_, v_ in TS.items():
        tabs[k_] = din("t_" + k_, list(v_.shape))

    o_y_p = dout("o_y_p", [SEQ, D])
    o_y_s = dout("o_y_s", [T_S, D])
    o_kv_p = dout("o_kv_p", [2, SEQ, 2048])
    o_kv_s = dout("o_kv_s", [2, T_S, 2048])
    o_win_p = dout("o_win_p", [2, 512, 1024])
    o_win_s = dout("o_win_s", [2, 512, 1024])
    o_ret_p = dout("o_ret_p", [2, 1536, 256])
    o_ret_s = dout("o_ret_s", [2, 1536, 256])
    o_conv_p = dout("o_conv_p", [DEPTH, 128, FC, 2])
    o_conv_s = dout("o_conv_s", [DEPTH, 128, FC, 2])
    o_mem_p = dout("o_mem_p", [DEPTH, 256, 1024])

    xbuf = dscr("xbuf", [NTOK, D], F32)
    r_xb = [Res(multi=True) for _ in range(9)]

    def xres(r0):
        return r_xb[min(r0 // 512, 8)]
    xmid_dbg = dscr("xmid_dbg", [NTOK, D], F32) if debug else None
    tokb = dscr("tokb", [NTOK, D], BF16)
    r_tokb = Res(multi=True)
    QT = dscr("QT", [12, 128, NTOK], BF16)
    r_QT = Res(multi=True)
    KT12 = dscr("KT12", [12, 128, NTOK], BF16)
    r_KT12 = Res(multi=True)
    rcT = dscr("rcT", [8, 128, SEQ], BF16)
    r_rcT = Res(multi=True)
    ksT = dscr("ksT", [4, 128, NTOK], BF16)
    r_ksT = Res(multi=True)
    kwT = dscr("kwT", [4, 128, NTOK], BF16)
    r_kwT = Res(multi=True)
    qmT = dscr("qmT", [4, 128, NTOK], BF16)
    r_qmT = Res(multi=True)
    gates = dscr("gates", [NTOK, 36], F32)
    r_gates = Res(multi=True)
    winr = dscr("winr", [NTOK, 1024], F32)
    r_winr = Res(multi=True)
    vtm = dscr("vtm", [NTOK, 1536], BF16)
    r_vtm = Res(multi=True)
    sgate = dscr("sgate", [NTOK, 1536], F32)
    r_sgate = Res(multi=True)
    r_okv = Res(multi=True)
    ksT_s = dscr("ksT_s", [4, 128, PAST], BF16)
    r_ksT_s = Res(multi=True)
    vs_s = dscr("vs_s", [4, PAST, 128], BF16)
    r_vs_s = Res(multi=True)

    nsa_w_b = dscr("nsa_w_b", [2, D, NSA_IN], BF16)
    ret_w_b = dscr("ret_w_b", [2, D, RET_IN], BF16)
    mem_w_b = dscr("mem_w_b", [DEPTH, D, 1024], BF16)
    wo_b = dscr("wo_b", [DEPTH, D, D], BF16)
    fin_b = dscr("fin_b", [DEPTH, D, 2 * FFN], BF16)
    fout_b = dscr("fout_b", [DEPTH, FFN, D], BF16)

    def gsb(name, shape, dt):
        return nc.alloc_sbuf_tensor(name, list(shape), dt)

    ident_f = gsb("ident_f", [128, 128], F32)
    ident_b = gsb("ident_b", [128, 128], BF16)
    r_ident = Res()
    psA = Pool([nc.alloc_psum_tensor("psA%d" % i, [128, 512], F32) for i in range(4)])
    psB = Pool([nc.alloc_psum_tensor("psB%d" % i, [128, 512], F32) for i in range(2)])
    psT = Pool([nc.alloc_psum_tensor("psT%d" % i, [128, 1024], BF16) for i in range(2)])
    st_pool = Pool([gsb("stat%d" % i, [128, 8], F32) for i in range(6)])

    evac_rr = [0]

    def evac_engine():
        evac_rr[0] ^= 1
        return "act" if evac_rr[0] else "dve"

    def copy_op(e, out, in_, reads, writes):
        if e == "act":
            S.op("act", lambda a: a.copy(out, in_), reads, writes)
        else:
            S.op(e, lambda v: v.tensor_copy(out, in_), reads, writes)

    S.dma("sp", ident_f[:], tabs["ident"], writes=[r_ident])
    S.op("dve", lambda v: v.tensor_copy(ident_b[:], ident_f[:]), reads=[r_ident], writes=[r_ident])

    class Scope:
        def __init__(self):
            self.es = ExitStack()

        def __enter__(self):
            self.es.__enter__()
            return self

        def __exit__(self, *a):
            if a[0] is None:
                S.barrier()
            return self.es.__exit__(*a)

        def sb(self, name, shape, dt):
            return self.es.enter_context(nc.sbuf_tensor(nm(name), list(shape), dt))

        def pool(self, name, n, shape, dt):
            return Pool([self.sb(name, shape, dt) for _ in range(n)])

    BLOCKS = [(b * 512, 512) for b in range(SEQ // 512)] + [(SEQ, T_S)]

    def x_rows(layer, r0, n):
        if layer == 0:
            return xp[r0:r0 + n, :] if r0 < SEQ else xs[r0 - SEQ:r0 - SEQ + n, :]
        return xbuf[r0:r0 + n, :]

    class NormBufs:
        def __init__(self, sc):
            self.g_bc = sc.pool("g_bc", 2, [128, D], F32)
            self.xt = sc.pool("xt", 2, [128, D], F32)
            self.hb = sc.pool("hb", 2, [128, D], BF16)
            self.junk = sc.sb("junk", [128, D], BF16)
            self.r_junk = Res()

    def load_gain(nb, g_row_ap):
        t, r = nb.g_bc.get()
        S.dma("sp", t[:], g_row_ap.to_broadcast([128, D]), writes=[r])
        return t, r

    def rstd_of(xt, rx, P, nb):
        st, rs = st_pool.get()
        S.op("act", lambda a: a.activation(out=nb.junk[:P, :], in_=xt[:P, :], func=AF.Square,
                                           accum_out=st[:P, 0:1]),
             reads=[rx], writes=[nb.r_junk, rs])
        S.op("act", lambda a: a.activation(out=st[:P, 1:2], in_=st[:P, 0:1], func=AF.Sqrt,
                                           scale=1.0 / D, bias=EPS),
             reads=[rs], writes=[rs])
        S.op("dve", lambda v: v.reciprocal(st[:P, 2:3], st[:P, 1:2]), reads=[rs], writes=[rs])
        return st, rs

    def transpose_into(src_bf, rsrc, P, nchunks, dst_fn, rdst):
        for c0 in range(0, nchunks, 8):
            n = min(8, nchunks - c0)
            pt, rp = psT.get()
            fns = []
            for j in range(n):
                fns.append(lambda pe, j=j: pe.transpose(
                    pt[:, j * 128:j * 128 + P], src_bf[:P, (c0 + j) * 128:(c0 + j + 1) * 128],
                    ident_b[:P, :P]))
            S.mm_group(fns, reads=[rsrc, r_ident], writes=[rp])
            src = pt[:].rearrange("p (j q) -> p j q", q=128)[:, :n, :P]
            copy_op(evac_engine(), dst_fn(c0, n), src, [rp], [rdst])

    def norm_tile(nb, x_src_ap, xsrc_res, P, gt, gr, col0, hT_t, r_hT_t):
        xt, rx = nb.xt.get()
        S.dma("sp", xt[:P, :], x_src_ap, reads=xsrc_res, writes=[rx])
        st, rs = rstd_of(xt, rx, P, nb)
        hb, rh = nb.hb.get()
        S.op("dve", lambda v: v.scalar_tensor_tensor(out=hb[:P, :], in0=xt[:P, :], scalar=st[:P, 2:3],
                                                     in1=gt[:P, :], op0=ALU.mult, op1=ALU.mult),
             reads=[rx, rs, gr], writes=[rh])
        transpose_into(hb, rh, P, KC, lambda c0, n: hT_t[:, c0:c0 + n, col0:col0 + P], r_hT_t)

    def load_w(w_pool, W2d, col0, ncols, dst_col=0, tile=None):
        if tile is None:
            wt, rw = w_pool.get()
        else:
            wt, rw = tile
        src = W2d[:, col0:col0 + ncols].rearrange("(kc p) n -> p kc n", p=128)
        S.dma("act", wt[:, :, dst_col:dst_col + ncols], src, writes=[rw])
        return wt, rw

    def linear_tm(w_pool, hT_t, r_hT_t, n, W2d, col0, ncols, sink):
        for cb0 in range(0, ncols, 512):
            cw = min(512, ncols - cb0)
            wt, rw = load_w(w_pool, W2d, col0 + cb0, cw)
            for t0 in range(0, n, 128):
                P = min(128, n - t0)
                ps, rp = psA.get()
                fns = []
                for kc in range(KC):
                    fns.append(lambda pe, kc=kc: pe.matmul(
                        ps[:P, :cw], hT_t[:, kc, t0:t0 + P], wt[:, kc, :cw],
                        start=(kc == 0), stop=(kc == KC - 1)))
                S.mm_group(fns, reads=[r_hT_t, rw], writes=[rp])
                sink(t0, P, cb0, cw, ps, rp)

    def linear_fm(w_pool, hT_t, r_hT_t, n, W2d, col0, nchunks, sink):
        for j0 in range(0, nchunks, 4):
            nj = min(4, nchunks - j0)
            wt, rw = load_w(w_pool, W2d, col0 + j0 * 128, nj * 128)
            for jj in range(nj):
                ps, rp = psA.get()
                fns = []
                for kc in range(KC):
                    fns.append(lambda pe, kc=kc: pe.matmul(
                        ps[:, :n], wt[:, kc, jj * 128:(jj + 1) * 128], hT_t[:, kc, :n],
                        start=(kc == 0), stop=(kc == KC - 1)))
                S.mm_group(fns, reads=[r_hT_t, rw], writes=[rp])
                sink(j0 + jj, ps, rp)

    def tm_sink_dram(stage_pool, dst2d, rdst, func=None, dt_stage=F32):
        def f(t0, P, cb0, cw, ps, rp):
            stg, rs = stage_pool.get()
            if func is None:
                copy_op(evac_engine(), stg[:P, :cw], ps[:P, :cw], [rp], [rs])
            else:
                S.op("act", lambda a: a.activation(out=stg[:P, :cw], in_=ps[:P, :cw], func=func),
                     reads=[rp], writes=[rs])
            S.dma("sp", dst2d[t0:t0 + P, cb0:cb0 + cw], stg[:P, :cw], reads=[rs], writes=rdst)
        return f

    def fm_sink_dram(stage_pool, dst3d, rdst, r0, n, add_tab=None):
        def f(j, ps, rp):
            stg, rs = stage_pool.get()
            if add_tab is None:
                copy_op(evac_engine(), stg[:, :n], ps[:, :n], [rp], [rs])
            else:
                tab, rt = add_tab(j)
                S.op("dve", lambda v: v.tensor_tensor(
                    out=stg[:, :n].rearrange("p (b j) -> p b j", j=32),
                    in0=ps[:, :n].rearrange("p (b j) -> p b j", j=32),
                    in1=tab.unsqueeze(1).to_broadcast([128, n // 32, 32]), op=ALU.add),
                    reads=[rp, rt], writes=[rs])
            S.dma("sp", dst3d[j, :, r0:r0 + n], stg[:, :n], reads=[rs], writes=rdst)
        return f

    def softmax_pv(ab, nq, nk, S_sb, rS, vchunk, out_fn):
        st, rst = st_pool.get()
        S.op("dve", lambda v: v.reduce_max(out=st[:nq, 0:1], in_=S_sb[:nq, :nk], axis=AX.X),
             reads=[rS], writes=[rst])
        S.op("dve", lambda v: v.tensor_scalar(out=st[:nq, 1:2], in0=st[:nq, 0:1], scalar1=-1.0e4,
                                              scalar2=-1.0, op0=ALU.max, op1=ALU.mult),
             reads=[rst], writes=[rst])
        P, rP = ab.P.get()
        S.op("act", lambda a: a.activation(out=P[:nq, :nk], in_=S_sb[:nq, :nk], func=AF.Exp,
                                           bias=st[:nq, 1:2], scale=1.0, accum_out=st[:nq, 2:3]),
             reads=[rS, rst], writes=[rP, rst])
        S.op("dve", lambda v: v.tensor_scalar(out=st[:nq, 3:4], in0=st[:nq, 2:3], scalar1=1.0e-30,
                                              scalar2=None, op0=ALU.max),
             reads=[rst], writes=[rst])
        S.op("dve", lambda v: v.reciprocal(st[:nq, 3:4], st[:nq, 3:4]), reads=[rst], writes=[rst])
        nch = (nk + 127) // 128
        PT, rPT = ab.PT.get()
        per_bank = 1024 // nq
        for c0 in range(0, nch, per_bank):
            n = min(per_bank, nch - c0)
            pt, rp = psT.get()
            fns = []
            for j in range(n):
                c = c0 + j
                kk = min(128, nk - c * 128)
                fns.append(lambda pe, j=j, c=c, kk=kk: pe.transpose(
                    pt[:kk, j * nq:(j + 1) * nq], P[:nq, c * 128:c * 128 + kk], ident_b[:nq, :nq]))
            S.mm_group(fns, reads=[rP, r_ident], writes=[rp])
            copy_op(evac_engine(), PT[:, c0 * nq:(c0 + n) * nq], pt[:, :n * nq], [rp], [rPT])
        ps, rp2 = psB.get()
        fns = []
        vres = []
        for c in range(nch):
            vap, vr, kk = vchunk(c)
            if vr not in vres:
                vres.append(vr)
            fns.append(lambda pe, c=c, vap=vap, kk=kk: pe.matmul(
                ps[:nq, :128], PT[:kk, c * nq:(c + 1) * nq], vap,
                start=(c == 0), stop=(c == nch - 1)))
        S.mm_group(fns, reads=[rPT] + vres, writes=[rp2])
        out_fn(ps, rp2, st, rst)

    class AttnBufs:
        def __init__(self, sc, nq, nkmax):
            self.S = sc.pool("S_sb", 2 if nq > 8 else 1, [nq, nkmax], F32)
            self.P = sc.pool("P_bf", 2 if nq > 8 else 1, [nq, nkmax], BF16)
            nch = (nkmax + 127) // 128
            self.PT = sc.pool("PT", 2 if nq > 8 else 1, [128, nch * nq], BF16)

    def topk_selneg(sc_pool, score, rscore, nq, nblk, selneg, rsel):
        st, rst = st_pool.get()
        m8, rm8 = sc_pool.get()
        S.op("dve", lambda v: v.max(out=m8[:nq, 0:8], in_=score[:nq, :nblk]),
             reads=[rscore], writes=[rm8])
        S.op("dve", lambda v: v.match_replace(out=m8[:nq, 16:16 + nblk], in_to_replace=m8[:nq, 0:8],
                                              in_values=score[:nq, :nblk], imm_value=-3.0e38),
             reads=[rscore, rm8], writes=[rm8])
        S.op("dve", lambda v: v.max(out=m8[:nq, 8:16], in_=m8[:nq, 16:16 + nblk]),
             reads=[rm8], writes=[rm8])
        S.op("dve", lambda v: v.tensor_scalar(out=selneg[:nq, :nblk], in0=score[:nq, :nblk],
                                              scalar1=m8[:nq, 15:16], scalar2=None, op0=ALU.is_ge),
             reads=[rscore, rm8], writes=[rsel])
        S.op("dve", lambda v: v.tensor_scalar(out=selneg[:nq, :nblk], in0=selneg[:nq, :nblk],
                                              scalar1=-1.0, scalar2=-NEG, op0=ALU.add, op1=ALU.mult),
             reads=[rsel], writes=[rsel])

    def mem_prepare(layer, mk, w_pool, nb, stage_pool, hT, r_hT):
        gt, gr = load_gain(nb, mem_norm_g[layer:layer + 1, :])
        for t in range(2):
            norm_tile(nb, memp[t * 128:(t + 1) * 128, :], [], 128, gt, gr, t * 128, hT, r_hT)

        def sink_tm(t0, P, cb0, cw, ps, rp):
            stg, rs = stage_pool.get()
            copy_op(evac_engine(), stg[:P, :cw], ps[:P, :cw], [rp], [rs])
            S.dma("sp", o_mem_p[layer][t0:t0 + P, cb0:cb0 + cw], stg[:P, :cw], reads=[rs])
            if cb0 == 512:
                S.op("pool", lambda g: g.tensor_copy(mk["pV"][:, t0 // 128, :], stg[:, :512]),
                     reads=[rs], writes=[mk["r_pV"]])
        linear_tm(w_pool, hT, r_hT, 256, mem_w_b[layer], 0, 1024, sink_tm)

        def sink_fm(j, ps, rp):
            copy_op(evac_engine(), mk["pKT"][:, j, :], ps[:, :256], [rp], [mk["r_pKT"]])
        linear_fm(w_pool, hT, r_hT, 256, mem_w_b[layer], 0, 4, sink_fm)
        for t in range(2):
            xt, rx = nb.xt.get()
            S.dma("sp", xt[:, :1024], mem_cache[layer][t * 128:(t + 1) * 128, :], writes=[rx])
            S.op("pool", lambda g: g.tensor_copy(mk["sV"][:, t, :], xt[:, 512:1024]),
                 reads=[rx], writes=[mk["r_sV"]])
            hb, rh = nb.hb.get()
            S.op("dve", lambda v: v.tensor_copy(hb[:, :512], xt[:, :512]), reads=[rx], writes=[rh])
            transpose_into(hb, rh, 128, 4,
                           lambda c0, n: mk["sKT"][:, c0:c0 + n, t * 128:(t + 1) * 128], mk["r_sKT"])

    WQ = {"act": "sp", "sp": "pool"}

    def phaseA_nsa(layer, mk):
        S.qmap = WQ
        try:
            _phaseA_nsa(layer, mk)
        finally:
            S.qmap = {}

    def phaseA_ret(layer, mk):
        S.qmap = WQ
        try:
            _phaseA_ret(layer, mk)
        finally:
            S.qmap = {}

    def ffn(layer, last):
        S.qmap = WQ
        try:
            _ffn(layer, last)
        finally:
            S.qmap = {}

    def _phaseA_nsa(layer, mk):
        a = layer // 2
        W = nsa_w_b[a]
        with Scope() as sc:
            nb = NormBufs(sc)
            hT = sc.sb("hT", [128, KC, 512], BF16)
            r_hT = Res()
            w_pool = sc.pool("wbuf", 5, [128, KC, 512], BF16)
            stage = sc.pool("stage", 4, [128, 512], F32)
            stage_b = sc.pool("stageb", 4, [128, 512], BF16)
            peT = sc.sb("peT", [128, 2, 32], F32)
            r_peT = Res()
            S.dma("sp", peT[:], cmp_peT[a].rearrange("t d j -> d t j"), writes=[r_peT])
            mem_prepare(layer, mk, w_pool, nb, stage, hT, r_hT)
            gt, gr = load_gain(nb, norm1_g[layer:layer + 1, :])
            for (r0, n) in BLOCKS:
                samp = r0 >= SEQ
                for t0 in range(0, n, 128):
                    P = min(128, n - t0)
                    norm_tile(nb, x_rows(layer, r0 + t0, P), [xres(r0)] if layer > 0 else [], P, gt, gr,
                              t0, hT, r_hT)
                okv = o_kv_s[a] if samp else o_kv_p[a][r0:r0 + n, :]
                linear_tm(w_pool, hT, r_hT, n, W, 1536, 2048, tm_sink_dram(stage, okv, [r_okv]))
                linear_tm(w_pool, hT, r_hT, n, W, 3584, 1024,
                          tm_sink_dram(stage, winr[r0:r0 + n, :], [r_winr]))
                linear_tm(w_pool, hT, r_hT, n, W, 4608, 36,
                          tm_sink_dram(stage, gates[r0:r0 + n, :], [r_gates], func=AF.Sigmoid))
                linear_fm(w_pool, hT, r_hT, n, W, 0, 12, fm_sink_dram(stage_b, QT, [r_QT], r0, n))
                if not samp:
                    linear_fm(w_pool, hT, r_hT, n, W, 1536, 8,
                              fm_sink_dram(stage_b, rcT, [r_rcT], r0, n,
                                           add_tab=lambda j: (peT[:, j // 4, :], r_peT)))
                linear_fm(w_pool, hT, r_hT, n, W, 2560, 4, fm_sink_dram(stage_b, ksT, [r_ksT], r0, n))
                linear_fm(w_pool, hT, r_hT, n, W, 3584, 4, fm_sink_dram(stage_b, kwT, [r_kwT], r0, n))
                linear_fm(w_pool, hT, r_hT, n, W, 4644, 4, fm_sink_dram(stage_b, qmT, [r_qmT], r0, n))
            S.dma("act", o_win_p[a], winr[SEQ - 512:SEQ, :], reads=[r_winr])
            S.dma("act", o_win_s[a][504:512, :], winr[SEQ:SEQ + T_S, :], reads=[r_winr])
            S.dma("act", o_win_s[a][0:504, :], win_state[a][8:512, :])

    def _phaseA_ret(layer, mk):
        bl = layer // 2
        W = ret_w_b[bl]
        with Scope() as sc:
            nb = NormBufs(sc)
            hT = sc.sb("hT", [128, KC, 512], BF16)
            r_hT = Res()
            w_pool = sc.pool("wbuf", 5, [128, KC, 512], BF16)
            stage = sc.pool("stage", 4, [128, 512], F32)
            stage_b = sc.pool("stageb", 4, [128, 512], BF16)
            tmp = sc.pool("rot", 4, [128, 512], F32)
            cosb = sc.sb("cosb", [128, 512], F32)
            sinb = sc.sb("sinb", [128, 512], F32)
            r_cs = Res()
            mem_prepare(layer, mk, w_pool, nb, stage, hT, r_hT)
            gt, gr = load_gain(nb, norm1_g[layer:layer + 1, :])
            for (r0, n) in BLOCKS:
                for t0 in range(0, n, 128):
                    P = min(128, n - t0)
                    norm_tile(nb, x_rows(layer, r0 + t0, P), [xres(r0)], P, gt, gr, t0, hT, r_hT)
                S.dma("sp", cosb[:, :n], tabs["cosT"][:, r0:r0 + n], writes=[r_cs])
                S.dma("sp", sinb[:, :n], tabs["sinT"][:, r0:r0 + n], writes=[r_cs])

                def rot_sink(dst, rdst, scl):
                    held = {}

                    def f(j, ps, rp):
                        if j % 2 == 0:
                            held["x1"] = (ps, rp)
                            return
                        p1, r1 = held["x1"]
                        p2, r2 = ps, rp
                        ta, ra = tmp.get()
                        tb, rb = tmp.get()
                        o1, ro1 = stage_b.get()
                        o2, ro2 = stage_b.get()
                        S.op("dve", lambda v: v.scalar_tensor_tensor(
                            out=ta[:, :n], in0=p1[:, :n], scalar=scl, in1=cosb[:, :n],
                            op0=ALU.mult, op1=ALU.mult), reads=[r1, r_cs], writes=[ra])
                        S.op("dve", lambda v: v.scalar_tensor_tensor(
                            out=tb[:, :n], in0=p2[:, :n], scalar=scl, in1=sinb[:, :n],
                            op0=ALU.mult, op1=ALU.mult), reads=[r2, r_cs], writes=[rb])
                        S.op("pool", lambda g: g.tensor_tensor(out=o1[:, :n], in0=ta[:, :n], in1=tb[:, :n],
                                                               op=ALU.subtract),
                             reads=[ra, rb], writes=[ro1])
                        tc_, rc_ = tmp.get()
                        td, rd = tmp.get()
                        S.op("dve", lambda v: v.scalar_tensor_tensor(
                            out=tc_[:, :n], in0=p1[:, :n], scalar=scl, in1=sinb[:, :n],
                            op0=ALU.mult, op1=ALU.mult), reads=[r1, r_cs], writes=[rc_])
                        S.op("dve", lambda v: v.scalar_tensor_tensor(
                            out=td[:, :n], in0=p2[:, :n], scalar=scl, in1=cosb[:, :n],
                            op0=ALU.mult, op1=ALU.mult), reads=[r2, r_cs], writes=[rd])
                        S.op("pool", lambda g: g.tensor_tensor(out=o2[:, :n], in0=tc_[:, :n], in1=td[:, :n],
                                                               op=ALU.add),
                             reads=[rc_, rd], writes=[ro2])
                        S.dma("sp", dst[j - 1, :, r0:r0 + n], o1[:, :n], reads=[ro1], writes=rdst)
                        S.dma("sp", dst[j, :, r0:r0 + n], o2[:, :n], reads=[ro2], writes=rdst)
                    return f
                linear_fm(w_pool, hT, r_hT, n, W, 0, 12, rot_sink(QT, [r_QT], 1.0))
                linear_fm(w_pool, hT, r_hT, n, W, 1536, 12, rot_sink(KT12, [r_KT12], 1.0 / 16.0))
                linear_fm(w_pool, hT, r_hT, n, W, 6144, 4, fm_sink_dram(stage_b, qmT, [r_qmT], r0, n))

                def v_sink(t0, P, cb0, cw, ps, rp):
                    stg, rs = stage_b.get()
                    copy_op(evac_engine(), stg[:P, :cw], ps[:P, :cw], [rp], [rs])
                    S.dma("sp", vtm[r0 + t0:r0 + t0 + P, cb0:cb0 + cw], stg[:P, :cw], reads=[rs],
                          writes=[r_vtm])
                linear_tm(w_pool, hT, r_hT, n, W, 3072, 1536, v_sink)
                linear_tm(w_pool, hT, r_hT, n, W, 4608, 1536,
                          tm_sink_dram(stage, sgate[r0:r0 + n, :], [r_sgate], func=AF.Silu))

    def gelu_tanh(sc_tmp, ps, rp, n, dst, rdst):
        x, rx = sc_tmp.get()
        u, ru = sc_tmp.get()
        copy_op("act", x[:, :n], ps[:, :n], [rp], [rx])
        S.op("dve", lambda v: v.tensor_tensor(out=u[:, :n], in0=x[:, :n], in1=x[:, :n], op=ALU.mult),
             reads=[rx], writes=[ru])
        S.op("dve", lambda v: v.tensor_scalar(out=u[:, :n], in0=u[:, :n], scalar1=0.044715, scalar2=1.0,
                                              op0=ALU.mult, op1=ALU.add), reads=[ru], writes=[ru])
        S.op("dve", lambda v: v.tensor_tensor(out=u[:, :n], in0=u[:, :n], in1=x[:, :n], op=ALU.mult),
             reads=[ru, rx], writes=[ru])
        S.op("act", lambda a: a.activation(out=u[:, :n], in_=u[:, :n], func=AF.Sigmoid,
                                           scale=2.0 * 0.7978845608028654),
             reads=[ru], writes=[ru])
        S.op("dve", lambda v: v.tensor_tensor(out=dst, in0=u[:, :n], in1=x[:, :n], op=ALU.mult),
             reads=[ru, rx], writes=[rdst])

    def compress(sc, a, ty, rc_ap, rrc, nblk, kcT_dst, vc_dst, rdst, w1t, w2t, rw, tmp):
        ps, rp = psA.get()
        rc3 = rc_ap.rearrange("p (b j) -> p j b", j=32)
        fns = []
        for j in range(32):
            fns.append(lambda pe, j=j: pe.matmul(ps[:, :nblk], w1t[:, ty, j, :], rc3[:, j, :],
                                                 start=(j == 0), stop=(j == 31)))
        S.mm_group(fns, reads=[rrc, rw], writes=[rp])
        gT, rg = tmp["g"].get()
        gelu_tanh(tmp["f"], ps, rp, nblk, gT[:, :nblk], rg)
        if ty == 0:
            ps2, rp2 = psA.get()
            S.mm_group([lambda pe: pe.matmul(ps2[:, :nblk], w2t[:, 0, :], gT[:, :nblk], start=True, stop=True)],
                       reads=[rg, rw], writes=[rp2])
            copy_op(evac_engine(), kcT_dst, ps2[:, :nblk], [rp2], [rdst])
        else:
            for c in range(nblk // 128):
                ps2, rp2 = psA.get()
                S.mm_group([lambda pe, c=c: pe.matmul(ps2[:, :128], gT[:, c * 128:(c + 1) * 128], w2t[:, 1, :],
                                                      start=True, stop=True)],
                           reads=[rg, rw], writes=[rp2])
                copy_op(evac_engine(), vc_dst(c), ps2[:, :128], [rp2], [rdst])

    def load_cmp_weights(sc, a):
        w1t = sc.sb("w1t", [128, 2, 32, 128], BF16)
        w2t = sc.sb("w2t", [128, 2, 128], BF16)
        rw = Res()
        for ty in range(2):
            S.dma("pool", w1t[:, ty, :, :], cmp_w1[a][ty].rearrange("(j d) o -> d j o", d=128), writes=[rw])
            S.dma("pool", w2t[:, ty, :], cmp_w2[a][ty], writes=[rw])
        return w1t, w2t, rw

    def nsa_mix_prompt(layer):
        a = layer // 2
        with Scope() as sc:
            ab = AttnBufs(sc, 128, 4096)
            cmpmask_t = sc.sb("cmpmask", [128, 32, 128], F32)
            bonus_t = sc.sb("bonus", [128, 32, 64], F32)
            tri_le = sc.sb("tri_le", [128, 128], F32)
            tri_gt = sc.sb("tri_gt", [128, 128], F32)
            gates_t = sc.sb("gates_t", [128, 32, 36], F32)
            r_tab = Res()
            S.dma("sp", cmpmask_t[:], tabs["cmpmask"].rearrange("(t p) n -> p t n", p=128), writes=[r_tab])
            S.dma("sp", bonus_t[:], tabs["bonus"].rearrange("(t p) n -> p t n", p=128), writes=[r_tab])
            S.dma("sp", tri_le[:], tabs["tri_le"], writes=[r_tab])
            S.dma("sp", tri_gt[:], tabs["tri_gt"], writes=[r_tab])
            S.dma("sp", gates_t[:], gates[0:SEQ, :].rearrange("(t p) c -> p t c", p=128),
                  reads=[r_gates], writes=[r_tab])
            kcT = sc.sb("kcT", [128, 4, 128], BF16)
            vc = sc.sb("vc", [128, 4, 128], BF16)
            r_kv = Res()
            with Scope() as scc:
                w1t, w2t, rw = load_cmp_weights(scc, a)
                tmp = {"g": scc.pool("gT", 2, [128, 512], BF16), "f": scc.pool("gf", 4, [128, 512], F32)}
                rcb = scc.pool("rcb", 2, [128, SEQ], BF16)
                for g in range(4):
                    for ty in range(2):
                        rc, rrc = rcb.get()
                        S.dma("sp", rc[:], rcT[ty * 4 + g], reads=[r_rcT], writes=[rrc])
                        compress(scc, a, ty, rc[:], rrc, 128, kcT[:, g, :], lambda c, g=g: vc[:, g, :], r_kv,
                                 w1t, w2t, rw, tmp)
            QTg = sc.sb("QTg", [128, 3, SEQ], BF16)
            ksTg = sc.sb("ksTg", [128, SEQ], BF16)
            kwTg = sc.sb("kwTg", [128, SEQ], BF16)
            vsg = sc.sb("vsg", [128, 32, 128], BF16)
            vwg = sc.sb("vwg", [128, 32, 128], BF16)
            r_grp = Res()
            small = sc.pool("small", 6, [128, 128], F32)
            pbf = sc.pool("pbf", 2, [128, 128], BF16)
            ptc = sc.pool("ptc", 2, [128, 128], BF16)
            sel_pool = sc.pool("selp", 2, [128, 64], F32)
            m8_pool = sc.pool("m8", 2, [128, 16 + 64], F32)
            ocomb_pool = sc.pool("ocomb", 2, [128, 384], F32)
            ob_pool = sc.pool("ob", 2, [128, 384], BF16)
            for g in range(4):
                S.dma("sp", QTg[:], QT[3 * g:3 * g + 3, :, 0:SEQ].rearrange("h p n -> p h n"),
                      reads=[r_QT], writes=[r_grp])
                S.dma("sp", ksTg[:], ksT[g, :, 0:SEQ], reads=[r_ksT], writes=[r_grp])
                S.dma("sp", kwTg[:], kwT[g, :, 0:SEQ], reads=[r_kwT], writes=[r_grp])
                S.dma("pool", vsg[:], o_kv_p[a][:, 1536 + g * 128:1536 + (g + 1) * 128].rearrange(
                    "(c p) d -> p c d", p=128), reads=[r_okv], writes=[r_grp])
                S.dma("pool", vwg[:], winr[0:SEQ, 512 + g * 128:512 + (g + 1) * 128].rearrange(
                    "(c p) d -> p c d", p=128), reads=[r_winr], writes=[r_grp])
                for i in range(32):
                    qs = slice(i * 128, (i + 1) * 128)
                    oc, roc = ocomb_pool.get()
                    pgrp, rpg = small.get()
                    first = [True]

                    def gated_out(col, hh, oc=oc, roc=roc, i=i):
                        h = 3 * g + hh
                        gcol = gates_t[:, i, h * 3 + col:h * 3 + col + 1]

                        def f(ps, rp, st, rst):
                            if st is not None:
                                S.op("dve", lambda v: v.tensor_tensor(out=st[:, 4:5], in0=st[:, 3:4], in1=gcol,
                                                                      op=ALU.mult),
                                     reads=[rst, r_tab], writes=[rst])
                                sc_ap, rr = st[:, 4:5], [rst]
                            else:
                                sc_ap, rr = gcol, [r_tab]
                            dst = oc[:, hh * 128:(hh + 1) * 128]
                            if col == 0:
                                S.op("dve", lambda v: v.tensor_scalar(out=dst, in0=ps[:, :128], scalar1=sc_ap,
                                                                      scalar2=None, op0=ALU.mult),
                                     reads=[rp] + rr, writes=[roc])
                            else:
                                S.op("dve", lambda v: v.scalar_tensor_tensor(
                                    out=dst, in0=ps[:, :128], scalar=sc_ap, in1=dst, op0=ALU.mult, op1=ALU.add),
                                    reads=[rp] + rr, writes=[roc])
                        return f
                    for hh in range(3):
                        ps, rp = psA.get()
                        S.mm_group([lambda pe, hh=hh: pe.matmul(ps[:, :128], QTg[:, hh, qs], kcT[:, g, :],
                                                                start=True, stop=True)],
                                   reads=[r_grp, r_kv], writes=[rp])
                        sc_t, rsc = small.get()
                        S.op("dve", lambda v: v.scalar_tensor_tensor(
                            out=sc_t[:], in0=ps[:, :128], scalar=SCALE, in1=cmpmask_t[:, i, :],
                            op0=ALU.mult, op1=ALU.add), reads=[rp, r_tab], writes=[rsc])
                        st, rst = st_pool.get()
                        S.op("dve", lambda v: v.reduce_max(out=st[:, 0:1], in_=sc_t[:], axis=AX.X),
                             reads=[rsc], writes=[rst])
                        S.op("dve", lambda v: v.tensor_scalar(out=st[:, 1:2], in0=st[:, 0:1], scalar1=-1.0e4,
                                                              scalar2=-1.0, op0=ALU.max, op1=ALU.mult),
                             reads=[rst], writes=[rst])
                        S.op("act", lambda a_: a_.activation(out=sc_t[:], in_=sc_t[:], func=AF.Exp,
                                                             bias=st[:, 1:2], scale=1.0, accum_out=st[:, 2:3]),
                             reads=[rsc, rst], writes=[rsc, rst])
                        S.op("dve", lambda v: v.tensor_scalar(out=st[:, 3:4], in0=st[:, 2:3], scalar1=1.0e-30,
                                                              scalar2=None, op0=ALU.max), reads=[rst], writes=[rst])
                        S.op("dve", lambda v: v.reciprocal(st[:, 3:4], st[:, 3:4]), reads=[rst], writes=[rst])
                        S.op("dve", lambda v: v.tensor_scalar(out=sc_t[:], in0=sc_t[:], scalar1=st[:, 3:4],
                                                              scalar2=None, op0=ALU.mult),
                             reads=[rsc, rst], writes=[rsc])
                        if hh == 0:
                            S.op("pool", lambda g_: g_.tensor_copy(pgrp[:], sc_t[:]), reads=[rsc], writes=[rpg])
                        else:
                            S.op("pool", lambda g_: g_.tensor_tensor(out=pgrp[:], in0=pgrp[:], in1=sc_t[:],
                                                                     op=ALU.add), reads=[rsc], writes=[rpg])
                        pb, rpb = pbf.get()
                        S.op("act", lambda a_: a_.copy(pb[:], sc_t[:]), reads=[rsc], writes=[rpb])
                        pt, rpt = psT.get()
                        S.mm_group([lambda pe: pe.transpose(pt[:, :128], pb[:], ident_b[:])],
                                   reads=[rpb, r_ident], writes=[rpt])
                        pc, rpc = ptc.get()
                        copy_op("act", pc[:], pt[:, :128], [rpt], [rpc])
                        ps2, rp2 = psB.get()
                        S.mm_group([lambda pe: pe.matmul(ps2[:, :128], pc[:], vc[:, g, :], start=True, stop=True)],
                                   reads=[rpc, r_kv], writes=[rp2])
                        gated_out(0, hh)(ps2, rp2, None, None)
                    score, rscore = sel_pool.get()
                    pg3 = pgrp[:].rearrange("p (b two) -> p b two", two=2)
                    S.op("dve", lambda v: v.tensor_tensor(out=score[:], in0=pg3[:, :, 0], in1=pg3[:, :, 1],
                                                          op=ALU.add), reads=[rpg], writes=[rscore])
                    S.op("dve", lambda v: v.tensor_tensor(out=score[:], in0=score[:], in1=bonus_t[:, i, :],
                                                          op=ALU.add), reads=[rscore, r_tab], writes=[rscore])
                    selneg, rsel = sel_pool.get()
                    topk_selneg(m8_pool, score, rscore, 128, 64, selneg, rsel)
                    for hh in range(3):
                        nk = (i + 1) * 128
                        Ssb, rS = ab.S.get()
                        for kb in range(0, nk, 512):
                            w = min(512, nk - kb)
                            ps, rp = psA.get()
                            S.mm_group([lambda pe, hh=hh, kb=kb, w=w: pe.matmul(
                                ps[:, :w], QTg[:, hh, qs], ksTg[:, kb:kb + w], start=True, stop=True)],
                                reads=[r_grp], writes=[rp])
                            nb_ = w // 64
                            S.op("dve", lambda v, kb=kb, w=w, nb_=nb_: v.scalar_tensor_tensor(
                                out=Ssb[:, kb:kb + w].rearrange("p (b k) -> p b k", k=64),
                                in0=ps[:, :w].rearrange("p (b k) -> p b k", k=64), scalar=SCALE,
                                in1=selneg[:, kb // 64:kb // 64 + nb_].unsqueeze(2).to_broadcast([128, nb_, 64]),
                                op0=ALU.mult, op1=ALU.add), reads=[rp, rsel], writes=[rS])
                        S.op("pool", lambda g_: g_.tensor_tensor(out=Ssb[:, i * 128:(i + 1) * 128],
                                                                 in0=Ssb[:, i * 128:(i + 1) * 128], in1=tri_le[:],
                                                                 op=ALU.add), reads=[r_tab], writes=[rS])
                        softmax_pv(ab, 128, nk, Ssb, rS, lambda c: (vsg[:, c, :], r_grp, 128), gated_out(1, hh))
                        c0 = max(0, i - 4)
                        nk = (i + 1 - c0) * 128
                        Ssb, rS = ab.S.get()
                        for kb in range(0, nk, 512):
                            w = min(512, nk - kb)
                            ps, rp = psA.get()
                            S.mm_group([lambda pe, hh=hh, kb=kb, w=w, c0=c0: pe.matmul(
                                ps[:, :w], QTg[:, hh, qs], kwTg[:, c0 * 128 + kb:c0 * 128 + kb + w],
                                start=True, stop=True)], reads=[r_grp], writes=[rp])
                            S.op("act", lambda a_, kb=kb, w=w: a_.activation(
                                out=Ssb[:, kb:kb + w], in_=ps[:, :w], func=AF.Identity, scale=SCALE),
                                reads=[rp], writes=[rS])
                        if i >= 4:
                            S.op("pool", lambda g_: g_.tensor_tensor(out=Ssb[:, 0:128], in0=Ssb[:, 0:128],
                                                                     in1=tri_gt[:], op=ALU.add),
                                 reads=[r_tab], writes=[rS])
                        S.op("pool", lambda g_, nk=nk: g_.tensor_tensor(out=Ssb[:, nk - 128:nk], in0=Ssb[:, nk - 128:nk],
                                                                        in1=tri_le[:], op=ALU.add),
                             reads=[r_tab], writes=[rS])
                        softmax_pv(ab, 128, nk, Ssb, rS, lambda c, c0=c0: (vwg[:, c0 + c, :], r_grp, 128),
                                   gated_out(2, hh))
                    ob, rob = ob_pool.get()
                    S.op("act", lambda a_: a_.copy(ob[:], oc[:]), reads=[roc], writes=[rob])
                    S.dma("sp", tokb[i * 128:(i + 1) * 128, g * 384:(g + 1) * 384], ob[:], reads=[rob],
                          writes=[r_tokb])

    def nsa_mix_sample(layer):
        a = layer // 2
        cache2d = cache.rearrange("a r c -> (a r) c")
        with Scope() as sc:
            kcT = sc.sb("kcT_s", [128, 4, 512], BF16)
            vc = sc.sb("vc_s", [128, 4, 4, 128], BF16)
            r_kv = Res()
            peT = sc.sb("peT_s", [128, 2, 32], F32)
            r_peT = Res()
            S.dma("sp", peT[:], cmp_peT[a].rearrange("t d j -> d t j"), writes=[r_peT])
            idx = sc.sb("idx", [128, NPAGE], I32)
            idxf = sc.sb("idxf", [128, NPAGE], F32)
            iot = sc.sb("iot", [128, 1], F32)
            r_idx = Res()
            S.dma("sp", idx[:], page_tab[0:1, :].to_broadcast([128, NPAGE]), writes=[r_idx])
            S.op("pool", lambda g_: g_.iota(iot[:], pattern=[[0, 1]], base=0, channel_multiplier=1,
                                            allow_small_or_imprecise_dtypes=True), writes=[r_idx])
            S.op("dve", lambda v: v.tensor_copy(idxf[:], idx[:]), reads=[r_idx], writes=[r_idx])
            S.op("dve", lambda v: v.tensor_scalar(out=idxf[:], in0=idxf[:], scalar1=128.0, scalar2=iot[:, 0:1],
                                                  op0=ALU.mult, op1=ALU.add), reads=[r_idx], writes=[r_idx])
            if a > 0:
                S.op("dve", lambda v: v.tensor_scalar(out=idxf[:], in0=idxf[:], scalar1=float(a * 1280 * 128),
                                                      scalar2=None, op0=ALU.add), reads=[r_idx], writes=[r_idx])
            S.op("dve", lambda v: v.tensor_copy(idx[:], idxf[:]), reads=[r_idx], writes=[r_idx])
            with Scope() as sc1:
                w1t, w2t, rw = load_cmp_weights(sc1, a)
                tmp = {"g": sc1.pool("gT", 2, [128, 512], BF16), "f": sc1.pool("gf", 4, [128, 512], F32)}
                page_pool = sc1.pool("page", 3, [128, 2048], F32)
                rcb = sc1.sb("rcb_s", [128, 8, 4096], BF16)
                r_rcb = Res()
                ksst = sc1.pool("ksst", 2, [128, 4, 512], BF16)
                vsst = sc1.pool("vsst", 2, [128, 4, 4, 128], BF16)
                psF = psB
                for batch in range(4):
                    for pq in range(8):
                        kst, rks = ksst.get()
                        vst, rvs = vsst.get()
                        for pp in range(4):
                            pg = batch * 32 + pq * 4 + pp
                            pt_, rpg = page_pool.get()
                            S.dma("pool", None, None, reads=[r_idx], writes=[rpg],
                                  fn=lambda g_, pt_=pt_, pg=pg: g_.indirect_dma_start(
                                      out=pt_[:, :], out_offset=None, in_=cache2d[:, :],
                                      in_offset=bass.IndirectOffsetOnAxis(ap=idx[:, pg:pg + 1], axis=0)))
                            S.op("pool", lambda g_, pt_=pt_, pp=pp: g_.tensor_copy(
                                vst[:, pp, :, :], pt_[:, 1536:2048].rearrange("p (g d) -> p g d", d=128)),
                                reads=[rpg], writes=[rvs])
                            for ty in range(3):
                                ps, rp = psF.get()
                                fns = []
                                for gg in range(4):
                                    col = (ty * 4 + gg) * 128
                                    fns.append(lambda pe, gg=gg, col=col, pt_=pt_: pe.transpose(
                                        ps[:, gg * 128:(gg + 1) * 128], pt_[:, col:col + 128], ident_f[:]))
                                S.mm_group(fns, reads=[rpg, r_ident], writes=[rp])
                                src = ps[:].rearrange("p (g r) -> p g r", r=128)
                                loc = (pq * 4 + pp) * 128
                                if ty < 2:
                                    S.op("dve", lambda v, ty=ty, loc=loc, src=src: v.tensor_tensor(
                                        out=rcb[:, ty * 4:(ty + 1) * 4, loc:loc + 128].rearrange(
                                            "p g (b j) -> p g b j", j=32),
                                        in0=src.rearrange("p g (b j) -> p g b j", j=32),
                                        in1=peT[:, ty, :].unsqueeze(1).unsqueeze(1).to_broadcast([128, 4, 4, 32]),
                                        op=ALU.add), reads=[rp, r_peT], writes=[r_rcb])
                                else:
                                    copy_op("act", kst[:, :, pp * 128:(pp + 1) * 128], src, [rp], [rks])
                        p0 = (batch * 32 + pq * 4) * 128
                        S.dma("sp", ksT_s[:, :, p0:p0 + 512].rearrange("g p n -> p g n"), kst[:], reads=[rks],
                              writes=[r_ksT_s])
                        for gg in range(4):
                            S.dma("sp", vs_s[gg, p0:p0 + 512, :].rearrange("(q r) d -> r q d", r=128),
                                  vst[:, :, gg, :], reads=[rvs], writes=[r_vs_s])
                    for g in range(4):
                        for ty in range(2):
                            compress(sc1, a, ty, rcb[:, ty * 4 + g, :], r_rcb, 128,
                                     kcT[:, g, batch * 128:(batch + 1) * 128],
                                     lambda c, g=g, batch=batch: vc[:, g, batch, :], r_kv, w1t, w2t, rw, tmp)
            with Scope() as sc2:
                NK = PAST + T_S
                ab = AttnBufs(sc2, T_S, NK)
                s_bonus = sc2.sb("s_bonus", [T_S, 264], F32)
                s_tri8 = sc2.sb("s_tri8", [T_S, 8], F32)
                s_winm = sc2.sb("s_winm", [T_S, 512], F32)
                gates_t = sc2.sb("gates_s", [T_S, 36], F32)
                r_tab = Res()
                S.dma("sp", s_bonus[:], tabs["s_bonus"][0:T_S, :], writes=[r_tab])
                S.dma("sp", s_tri8[:], tabs["s_tri8"][0:T_S, :], writes=[r_tab])
                S.dma("sp", s_winm[:], tabs["s_winmask"][0:T_S, :], writes=[r_tab])
                S.dma("sp", gates_t[:], gates[SEQ:NTOK, :], reads=[r_gates], writes=[r_tab])
                QTg = sc2.sb("QTg_s", [128, 3, T_S], BF16)
                ksTg = sc2.sb("ksTg_s", [128, NK], BF16)
                vsg = sc2.sb("vsg_s", [128, 129, 128], BF16)
                kwTg = sc2.sb("kwTg_s", [128, 520], BF16)
                vwg = sc2.sb("vwg_s", [128, 5, 128], BF16)
                wst = sc2.sb("wst", [128, 4, 128], F32)
                wsb = sc2.sb("wsb", [128, 512], BF16)
                r_ws = Res()
                r_grp = Res()
                small = sc2.pool("small_s", 3, [T_S, 512], F32)
                pgrp_t = sc2.sb("pgrp_s", [T_S, 512], F32)
                r_pgrp = Res()
                pbf = sc2.pool("pbf_s", 2, [T_S, 512], BF16)
                ptc = sc2.pool("ptc_s", 2, [128, 4 * T_S], BF16)
                sel_pool = sc2.pool("selp_s", 2, [T_S, 264], F32)
                m8_pool = sc2.pool("m8_s", 2, [T_S, 16 + 264], F32)
                oc = sc2.sb("ocomb_s", [T_S, 384], F32)
                roc = Res()
                ob = sc2.sb("ob_s", [T_S, 384], BF16)
                for g in range(4):
                    S.dma("sp", QTg[:], QT[3 * g:3 * g + 3, :, SEQ:NTOK].rearrange("h p n -> p h n"),
                          reads=[r_QT], writes=[r_grp])
                    S.dma("sp", ksTg[:, 0:PAST], ksT_s[g], reads=[r_ksT_s], writes=[r_grp])
                    S.dma("sp", ksTg[:, PAST:NK], ksT[g, :, SEQ:NTOK], reads=[r_ksT], writes=[r_grp])
                    S.dma("sp", vsg[:, 0:128, :], vs_s[g].rearrange("(c p) d -> p c d", p=128),
                          reads=[r_vs_s], writes=[r_grp])
                    S.dma("pool", vsg[:T_S, 128, :], o_kv_s[a][:, 1536 + g * 128:1536 + (g + 1) * 128],
                          reads=[r_okv], writes=[r_grp])
                    S.dma("sp", wst[:], win_state[a][:, g * 128:(g + 1) * 128].rearrange("(c p) d -> p c d", p=128),
                          writes=[r_ws])
                    S.op("dve", lambda v: v.tensor_copy(wsb[:], wst[:].rearrange("p c d -> p (c d)")),
                         reads=[r_ws], writes=[r_ws])
                    transpose_into(wsb, r_ws, 128, 4,
                                   lambda c0, n: kwTg[:, 0:512].rearrange("p (c r) -> p c r", r=128)[:, c0:c0 + n, :],
                                   r_grp)
                    S.dma("sp", kwTg[:, 512:520], kwT[g, :, SEQ:NTOK], reads=[r_kwT], writes=[r_grp])
                    S.dma("pool", vwg[:, 0:4, :],
                          win_state[a][:, 512 + g * 128:512 + (g + 1) * 128].rearrange("(c p) d -> p c d", p=128),
                          writes=[r_grp])
                    S.dma("pool", vwg[:T_S, 4, :], winr[SEQ:NTOK, 512 + g * 128:512 + (g + 1) * 128],
                          reads=[r_winr], writes=[r_grp])
                    pgrp, rpg = pgrp_t, r_pgrp

                    def gated_out(col, hh):
                        h = 3 * g + hh
                        gcol = gates_t[:, h * 3 + col:h * 3 + col + 1]

                        def f(ps, rp, st, rst):
                            if st is not None:
                                S.op("dve", lambda v: v.tensor_tensor(out=st[:T_S, 4:5], in0=st[:T_S, 3:4], in1=gcol,
                                                                      op=ALU.mult), reads=[rst, r_tab], writes=[rst])
                                sc_ap, rr = st[:T_S, 4:5], [rst]
                            else:
                                sc_ap, rr = gcol, [r_tab]
                            dst = oc[:, hh * 128:(hh + 1) * 128]
                            if col == 0:
                                S.op("dve", lambda v: v.tensor_scalar(out=dst, in0=ps[:T_S, :128], scalar1=sc_ap,
                                                                      scalar2=None, op0=ALU.mult),
                                     reads=[rp] + rr, writes=[roc])
                            else:
                                S.op("dve", lambda v: v.scalar_tensor_tensor(
                                    out=dst, in0=ps[:T_S, :128], scalar=sc_ap, in1=dst, op0=ALU.mult, op1=ALU.add),
                                    reads=[rp] + rr, writes=[roc])
                        return f
                    for hh in range(3):
                        ps, rp = psA.get()
                        S.mm_group([lambda pe, hh=hh: pe.matmul(ps[:T_S, :512], QTg[:, hh, :], kcT[:, g, :],
                                                                start=True, stop=True)],
                                   reads=[r_grp, r_kv], writes=[rp])
                        sc_t, rsc = small.get()
                        S.op("act", lambda a_: a_.activation(out=sc_t[:], in_=ps[:T_S, :512], func=AF.Identity,
                                                             scale=SCALE), reads=[rp], writes=[rsc])
                        st, rst = st_pool.get()
                        S.op("dve", lambda v: v.reduce_max(out=st[:T_S, 0:1], in_=sc_t[:], axis=AX.X),
                             reads=[rsc], writes=[rst])
                        S.op("dve", lambda v: v.tensor_scalar(out=st[:T_S, 1:2], in0=st[:T_S, 0:1], scalar1=-1.0,
                                                              scalar2=None, op0=ALU.mult), reads=[rst], writes=[rst])
                        S.op("act", lambda a_: a_.activation(out=sc_t[:], in_=sc_t[:], func=AF.Exp,
                                                             bias=st[:T_S, 1:2], scale=1.0, accum_out=st[:T_S, 2:3]),
                             reads=[rsc, rst], writes=[rsc, rst])
                        S.op("dve", lambda v: v.reciprocal(st[:T_S, 3:4], st[:T_S, 2:3]), reads=[rst], writes=[rst])
                        S.op("dve", lambda v: v.tensor_scalar(out=sc_t[:], in0=sc_t[:], scalar1=st[:T_S, 3:4],
                                                              scalar2=None, op0=ALU.mult),
                             reads=[rsc, rst], writes=[rsc])
                        if hh == 0:
                            S.op("pool", lambda g_: g_.tensor_copy(pgrp[:], sc_t[:]), reads=[rsc], writes=[rpg])
                        else:
                            S.op("pool", lambda g_: g_.tensor_tensor(out=pgrp[:], in0=pgrp[:], in1=sc_t[:],
                                                                     op=ALU.add), reads=[rsc], writes=[rpg])
                        pb, rpb = pbf.get()
                        S.op("act", lambda a_: a_.copy(pb[:], sc_t[:]), reads=[rsc], writes=[rpb])
                        pt, rpt = psT.get()
                        S.mm_group([lambda pe, c=c: pe.transpose(pt[:, c * T_S:(c + 1) * T_S],
                                                                 pb[:, c * 128:(c + 1) * 128], ident_b[:T_S, :T_S])
                                    for c in range(4)], reads=[rpb, r_ident], writes=[rpt])
                        pc, rpc = ptc.get()
                        copy_op("act", pc[:], pt[:, :4 * T_S], [rpt], [rpc])
                        ps2, rp2 = psB.get()
                        S.mm_group([lambda pe, c=c: pe.matmul(ps2[:T_S, :128], pc[:, c * T_S:(c + 1) * T_S],
                                                              vc[:, g, c, :], start=(c == 0), stop=(c == 3))
                                    for c in range(4)], reads=[rpc, r_kv], writes=[rp2])
                        gated_out(0, hh)(ps2, rp2, None, None)
                    score, rscore = sel_pool.get()
                    pg3 = pgrp[:].rearrange("p (b two) -> p b two", two=2)
                    S.op("dve", lambda v: v.tensor_copy(score[:], s_bonus[:]), reads=[r_tab], writes=[rscore])
                    S.op("dve", lambda v: v.tensor_tensor(out=score[:, 0:256], in0=score[:, 0:256], in1=pg3[:, :, 0],
                                                          op=ALU.add), reads=[rpg], writes=[rscore])
                    S.op("dve", lambda v: v.tensor_tensor(out=score[:, 0:256], in0=score[:, 0:256], in1=pg3[:, :, 1],
                                                          op=ALU.add), reads=[rpg], writes=[rscore])
                    selneg, rsel = sel_pool.get()
                    topk_selneg(m8_pool, score, rscore, T_S, 264, selneg, rsel)
                    for hh in range(3):
                        Ssb, rS = ab.S.get()
                        for kb in range(0, PAST, 512):
                            ps, rp = psA.get()
                            S.mm_group([lambda pe, hh=hh, kb=kb: pe.matmul(
                                ps[:T_S, :512], QTg[:, hh, :], ksTg[:, kb:kb + 512], start=True, stop=True)],
                                reads=[r_grp], writes=[rp])
                            S.op("dve", lambda v, kb=kb: v.scalar_tensor_tensor(
                                out=Ssb[:, kb:kb + 512].rearrange("p (b k) -> p b k", k=64),
                                in0=ps[:T_S, :512].rearrange("p (b k) -> p b k", k=64), scalar=SCALE,
                                in1=selneg[:, kb // 64:kb // 64 + 8].unsqueeze(2).to_broadcast([T_S, 8, 64]),
                                op0=ALU.mult, op1=ALU.add), reads=[rp, rsel], writes=[rS])
                        ps, rp = psA.get()
                        S.mm_group([lambda pe, hh=hh: pe.matmul(ps[:T_S, :T_S], QTg[:, hh, :], ksTg[:, PAST:NK],
                                                                start=True, stop=True)], reads=[r_grp], writes=[rp])
                        S.op("dve", lambda v: v.scalar_tensor_tensor(
                            out=Ssb[:, PAST:NK], in0=ps[:T_S, :T_S], scalar=SCALE, in1=s_tri8[:],
                            op0=ALU.mult, op1=ALU.add), reads=[rp, r_tab], writes=[rS])
                        S.op("dve", lambda v: v.tensor_scalar(out=Ssb[:, PAST:NK], in0=Ssb[:, PAST:NK],
                                                              scalar1=selneg[:, 256:257], scalar2=None, op0=ALU.add),
                             reads=[rsel], writes=[rS])
                        softmax_pv(ab, T_S, NK, Ssb, rS,
                                   lambda c: (vsg[:, c, :], r_grp, 128) if c < 128 else (vsg[:T_S, 128, :], r_grp, T_S),
                                   gated_out(1, hh))
                        Ssb, rS = ab.S.get()
                        ps, rp = psA.get()
                        S.mm_group([lambda pe, hh=hh: pe.matmul(ps[:T_S, :512], QTg[:, hh, :], kwTg[:, 0:512],
                                                                start=True, stop=True)], reads=[r_grp], writes=[rp])
                        S.op("dve", lambda v: v.scalar_tensor_tensor(
                            out=Ssb[:, 0:512], in0=ps[:T_S, :512], scalar=SCALE, in1=s_winm[:],
                            op0=ALU.mult, op1=ALU.add), reads=[rp, r_tab], writes=[rS])
                        ps, rp = psA.get()
                        S.mm_group([lambda pe, hh=hh: pe.matmul(ps[:T_S, :T_S], QTg[:, hh, :], kwTg[:, 512:520],
                                                                start=True, stop=True)], reads=[r_grp], writes=[rp])
                        S.op("dve", lambda v: v.scalar_tensor_tensor(
                            out=Ssb[:, 512:520], in0=ps[:T_S, :T_S], scalar=SCALE, in1=s_tri8[:],
                            op0=ALU.mult, op1=ALU.add), reads=[rp, r_tab], writes=[rS])
                        softmax_pv(ab, T_S, 520, Ssb, rS,
                                   lambda c: (vwg[:, c, :], r_grp, 128) if c < 4 else (vwg[:T_S, 4, :], r_grp, T_S),
                                   gated_out(2, hh))
                    S.op("act", lambda a_: a_.copy(ob[:], oc[:]), reads=[roc], writes=[roc])
                    S.dma("sp", tokb[SEQ:NTOK, g * 384:(g + 1) * 384], ob[:], reads=[roc], writes=[r_tokb])

    def ret_mix(layer):
        bl = layer // 2
        with Scope() as sc:
            S32 = sc.sb("S32", [128, 6, 2, 256], F32)
            Sb = sc.sb("Sb", [128, 6, 2, 256], BF16)
            r_S = [Res() for _ in range(6)]
            gn_bc = sc.sb("gn_bc", [128, 1536], F32)
            r_gn = Res()
            S.dma("sp", gn_bc[:], ret_gn[bl:bl + 1, :].to_broadcast([128, 1536]), writes=[r_gn])
            qc_pool = sc.pool("qc", 2, [128, 12, 128], BF16)
            kc_pool = sc.pool("kc", 2, [128, 12, 128], BF16)
            v_pool = sc.pool("vch", 2, [128, 1536], BF16)
            sg_pool = sc.pool("sgch", 2, [128, 1536], F32)
            qd_pool = sc.pool("qdT", 2, [128, 2, 128], BF16)
            in_pool = sc.pool("inT", 2, [128, 128], BF16)
            kd_pool = sc.pool("kd", 2, [128, 256], BF16)
            y_pool = sc.pool("yf", 2, [128, 256], F32)
            tok_pool = sc.pool("tokc", 2, [128, 1536], BF16)
            bn_pool = sc.pool("bn", 4, [128, 8], F32)
            for mode in ("prompt", "sample"):
                C = 128 if mode == "prompt" else T_S
                tag = "128" if mode == "prompt" else "8"
                dmT = sc.sb("dmT" + tag, [128, 6, C], F32)
                qdt = sc.sb("qd" + tag, [128, 6, C], F32)
                kdt = sc.sb("kd" + tag, [128, 6], F32)
                r_dt = Res()
                S.dma("sp", dmT[:], tabs["dmT" + tag].rearrange("h j i -> j h i"), writes=[r_dt])
                S.dma("sp", qdt[:], tabs["qd" + tag].rearrange("h j i -> j h i"), writes=[r_dt])
                S.dma("sp", kdt[:], tabs["kd" + tag], writes=[r_dt])
                if mode == "prompt":
                    for h in range(6):
                        S.op("pool", lambda g_, h=h: g_.memset(S32[:, h, :, :], 0.0), writes=[r_S[h]])
                        S.op("pool", lambda g_, h=h: g_.memset(Sb[:, h, :, :], 0.0), writes=[r_S[h]])
                    chunks = [(c * 128, 128) for c in range(32)]
                else:
                    for h in range(6):
                        S.dma("sp", S32[:, h, :, :],
                              ret_state[bl][h * 256:(h + 1) * 256, :].rearrange("(dc p) v -> p dc v", p=128),
                              writes=[r_S[h]])
                        S.op("act", lambda a_, h=h: a_.copy(Sb[:, h, :, :], S32[:, h, :, :]), reads=[r_S[h]],
                             writes=[r_S[h]])
                    chunks = [(SEQ, T_S)]
                for (r0, Cn) in chunks:
                    qc, rq = qc_pool.get()
                    kc, rk = kc_pool.get()
                    vch, rv = v_pool.get()
                    sg, rsg = sg_pool.get()
                    S.dma("sp", qc[:, :, :Cn], QT[:, :, r0:r0 + Cn].rearrange("j p n -> p j n"), reads=[r_QT],
                          writes=[rq])
                    S.dma("sp", kc[:, :, :Cn], KT12[:, :, r0:r0 + Cn].rearrange("j p n -> p j n"), reads=[r_KT12],
                          writes=[rk])
                    S.dma("sp", vch[:Cn, :], vtm[r0:r0 + Cn, :], reads=[r_vtm], writes=[rv])
                    S.dma("sp", sg[:Cn, :], sgate[r0:r0 + Cn, :], reads=[r_sgate], writes=[rsg])
                    tk, rtk = tok_pool.get()
                    for h in range(6):
                        ps, rp = psA.get()
                        S.mm_group([lambda pe, dc=dc, h=h: pe.matmul(ps[:Cn, :Cn], kc[:, 2 * h + dc, :Cn],
                                                                     qc[:, 2 * h + dc, :Cn], start=(dc == 0),
                                                                     stop=(dc == 1)) for dc in range(2)],
                                   reads=[rq, rk], writes=[rp])
                        inT, rin = in_pool.get()
                        S.op("dve", lambda v, h=h: v.tensor_tensor(out=inT[:Cn, :Cn], in0=ps[:Cn, :Cn],
                                                                   in1=dmT[:Cn, h, :Cn], op=ALU.mult),
                             reads=[rp, r_dt], writes=[rin])
                        qd, rqd = qd_pool.get()
                        S.op("pool", lambda g_, h=h: g_.tensor_tensor(
                            out=qd[:, :, :Cn], in0=qc[:, 2 * h:2 * h + 2, :Cn],
                            in1=qdt[:, h, :Cn].unsqueeze(1).to_broadcast([128, 2, Cn]), op=ALU.mult),
                            reads=[rq, r_dt], writes=[rqd])
                        po, rpo = psB.get()
                        fns = [lambda pe, h=h: pe.matmul(po[:Cn, :256], inT[:Cn, :Cn], vch[:Cn, h * 256:(h + 1) * 256],
                                                         start=True, stop=False)]
                        for dc in range(2):
                            fns.append(lambda pe, dc=dc, h=h: pe.matmul(po[:Cn, :256], qd[:, dc, :Cn], Sb[:, h, dc, :],
                                                                        start=False, stop=(dc == 1)))
                        S.mm_group(fns, reads=[rin, rv, rqd, r_S[h]], writes=[rpo])
                        pt, rpt = psT.get()
                        S.mm_group([lambda pe, dc=dc, h=h: pe.transpose(pt[:Cn, dc * 128:(dc + 1) * 128],
                                                                        kc[:, 2 * h + dc, :Cn], ident_b[:, :])
                                    for dc in range(2)], reads=[rk, r_ident], writes=[rpt])
                        kd, rkd = kd_pool.get()
                        S.op("dve", lambda v, h=h: v.tensor_scalar(out=kd[:Cn, :], in0=pt[:Cn, :256],
                                                                   scalar1=kdt[:Cn, h:h + 1], scalar2=None,
                                                                   op0=ALU.mult), reads=[rpt, r_dt], writes=[rkd])
                        for dc in range(2):
                            pss, rps = psA.get()
                            S.mm_group([lambda pe, dc=dc, h=h: pe.matmul(
                                pss[:, :256], kd[:Cn, dc * 128:(dc + 1) * 128], vch[:Cn, h * 256:(h + 1) * 256],
                                start=True, stop=True)], reads=[rkd, rv], writes=[rps])
                            S.op("dve", lambda v, dc=dc, h=h: v.scalar_tensor_tensor(
                                out=S32[:, h, dc, :], in0=S32[:, h, dc, :], scalar=cdec(h, C), in1=pss[:, :256],
                                op0=ALU.mult, op1=ALU.add), reads=[rps], writes=[r_S[h]])
                        S.op("act", lambda a_, h=h: a_.copy(Sb[:, h, :, :], S32[:, h, :, :]), reads=[],
                             writes=[r_S[h]])
                        bn, rbn = bn_pool.get()
                        S.op("dve", lambda v: v.bn_stats(out=bn[:Cn, 0:6], in_=po[:Cn, :256]), reads=[rpo],
                             writes=[rbn])
                        S.op("dve", lambda v: v.bn_aggr(out=bn[:Cn, 6:8], in_=bn[:Cn, 0:6]), reads=[rbn],
                             writes=[rbn])
                        S.op("act", lambda a_: a_.activation(out=bn[:Cn, 0:1], in_=bn[:Cn, 7:8], func=AF.Sqrt,
                                                             scale=1.0, bias=EPS), reads=[rbn], writes=[rbn])
                        S.op("dve", lambda v: v.reciprocal(bn[:Cn, 1:2], bn[:Cn, 0:1]), reads=[rbn], writes=[rbn])
                        yf, ry = y_pool.get()
                        S.op("dve", lambda v: v.tensor_scalar(out=yf[:Cn, :], in0=po[:Cn, :256], scalar1=bn[:Cn, 6:7],
                                                              scalar2=bn[:Cn, 1:2], op0=ALU.subtract, op1=ALU.mult),
                             reads=[rpo, rbn], writes=[ry])
                        S.op("pool", lambda g_, h=h: g_.tensor_tensor(out=yf[:Cn, :], in0=yf[:Cn, :],
                                                                      in1=gn_bc[:Cn, h * 256:(h + 1) * 256],
                                                                      op=ALU.mult), reads=[r_gn], writes=[ry])
                        S.op("pool", lambda g_, h=h: g_.tensor_tensor(out=tk[:Cn, h * 256:(h + 1) * 256],
                                                                      in0=yf[:Cn, :],
                                                                      in1=sg[:Cn, h * 256:(h + 1) * 256], op=ALU.mult),
                             reads=[ry, rsg], writes=[rtk])
                    S.dma("sp", tokb[r0:r0 + Cn, 0:1536], tk[:Cn, :], reads=[rtk], writes=[r_tokb])
                dst = o_ret_p[bl] if mode == "prompt" else o_ret_s[bl]
                for h in range(6):
                    S.dma("sp", dst[h * 256:(h + 1) * 256, :].rearrange("(dc p) v -> p dc v", p=128),
                          S32[:, h, :, :], reads=[r_S[h]])

    def mem_mix(layer, mk):
        with Scope() as sc:
            ab = AttnBufs(sc, 128, 256)
            qm_pool = sc.pool("qmt", 2, [128, 4, 128], BF16)
            om_pool = sc.pool("om", 2, [128, 512], BF16)
            tiles = [(t * 128, 128) for t in range(32)] + [(SEQ, T_S)]
            for (r0, nq) in tiles:
                samp = r0 >= SEQ
                KTt, rKT = (mk["sKT"], mk["r_sKT"]) if samp else (mk["pKT"], mk["r_pKT"])
                Vt, rV = (mk["sV"], mk["r_sV"]) if samp else (mk["pV"], mk["r_pV"])
                qm, rqm = qm_pool.get()
                S.dma("sp", qm[:, :, :nq], qmT[:, :, r0:r0 + nq].rearrange("h p n -> p h n"), reads=[r_qmT],
                      writes=[rqm])
                om, rom = om_pool.get()
                for h in range(4):
                    ps, rp = psA.get()
                    S.mm_group([lambda pe, h=h: pe.matmul(ps[:nq, :256], qm[:, h, :nq], KTt[:, h, :],
                                                          start=True, stop=True)], reads=[rqm, rKT], writes=[rp])
                    Ssb, rS = ab.S.get()
                    S.op("act", lambda a_: a_.activation(out=Ssb[:nq, :256], in_=ps[:nq, :256], func=AF.Identity,
                                                         scale=SCALE), reads=[rp], writes=[rS])

                    def out_fn(ps2, rp2, st, rst, h=h):
                        S.op("dve", lambda v: v.tensor_scalar(out=om[:nq, h * 128:(h + 1) * 128], in0=ps2[:nq, :128],
                                                              scalar1=st[:nq, 3:4], scalar2=None, op0=ALU.mult),
                             reads=[rp2, rst], writes=[rom])
                    softmax_pv(ab, nq, 256, Ssb, rS, lambda c, h=h: (Vt[:, c, h * 128:(h + 1) * 128], rV, 128), out_fn)
                S.dma("sp", tokb[r0:r0 + nq, 1536:2048], om[:nq, :], reads=[rom], writes=[r_tokb])

    def out_proj(layer):
        with Scope() as sc:
            Wo = sc.sb("Wo", [128, KC, D], BF16)
            r_Wo = Res()
            for cb in range(4):
                S.dma("act", Wo[:, :, cb * 512:(cb + 1) * 512],
                      wo_b[layer][:, cb * 512:(cb + 1) * 512].rearrange("(kc p) n -> p kc n", p=128), writes=[r_Wo])
            tk_pool = sc.pool("tkt", 2, [128, D], BF16)
            tT_pool = sc.pool("tokT", 2, [128, KC, 128], BF16)
            x_pool = sc.pool("xo", 2, [128, D], F32)
            tiles = [(t * 128, 128) for t in range(32)] + [(SEQ, T_S)]
            for (r0, P) in tiles:
                tk, rtk = tk_pool.get()
                S.dma("sp", tk[:P, :], tokb[r0:r0 + P, :], reads=[r_tokb], writes=[rtk])
                tT, rtT = tT_pool.get()
                transpose_into(tk, rtk, P, KC, lambda c0, n: tT[:, c0:c0 + n, :P], rtT)
                xt, rx = x_pool.get()
                S.dma("sp", xt[:P, :], x_rows(layer, r0, P), reads=[xres(r0)] if layer > 0 else [], writes=[rx])
                for cb in range(4):
                    ps, rp = psA.get()
                    S.mm_group([lambda pe, kc=kc, cb=cb: pe.matmul(ps[:P, :512], tT[:, kc, :P],
                                                                   Wo[:, kc, cb * 512:(cb + 1) * 512],
                                                                   start=(kc == 0), stop=(kc == KC - 1))
                                for kc in range(KC)], reads=[rtT, r_Wo], writes=[rp])
                    S.op("dve", lambda v, cb=cb: v.tensor_tensor(out=xt[:P, cb * 512:(cb + 1) * 512],
                                                                 in0=xt[:P, cb * 512:(cb + 1) * 512], in1=ps[:P, :512],
                                                                 op=ALU.add), reads=[rp], writes=[rx])
                S.dma("sp", xbuf[r0:r0 + P, :], xt[:P, :], reads=[rx], writes=[xres(r0)])
                if debug and layer == 0:
                    S.dma("sp", xmid_dbg[r0:r0 + P, :], xt[:P, :], reads=[rx])

    def _ffn(layer, last):
        with Scope() as sc:
            nb = NormBufs(sc)
            hT = sc.sb("hT", [128, KC, 512], BF16)
            r_hT = Res()
            uT = sc.sb("uT", [128, FC, 512], BF16)
            r_uT = Res()
            w_pool = sc.pool("wbuf", 3, [128, KC, 512], BF16)
            wo_pool = sc.pool("wobuf", 3, [128, 11, 512], BF16)
            cp = sc.sb("convp", [128, 4, FC], F32)
            carry = sc.sb("carry", [128, FC, 2], F32)
            r_cp = Res()
            r_carry = Res()
            S.dma("sp", cp[:], convp[layer], writes=[r_cp])
            a_pool = sc.pool("abuf", 2, [128, 514], F32)
            acc_pool = sc.pool("acc", 2, [128, 512], F32)
            gt, gr = load_gain(nb, norm2_g[layer:layer + 1, :])
            if last:
                gtf, grf = load_gain(nb, final_g[0:1, :])
            Win = fin_b[layer]
            Wout = fout_b[layer]
            for (r0, n) in BLOCKS:
                samp = r0 >= SEQ
                if r0 == 0:
                    S.op("pool", lambda g_: g_.memset(carry[:], 0.0), writes=[r_carry])
                if samp:
                    S.dma("sp", o_conv_p[layer], carry[:], reads=[r_carry])
                    S.dma("sp", carry[:], conv_state[layer], writes=[r_carry])
                for t0 in range(0, n, 128):
                    P = min(128, n - t0)
                    norm_tile(nb, xbuf[r0 + t0:r0 + t0 + P, :], [xres(r0)], P, gt, gr, t0, hT, r_hT)
                for j0 in range(0, FC, 2):
                    wt, rw = w_pool.get()
                    load_w(None, Win, j0 * 128, 256, dst_col=0, tile=(wt, rw))
                    load_w(None, Win, FFN + j0 * 128, 256, dst_col=256, tile=(wt, rw))
                    for jj in range(2):
                        j = j0 + jj
                        pa, rpa = psA.get()
                        S.mm_group([lambda pe, kc=kc, jj=jj: pe.matmul(pa[:, :n], wt[:, kc, jj * 128:(jj + 1) * 128],
                                                                       hT[:, kc, :n], start=(kc == 0),
                                                                       stop=(kc == KC - 1)) for kc in range(KC)],
                                   reads=[r_hT, rw], writes=[rpa])
                        pg, rpg = psA.get()
                        S.mm_group([lambda pe, kc=kc, jj=jj: pe.matmul(pg[:, :n],
                                                                       wt[:, kc, 256 + jj * 128:256 + (jj + 1) * 128],
                                                                       hT[:, kc, :n], start=(kc == 0),
                                                                       stop=(kc == KC - 1)) for kc in range(KC)],
                                   reads=[r_hT, rw], writes=[rpg])
                        ab_, rab = a_pool.get()
                        S.op("act", lambda a_, j=j: a_.copy(ab_[:, 2:2 + n], pa[:, :n]), reads=[rpa], writes=[rab])
                        S.op("act", lambda g_, j=j: g_.copy(ab_[:, 0:2], carry[:, j, :]), reads=[r_carry],
                             writes=[rab])
                        acc, racc = acc_pool.get()
                        S.op("dve", lambda v, j=j: v.tensor_scalar(out=acc[:, :n], in0=ab_[:, 2:2 + n],
                                                                   scalar1=cp[:, 2, j:j + 1], scalar2=cp[:, 3, j:j + 1],
                                                                   op0=ALU.mult, op1=ALU.add),
                             reads=[rab, r_cp], writes=[racc])
                        S.op("dve", lambda v, j=j: v.scalar_tensor_tensor(out=acc[:, :n], in0=ab_[:, 1:1 + n],
                                                                          scalar=cp[:, 1, j:j + 1], in1=acc[:, :n],
                                                                          op0=ALU.mult, op1=ALU.add),
                             reads=[rab, r_cp], writes=[racc])
                        S.op("dve", lambda v, j=j: v.scalar_tensor_tensor(out=acc[:, :n], in0=ab_[:, 0:n],
                                                                          scalar=cp[:, 0, j:j + 1], in1=acc[:, :n],
                                                                          op0=ALU.mult, op1=ALU.add),
                             reads=[rab, r_cp], writes=[racc])
                        S.op("act", lambda g_, j=j: g_.copy(carry[:, j, :], ab_[:, n:n + 2]), reads=[rab],
                             writes=[r_carry])
                        S.op("act", lambda a_: a_.activation(out=acc[:, :n], in_=acc[:, :n], func=AF.Silu),
                             reads=[racc], writes=[racc])
                        S.op("dve", lambda v, j=j: v.tensor_tensor(out=uT[:, j, :n], in0=acc[:, :n], in1=pg[:, :n],
                                                                   op=ALU.mult), reads=[racc, rpg], writes=[r_uT])
                if samp:
                    S.dma("sp", o_conv_s[layer], carry[:], reads=[r_carry])
                xtiles = [(t0, min(128, n - t0)) for t0 in range(0, n, 128)]
                for cb in range(4):
                    pss = [psA.get() for _ in xtiles]
                    for q4 in range(4):
                        wt, rw = wo_pool.get()
                        src = Wout[q4 * 11 * 128:(q4 + 1) * 11 * 128, cb * 512:(cb + 1) * 512].rearrange(
                            "(kc p) n -> p kc n", p=128)
                        S.dma("act", wt[:], src, writes=[rw])
                        for ti, (t0, P) in enumerate(xtiles):
                            ps, rp = pss[ti]
                            S.mm_group([lambda pe, kc=kc, q4=q4, t0=t0, P=P, ps=ps, wt=wt: pe.matmul(
                                ps[:P, :512], uT[:, q4 * 11 + kc, t0:t0 + P], wt[:, kc, :],
                                start=(q4 == 0 and kc == 0), stop=(q4 == 3 and kc == 10)) for kc in range(11)],
                                reads=[r_uT, rw], writes=[rp])
                    for ti, (t0, P) in enumerate(xtiles):
                        ps, rp = pss[ti]
                        stg, rs = acc_pool.get()
                        S.dma("sp", stg[:P, :], xbuf[r0 + t0:r0 + t0 + P, cb * 512:(cb + 1) * 512],
                              reads=[xres(r0)], writes=[rs])
                        S.op("dve", lambda v, ps=ps, P=P, stg=stg: v.tensor_tensor(out=stg[:P, :], in0=stg[:P, :],
                                                                                 in1=ps[:P, :512], op=ALU.add),
                             reads=[rp], writes=[rs])
                        S.dma("sp", xbuf[r0 + t0:r0 + t0 + P, cb * 512:(cb + 1) * 512], stg[:P, :], reads=[rs],
                              writes=[xres(r0)])
                if last:
                    for t0 in range(0, n, 128):
                        P = min(128, n - t0)
                        xt, rx = nb.xt.get()
                        S.dma("sp", xt[:P, :], xbuf[r0 + t0:r0 + t0 + P, :], reads=[xres(r0)], writes=[rx])
                        st, rs = rstd_of(xt, rx, P, nb)
                        S.op("dve", lambda v: v.scalar_tensor_tensor(out=xt[:P, :], in0=xt[:P, :], scalar=st[:P, 2:3],
                                                                     in1=gtf[:P, :], op0=ALU.mult, op1=ALU.mult),
                             reads=[rs, grf], writes=[rx])
                        dst = o_y_s[:, :] if samp else o_y_p[r0 + t0:r0 + t0 + P, :]
                        S.dma("sp", dst, xt[:P, :], reads=[rx])

    mkp = {}
    mkp["pKT"] = gsb("m_pKT", [128, 4, 256], BF16)
    mkp["pV"] = gsb("m_pV", [128, 2, 512], BF16)
    mkp["sKT"] = gsb("m_sKT", [128, 4, 256], BF16)
    mkp["sV"] = gsb("m_sV", [128, 2, 512], BF16)
    for k_ in ("pKT", "pV", "sKT", "sV"):
        mkp["r_" + k_] = Res(multi=True)

    def precast():
        with Scope() as sc:
            fpool = sc.pool("pc_f", 2, [128, 2 * FFN], F32)
            bpool = sc.pool("pc_b", 2, [128, 2 * FFN], BF16)
            k_ = [0]
            for (src, dst) in ((nsa_w_in, nsa_w_b), (w_mem_kv, mem_w_b), (w_o, wo_b), (ffn_w_in, fin_b),
                               (ffn_w_out, fout_b), (ret_w_in, ret_w_b)):
                s2 = src.rearrange("l r c -> (l r) c")
                d2 = dst.rearrange("l r c -> (l r) c")
                R_, C_ = s2.shape[0], s2.shape[1]
                for r0 in range(0, R_, 128):
                    ft, rf = fpool.get()
                    bt, rb = bpool.get()
                    S.dma("sp", ft[:, :C_], s2[r0:r0 + 128, :], writes=[rf])
                    k_[0] ^= 1
                    copy_op("act" if k_[0] else "dve", bt[:, :C_], ft[:, :C_], [rf], [rb])
                    S.dma("pool", d2[r0:r0 + 128, :], bt[:, :C_], reads=[rb])

    precast()
    for layer in range(nlayers):
        if layer % 2 == 0:
            phaseA_nsa(layer, mkp)
            nsa_mix_prompt(layer)
            nsa_mix_sample(layer)
        else:
            phaseA_ret(layer, mkp)
            ret_mix(layer)
        mem_mix(layer, mkp)
        out_proj(layer)
        ffn(layer, layer == nlayers - 1)

    S.finish()
    return nc, S.n_inst


_CACHE = {}


def kernel(**inp):
    if "nc" not in _CACHE:
        _CACHE["nc"] = build_program()
    nc, _ = _CACHE["nc"]
    f = lambda a: np.ascontiguousarray(np.asarray(a, dtype=np.float32))
    x_prompt = f(inp["x_prompt"])
    x_sample = f(inp["x_sample"])
    mem_prompt = f(inp["mem_prompt"])
    state_nsa_win = f(inp["state_nsa_win"])
    state_ret = f(inp["state_ret"])
    state_ffn_conv = f(inp["state_ffn_conv"])
    cache_mem_kv = f(inp["cache_mem_kv"])
    page_table = np.ascontiguousarray(np.asarray(inp["page_table"], dtype=np.int32))
    conv_w = f(inp["ffn_conv_w"])
    conv_b = f(inp["ffn_conv_b"])
    cpar = np.concatenate([conv_w, conv_b[:, None, :]], axis=1)
    cpar = np.ascontiguousarray(cpar.reshape(DEPTH, 4, FC, 128).transpose(0, 3, 1, 2))
    peT = np.ascontiguousarray(f(inp["nsa_cmp_pe"]).transpose(0, 1, 3, 2))
    shared = {
        "cache": f(inp["cache_nsa_kv"]).reshape(2, 1280 * 128, 2048),
        "norm1_g": f(inp["norm1_g"]), "norm2_g": f(inp["norm2_g"]), "mem_norm_g": f(inp["mem_norm_g"]),
        "final_g": f(inp["final_norm_g"]).reshape(1, D),
        "nsa_w_in": f(inp["nsa_w_in"]), "ret_w_in": f(inp["ret_w_in"]), "ret_gn": f(inp["ret_gn_g"]),
        "w_mem_kv": f(inp["w_mem_kv"]), "w_o": f(inp["w_o"]),
        "ffn_w_in": f(inp["ffn_w_in"]), "ffn_w_out": f(inp["ffn_w_out"]),
        "convp": cpar, "cmp_peT": peT, "cmp_w1": f(inp["nsa_cmp_w1"]), "cmp_w2": f(inp["nsa_cmp_w2"]),
    }
    for k_, v_ in make_tables().items():
        shared["t_" + k_] = v_
    in_maps = []
    for c in range(8):
        b = c // 4
        m = dict(shared)
        m["xp"] = x_prompt[b]
        m["xs"] = x_sample[c]
        m["memp"] = mem_prompt[b]
        m["win_state"] = np.ascontiguousarray(state_nsa_win[:, c].reshape(2, 512, 1024))
        m["ret_state"] = np.ascontiguousarray(state_ret[:, c].reshape(2, 1536, 256))
        m["conv_state"] = np.ascontiguousarray(
            state_ffn_conv[:, c].reshape(DEPTH, 2, FC, 128).transpose(0, 3, 2, 1))
        m["mem_cache"] = np.ascontiguousarray(cache_mem_kv[:, c].reshape(DEPTH, 256, 1024))
        m["page_tab"] = page_table[c:c + 1]
        in_maps.append(m)
    res = run_bass_kernel_spmd(nc, in_maps, core_ids=list(range(8)))
    R = res.results
    _CACHE["raw"] = R
    pc = [0, 4]
    y_prompt = np.stack([R[c]["o_y_p"] for c in pc])
    y_sample = np.stack([R[c]["o_y_s"] for c in range(8)])
    kv_p = np.stack([R[c]["o_kv_p"] for c in pc], axis=1).reshape(2, 2, SEQ, 4, 4, 128)
    kv_s = np.stack([R[c]["o_kv_s"] for c in range(8)], axis=1).reshape(2, 8, T_S, 4, 4, 128)
    win_p = np.stack([R[c]["o_win_p"] for c in pc], axis=1).reshape(2, 2, 512, 2, 4, 128)
    win_s = np.stack([R[c]["o_win_s"] for c in range(8)], axis=1).reshape(2, 8, 512, 2, 4, 128)
    ret_p = np.stack([R[c]["o_ret_p"] for c in pc], axis=1).reshape(2, 2, 6, 256, 256)
    ret_s = np.stack([R[c]["o_ret_s"] for c in range(8)], axis=1).reshape(2, 8, 6, 256, 256)

    def conv_out(a):
        return np.ascontiguousarray(a.transpose(0, 3, 2, 1).reshape(DEPTH, 2, FFN))
    conv_p = np.stack([conv_out(R[c]["o_conv_p"]) for c in pc], axis=1)
    conv_s = np.stack([conv_out(R[c]["o_conv_s"]) for c in range(8)], axis=1)
    mem_p = np.stack([R[c]["o_mem_p"] for c in pc], axis=1).reshape(DEPTH, 2, 256, 2, 4, 128)
    return (y_prompt, y_sample, kv_p, kv_s, win_p, win_s, ret_p, ret_s, conv_p, conv_s, mem_p)
```
